# Optimizing a Trainium2 kernel written in Bass

```python
import jax, jax.numpy as jnp
from jax import lax
import numpy as np

D_MODEL = 4096
BATCH = 2
SEQ = 4096
DEPTH = 2

N_A_LAYERS = DEPTH // 2
N_B_LAYERS = DEPTH - N_A_LAYERS
D_FF = 4 * D_MODEL
EPS = 1e-6
NEG_BIG = -1e30

M_HEADS = 8
M_DV = D_MODEL // M_HEADS
M_DK = M_DV // 2
M_CHUNK = 64
GATE_CAP = 15.0
M_QK = M_HEADS * M_DK
M_SPLITS = (M_QK, 2 * M_QK, 2 * M_QK + D_MODEL, 2 * M_QK + 2 * D_MODEL)
M_IN = 2 * M_QK + 2 * D_MODEL + 2 * M_HEADS

N_HEADS = 32
HEAD_DIM = D_MODEL // N_HEADS
N_KV = 4
HPG = N_HEADS // N_KV
KV_DIM = N_KV * HEAD_DIM
N_BRANCH = 3
CMP_BLOCK = 32
CMP_STRIDE = 16
CMP_HIDDEN = 4 * HEAD_DIM
SEL_BLOCK = 64
N_SEL = 16
WINDOW = 512
Q_BLOCK = 64

kernel_name = "hybrid_mlstm_nsa_yoco"


def rms_norm(x, g):
    xf = x.astype(jnp.float32)
    y = xf * lax.rsqrt(jnp.mean(xf * xf, axis=-1, keepdims=True) + EPS)
    return (y * g.astype(jnp.float32)).astype(x.dtype)


def masked_softmax(s, mask):
    s = jnp.where(mask, s.astype(jnp.float32), NEG_BIG)
    return jax.nn.softmax(s, axis=-1) * mask


def soft_cap(x, cap):
    return cap * jnp.tanh(x / cap)


def squared_relu_mlp(xn, w_up, w_down):
    h = jax.nn.relu(xn @ w_up)
    return (h * h) @ w_down


def mlstm_chunkwise(q, k, v, log_i, log_f):
    B, H, S, DK = q.shape
    DV = v.shape[-1]
    L = M_CHUNK
    NC = S // L

    def to_chunks(a):
        a = a.astype(jnp.float32).reshape(B, H, NC, L, *a.shape[3:])
        return jnp.moveaxis(a, 2, 0)

    xs = (to_chunks(q), to_chunks(k), to_chunks(v), to_chunks(log_i), to_chunks(log_f))
    causal = jnp.tril(jnp.ones((L, L), dtype=bool))

    def step(carry, inp):
        C, n, m = carry
        qc, kc, vc, li, lf = inp
        b = jnp.cumsum(lf, axis=-1)
        log_d = jnp.where(causal, b[..., :, None] - b[..., None, :] + li[..., None, :], -jnp.inf)
        log_inter = b + m[..., None]
        m_t = jnp.maximum(log_inter, jnp.max(log_d, axis=-1))
        w_intra = jnp.exp(log_d - m_t[..., None])
        w_inter = jnp.exp(log_inter - m_t)
        s = jnp.einsum("bhtd,bhsd->bhts", qc, kc) * w_intra
        num = w_inter[..., None] * jnp.einsum("bhtd,bhde->bhte", qc, C) + jnp.einsum("bhts,bhse->bhte", s, vc)
        den = w_inter * jnp.einsum("bhtd,bhd->bht", qc, n) + jnp.sum(s, axis=-1)
        h = num / jnp.maximum(jnp.abs(den), jnp.exp(-m_t))[..., None]
        b_last = b[..., -1]
        log_w = b_last[..., None] - b + li
        m_new = jnp.maximum(b_last + m, jnp.max(log_w, axis=-1))
        wk = jnp.exp(log_w - m_new[..., None])[..., None] * kc
        decay = jnp.exp(b_last + m - m_new)
        C = decay[..., None, None] * C + jnp.einsum("bhsd,bhse->bhde", wk, vc)
        n = decay[..., None] * n + jnp.sum(wk, axis=2)
        return (C, n, m_new), h

    init = (jnp.zeros((B, H, DK, DV), jnp.float32),
            jnp.zeros((B, H, DK), jnp.float32),
            jnp.zeros((B, H), jnp.float32))
    _, h = lax.scan(step, init, xs)
    return jnp.moveaxis(h, 0, 2).reshape(B, H, S, DV)


def mlstm_mixer(xn, w_in, b_gate, head_g, w_out):
    B, S, _ = xn.shape
    proj = xn @ w_in
    q, k, v, o, g = jnp.split(proj, M_SPLITS, axis=-1)
    q = q.reshape(B, S, M_HEADS, M_DK).transpose(0, 2, 1, 3)
    k = k.reshape(B, S, M_HEADS, M_DK).transpose(0, 2, 1, 3) * (M_DK ** -0.5)
    v = v.reshape(B, S, M_HEADS, M_DV).transpose(0, 2, 1, 3)
    g = soft_cap(g.astype(jnp.float32) + b_gate.astype(jnp.float32), GATE_CAP)
    log_i = g[..., :M_HEADS].transpose(0, 2, 1)
    log_f = jax.nn.log_sigmoid(g[..., M_HEADS:]).transpose(0, 2, 1)
    h = mlstm_chunkwise(q, k, v, log_i, log_f)
    h = rms_norm(h, head_g.reshape(M_HEADS, 1, M_DV))
    h = h.transpose(0, 2, 1, 3).reshape(B, S, D_MODEL).astype(xn.dtype)
    return (jax.nn.sigmoid(o) * h) @ w_out


def nsa_shared_kv(xs, w_kv, k_norm_g, cmp_pos, cmp_w1, cmp_w2):
    B, S, _ = xs.shape
    kv = (xs @ w_kv).reshape(B, S, 2 * N_BRANCH, N_KV, HEAD_DIM)
    n_cmp = (S - CMP_BLOCK) // CMP_STRIDE + 1
    idx = np.arange(n_cmp)[:, None] * CMP_STRIDE + np.arange(CMP_BLOCK)[None, :]

    def compress(a, j):
        blk = a[:, idx] + cmp_pos[j][None, None, :, None, :]
        blk = blk.transpose(0, 3, 1, 2, 4).reshape(B, N_KV, n_cmp, CMP_BLOCK * HEAD_DIM)
        return jax.nn.gelu(blk @ cmp_w1[j]) @ cmp_w2[j]

    def to_bgsd(a):
        return a.transpose(0, 2, 1, 3)

    k_cmp = rms_norm(compress(kv[:, :, 0], 0), k_norm_g[0])
    v_cmp = compress(kv[:, :, 1], 1)
    k_sel = rms_norm(to_bgsd(kv[:, :, 2]), k_norm_g[1])
    v_sel = to_bgsd(kv[:, :, 3])
    k_win = rms_norm(to_bgsd(kv[:, :, 4]), k_norm_g[2])
    v_win = to_bgsd(kv[:, :, 5])
    return k_cmp, v_cmp, k_sel, v_sel, k_win, v_win


def cmp_to_sel_matrix(n_cmp, n_sb):
    c0 = np.arange(n_cmp)[:, None] * CMP_STRIDE
    s0 = np.arange(n_sb)[None, :] * SEL_BLOCK
    ov = np.minimum(c0 + CMP_BLOCK, s0 + SEL_BLOCK) - np.maximum(c0, s0)
    return jnp.asarray(np.maximum(ov, 0) / CMP_BLOCK, dtype=jnp.float32)


def nsa_mixer(xn, kv, w_qg, q_norm_g, w_out):
    k_cmp, v_cmp, k_sel, v_sel, k_win, v_win = kv
    B, S, _ = xn.shape
    n_cmp = k_cmp.shape[2]
    n_sb = S // SEL_BLOCK
    n_qb = S // Q_BLOCK
    top = min(N_SEL, n_sb)
    scale = HEAD_DIM ** -0.5
    t = jnp.arange(S)

    proj = xn @ w_qg
    q = proj[..., :D_MODEL].reshape(B, S, N_KV, HPG, HEAD_DIM).transpose(0, 2, 3, 1, 4)
    gates = jax.nn.sigmoid(proj[..., D_MODEL:].astype(jnp.float32))
    gates = gates.reshape(B, S, N_BRANCH, N_KV, HPG).transpose(2, 0, 3, 4, 1)[..., None]
    q_cmp = rms_norm(q, q_norm_g[0])
    q_sel = rms_norm(q, q_norm_g[1])
    q_win = rms_norm(q, q_norm_g[2])

    s = jnp.einsum("bghtd,bgcd->bghtc", q_cmp, k_cmp) * scale
    cmp_valid = (jnp.arange(n_cmp) * CMP_STRIDE + CMP_BLOCK - 1)[None, :] <= t[:, None]
    p_cmp = masked_softmax(s, cmp_valid)
    o_cmp = jnp.einsum("bghtc,bgcd->bghtd", p_cmp.astype(v_cmp.dtype), v_cmp)

    p_blk = jnp.einsum("bghtc,cj->bgtj", p_cmp, cmp_to_sel_matrix(n_cmp, n_sb))
    blk = jnp.arange(n_sb)[None, :]
    cur = (t // SEL_BLOCK)[:, None]
    forced = (blk == cur) | (blk == 0)
    p_blk = jnp.where(forced, jnp.inf, jnp.where(blk <= cur, p_blk, -jnp.inf))
    _, sel_idx = lax.top_k(p_blk, top)

    k_blocks = k_sel.reshape(B, N_KV, n_sb, SEL_BLOCK, HEAD_DIM)
    v_blocks = v_sel.reshape(B, N_KV, n_sb, SEL_BLOCK, HEAD_DIM)
    k_pad = jnp.pad(k_win, ((0, 0), (0, 0), (WINDOW, 0), (0, 0)))
    v_pad = jnp.pad(v_win, ((0, 0), (0, 0), (WINDOW, 0), (0, 0)))
    gather = jax.vmap(jax.vmap(lambda blocks, ix: blocks[ix].reshape(ix.shape[0], -1, HEAD_DIM)))

    def split_q(a):
        return jnp.moveaxis(a.reshape(B, N_KV, HPG, n_qb, Q_BLOCK, HEAD_DIM), 3, 0)

    xs = (split_q(q_sel), split_q(q_win),
          jnp.moveaxis(sel_idx.reshape(B, N_KV, n_qb, Q_BLOCK, top), 2, 0),
          jnp.arange(n_qb) * Q_BLOCK)

    def block_fn(inp):
        qs, qw, ix, start = inp
        tq = start + jnp.arange(Q_BLOCK)
        ks = gather(k_blocks, ix)
        vs = gather(v_blocks, ix)
        kpos = (ix[..., None] * SEL_BLOCK + jnp.arange(SEL_BLOCK)).reshape(B, N_KV, Q_BLOCK, -1)
        s_sel = jnp.einsum("bghqd,bgqkd->bghqk", qs, ks) * scale
        p_sel = masked_softmax(s_sel, (kpos <= tq[:, None])[:, :, None])
        o_sel = jnp.einsum("bghqk,bgqkd->bghqd", p_sel.astype(vs.dtype), vs)
        kw = lax.dynamic_slice_in_dim(k_pad, start, WINDOW + Q_BLOCK, axis=2)
        vw = lax.dynamic_slice_in_dim(v_pad, start, WINDOW + Q_BLOCK, axis=2)
        wpos = start - WINDOW + jnp.arange(WINDOW + Q_BLOCK)
        diff = tq[:, None] - wpos[None, :]
        wmask = (diff >= 0) & (diff < WINDOW) & (wpos[None, :] >= 0)
        s_win = jnp.einsum("bghqd,bgkd->bghqk", qw, kw) * scale
        p_win = masked_softmax(s_win, wmask)
        o_win = jnp.einsum("bghqk,bgkd->bghqd", p_win.astype(vw.dtype), vw)
        return o_sel, o_win

    o_sel, o_win = lax.map(block_fn, xs)

    def merge(o):
        return jnp.moveaxis(o, 0, 3).reshape(B, N_KV, HPG, S, HEAD_DIM)

    o = gates[0] * o_cmp + gates[1] * merge(o_sel) + gates[2] * merge(o_win)
    o = o.transpose(0, 3, 1, 2, 4).reshape(B, S, D_MODEL).astype(xn.dtype)
    return o @ w_out


def setup_inputs(seed: int = 0) -> dict:
    key = jax.random.key(seed)
    ks = jax.random.split(key, 24)
    f32 = jnp.float32

    def dense(k, shape, fan_in):
        return jax.random.normal(k, shape, f32) * (fan_in ** -0.5)

    def gain(k, shape):
        return 1.0 + 0.02 * jax.random.normal(k, shape, f32)

    gate_base = jnp.concatenate([jnp.full((M_HEADS,), -2.0, f32), jnp.full((M_HEADS,), 3.0, f32)])
    m_b_gate = gate_base[None, :] + 0.1 * jax.random.normal(ks[4], (N_A_LAYERS, 2 * M_HEADS), f32)
    return {
        "x": jax.random.normal(ks[0], (BATCH, SEQ, D_MODEL), f32),
        "attn_norm_g": gain(ks[1], (DEPTH, D_MODEL)),
        "mlp_norm_g": gain(ks[2], (DEPTH, D_MODEL)),
        "m_w_in": dense(ks[3], (N_A_LAYERS, D_MODEL, M_IN), D_MODEL),
        "m_b_gate": m_b_gate,
        "m_head_g": gain(ks[5], (N_A_LAYERS, D_MODEL)),
        "m_w_out": dense(ks[6], (N_A_LAYERS, D_MODEL, D_MODEL), D_MODEL),
        "kv_norm_g": gain(ks[7], (D_MODEL,)),
        "w_kv": dense(ks[8], (D_MODEL, 2 * N_BRANCH * KV_DIM), D_MODEL),
        "k_norm_g": gain(ks[9], (N_BRANCH, HEAD_DIM)),
        "cmp_pos": 0.1 * jax.random.normal(ks[10], (2, CMP_BLOCK, HEAD_DIM), f32),
        "cmp_w1": dense(ks[11], (2, CMP_BLOCK * HEAD_DIM, CMP_HIDDEN), CMP_BLOCK * HEAD_DIM),
        "cmp_w2": dense(ks[12], (2, CMP_HIDDEN, HEAD_DIM), CMP_HIDDEN),
        "n_w_qg": dense(ks[13], (N_B_LAYERS, D_MODEL, D_MODEL + N_BRANCH * N_HEADS), D_MODEL),
        "q_norm_g": gain(ks[14], (N_B_LAYERS, N_BRANCH, HEAD_DIM)),
        "n_w_out": dense(ks[15], (N_B_LAYERS, D_MODEL, D_MODEL), D_MODEL),
        "mlp_w_up": dense(ks[16], (DEPTH, D_MODEL, D_FF), D_MODEL),
        "mlp_w_down": dense(ks[17], (DEPTH, D_FF, D_MODEL), D_FF),
    }


def reference(x, attn_norm_g, mlp_norm_g, m_w_in, m_b_gate, m_head_g, m_w_out, kv_norm_g, w_kv,
              k_norm_g, cmp_pos, cmp_w1, cmp_w2, n_w_qg, q_norm_g, n_w_out, mlp_w_up, mlp_w_down):
    shared_kv = None
    for layer in range(DEPTH):
        h = rms_norm(x, attn_norm_g[layer])
        if layer < N_A_LAYERS:
            a = layer
            x = x + mlstm_mixer(h, m_w_in[a], m_b_gate[a], m_head_g[a], m_w_out[a])
        else:
            if shared_kv is None:
                shared_kv = nsa_shared_kv(rms_norm(x, kv_norm_g), w_kv, k_norm_g, cmp_pos, cmp_w1, cmp_w2)
            b = layer - N_A_LAYERS
            x = x + nsa_mixer(h, shared_kv, n_w_qg[b], q_norm_g[b], n_w_out[b])
        x = x + squared_relu_mlp(rms_norm(x, mlp_norm_g[layer]), mlp_w_up[layer], mlp_w_down[layer])
    return x
```

```python
import numpy as np
from contextlib import ExitStack
import ml_dtypes
import concourse.bass as bass
import concourse.mybir as mybir
from concourse.bass_utils import run_bass_kernel_spmd

F32 = mybir.dt.float32
BF16 = mybir.dt.bfloat16
ALU = mybir.AluOpType
AF = mybir.ActivationFunctionType
AX = mybir.AxisListType
NPBF = ml_dtypes.bfloat16

EPS = 1e-6
ENGS = ("pe", "act", "dve", "pool", "sp")


class Op:
    __slots__ = ("eng", "fn", "deps", "is_dma", "sem", "val", "signal", "ring", "slot")

    def __init__(self, eng, fn, deps, is_dma=False):
        self.eng = eng
        self.fn = fn
        self.deps = [d for d in deps if d is not None]
        self.is_dma = is_dma
        self.sem = None
        self.val = None
        self.signal = False
        self.ring = None
        self.slot = None


def X(name, *a, **kw):
    return (name, a, kw)


class Prog:
    def __init__(self, nc):
        self.nc = nc
        self.ops = {e: [] for e in ENGS}
        self.rings = {}

    def op(self, eng, fn, deps=()):
        o = Op(eng, fn, deps)
        self.ops[eng].append(o)
        return o

    def dma(self, q, out, in_, deps=(), ring="dflt", nslots=4, **kw):
        def fn(e, out=out, in_=in_, kw=kw):
            return e.dma_start(out=out, in_=in_, **kw)
        o = Op(q, fn, deps, is_dma=True)
        self.ops[q].append(o)
        r = self.rings.setdefault(ring, {"n": nslots, "count": 0, "uses": {}, "last": {}})
        s = r["count"] % r["n"]
        r["count"] += 1
        r["uses"][s] = r["uses"].get(s, 0) + 1
        if s in r["last"]:
            o.deps.append(r["last"][s])
        r["last"][s] = o
        o.ring = ring
        o.slot = s
        o.val = 16 * r["uses"][s]
        return o

    def wait(self, eng, deps):
        o = Op(eng, None, deps)
        self.ops[eng].append(o)
        return o

    def emit(self, ctx):
        nc = self.nc
        esem = {e: ctx.enter_context(nc.semaphore("s_" + e)) for e in ("pe", "act", "dve", "pool")}
        rsem = {}
        for name, r in self.rings.items():
            rsem[name] = [ctx.enter_context(nc.semaphore("r_%s_%d" % (name, i))) for i in range(len(r["uses"]))]
        for e in ENGS:
            for o in self.ops[e]:
                for d in o.deps:
                    if (not d.is_dma) and (d.eng != e or e != "pe"):
                        d.signal = True
        for e in ("pe", "act", "dve", "pool"):
            c = 0
            for o in self.ops[e]:
                if o.is_dma:
                    continue
                if o.signal:
                    c += 1
                    o.sem = esem[e]
                    o.val = c
        for e in ENGS:
            for o in self.ops[e]:
                if o.is_dma:
                    o.sem = rsem[o.ring][o.slot]
        blk = ctx.enter_context(nc.Block())

        def run(e, engobj):
            waited = {}
            for o in self.ops[e]:
                for d in o.deps:
                    if (not d.is_dma) and d.eng == e and e == "pe":
                        continue
                    key = id(d.sem)
                    if waited.get(key, 0) >= d.val:
                        continue
                    engobj.wait_ge(d.sem, d.val)
                    waited[key] = d.val
                if o.fn is None:
                    continue
                if isinstance(o.fn, tuple):
                    ins = getattr(engobj, o.fn[0])(*o.fn[1], **o.fn[2])
                else:
                    ins = o.fn(engobj)
                if o.is_dma:
                    ins.then_inc(o.sem, 16)
                elif o.signal:
                    ins.then_inc(o.sem, 1)

        @blk.tensor
        def _(eng):
            run("pe", eng)

        @blk.scalar
        def _(eng):
            run("act", eng)

        @blk.vector
        def _(eng):
            run("dve", eng)

        @blk.gpsimd
        def _(eng):
            run("pool", eng)

        @blk.sync
        def _(eng):
            run("sp", eng)


class Buf:
    def __init__(self, t):
        self.t = t
        self.ws = {}
        self.rs = {}
        self.dw = []
        self.dr = []

    @staticmethod
    def _add(d, lst, op):
        if op.is_dma:
            lst.append(op)
        else:
            d[op.eng] = op

    def wdeps(self):
        return list(self.ws.values()) + self.dw + list(self.rs.values()) + self.dr

    def rdeps(self):
        return list(self.ws.values()) + self.dw

    def wrote(self, op):
        self.ws = {}
        self.rs = {}
        self.dw = []
        self.dr = []
        self._add(self.ws, self.dw, op)

    def also_wrote(self, op):
        self._add(self.ws, self.dw, op)

    def read(self, op):
        self._add(self.rs, self.dr, op)


class Ring:
    def __init__(self, bufs):
        self.bufs = bufs
        self.i = 0

    def next(self):
        b = self.bufs[self.i % len(self.bufs)]
        self.i += 1
        return b


class WStream:
    def __init__(self, kb, nstg=2, elems=4096, name="stg"):
        self.kb = kb
        self.n = nstg
        self.elems = elems
        self.name = name
        self.ring = kb.sbring(name, nstg, [128, elems], F32)
        self.i = 0

    def load(self, dst_ap, src_ap, a, b, wd, dstbuf, first):
        p = self.kb.p
        stg = self.ring.next()
        view = stg.t[:, 0:a * b].rearrange("p (a b) -> p a b", a=a)
        d = p.dma("sp", view, src_ap, deps=stg.wdeps(), ring=self.name, nslots=self.n)
        stg.wrote(d)
        if self.i % 2 == 0:
            c = p.op("act", X("copy", out=dst_ap, in_=view), deps=[d] + list(wd))
        else:
            c = p.op("pool", X("tensor_copy", out=dst_ap, in_=view), deps=[d] + list(wd))
        self.i += 1
        stg.read(c)
        if first:
            dstbuf.wrote(c)
        else:
            dstbuf.also_wrote(c)
        return c


class KB:
    def __init__(self, nc, ctx):
        self.nc = nc
        self.ctx = ctx
        self.p = Prog(nc)
        self.nps = 0

    def sb(self, name, shape, dt):
        return self.ctx.enter_context(self.nc.sbuf_tensor(name, list(shape), dt))

    def psum(self, name, shape, dt=F32):
        return self.ctx.enter_context(self.nc.psum_tensor(name, list(shape), dt))

    def sbring(self, name, n, shape, dt):
        return Ring([Buf(self.sb("%s%d" % (name, i), shape, dt)) for i in range(n)])

    def psring(self, name, n, shape=(128, 512), dt=F32):
        return Ring([Buf(self.psum("%s%d" % (name, i), shape, dt)) for i in range(n)])

    def din(self, name, shape, dt):
        return self.nc.dram_tensor(name, list(shape), dt, kind="ExternalInput").ap()

    def dout(self, name, shape, dt):
        return self.nc.dram_tensor(name, list(shape), dt, kind="ExternalOutput").ap()


def mm_group(kb, bank, pairs, extra_deps=(), out_ap=None):
    p = kb.p
    n = len(pairs)
    last = None
    out = bank.t[:] if out_ap is None else out_ap
    for i, (l, r, deps) in enumerate(pairs):
        d = list(deps)
        if i == 0:
            d += bank.wdeps() + list(extra_deps)
        last = p.op("pe", X("matmul", out, lhsT=l, rhs=r, start=(i == 0), stop=(i == n - 1)), deps=d)
    bank.wrote(last)
    return last


def rms_stats(kb, XT, KD, D, ones, sqring, ssbank, R, xt_ready):
    p = kb.p
    last = None
    for k in range(KD):
        sq = sqring.next()
        o = p.op("act", X("activation", out=sq.t[:], in_=XT.t[:, k, :], func=AF.Square), deps=sq.wdeps() + xt_ready)
        sq.wrote(o)
        d = [o]
        if k == 0:
            d += ssbank.wdeps()
        last = p.op("pe", X("matmul", ssbank.t[:], lhsT=ones[:], rhs=sq.t[:], start=(k == 0), stop=(k == KD - 1)), deps=d)
        sq.read(last)
    ssbank.wrote(last)
    o2 = rstd_from(kb, ssbank, R, 1.0 / D, [last])
    R.wrote(o2)
    return o2


def rstd_from(kb, bank, dst, inv_n, deps, eps_ap=None):
    p = kb.p
    o1 = p.op("act", X("activation", out=dst.t[:], in_=bank.t[:], func=AF.Sqrt, scale=inv_n, bias=kb.eps[:, 0:1]), deps=list(deps) + dst.wdeps() + [kb.eps_op])
    bank.read(o1)
    o2 = p.op("dve", X("reciprocal", out=dst.t[:], in_=dst.t[:]), deps=[o1])
    return o2


def build_BD(T, D, DFF, tail, HD=128, NKV=4, NQH=32, NG=96):
    nc = bass.Bass("TRN2", target_bir_lowering=False)
    KD = D // 128
    KF = DFF // 128
    NP = T // 512
    JR = 8
    NR = KF // JR
    with ExitStack() as ctx:
        kb = KB(nc, ctx)
        p = kb.p
        xT_d = kb.din("xT", [D, T], F32)
        aT_d = kb.din("aT", [D, T], BF16)
        w_o = kb.din("w_o", [D, D], F32)
        w_up = kb.din("w_up", [D, DFF], F32)
        w_dn = kb.din("w_dn", [DFF, D], F32)
        g_d = kb.din("gains", [128, 3 * KD], F32)
        xo_d = kb.dout("xoT", [D, T], F32)
        if tail:
            KVD = NKV * HD
            w_qg = kb.din("w_qg", [D, D + NG], F32)
            w_kv = kb.din("w_kv", [D, 6 * KVD], F32)
            kng_d = kb.din("kng", [128, 3], F32)
            qng_d = kb.din("qng", [128, 3], F32)
            qT_o = kb.dout("qT", [D, T], BF16)
            gT_o = kb.dout("gT", [NG, T], F32)
            kcT_o = kb.dout("kcT", [KVD, T], BF16)
            vcT_o = kb.dout("vcT", [KVD, T], BF16)
            ksT_o = kb.dout("ksT", [KVD, T], BF16)
            kwT_o = kb.dout("kwT", [KVD, T], BF16)
            vs_o = kb.dout("vs", [T, KVD], BF16)
            vw_o = kb.dout("vw", [T, KVD], BF16)

        XT = Buf(kb.sb("XT", [128, KD, 512], F32))
        XN = Buf(kb.sb("XN", [128, KD, 512], BF16))
        HT = kb.sbring("HT", 2, [128, JR, 512], BF16)
        wA = kb.sbring("wA", 2, [128, KD, 256], BF16)
        wB = kb.sbring("wB", 2, [128, JR, 512], BF16)
        ws = WStream(kb, 2, max(4096, JR * 512, (KD // 2) * 256))
        sqr = kb.sbring("SQ", 3, [128, 512], BF16)
        sqf = kb.sbring("SQF", 2, [128, 512], F32)
        R = Buf(kb.sb("R", [128, 512], F32))
        gains = kb.sb("gains_sb", [128, 3 * KD], F32)
        ones = kb.sb("ones", [128, 128], BF16)
        psA = kb.psring("psA", 3)
        psD = kb.psring("psD", 4)
        psS = Buf(kb.psum("psS", [128, 512]))
        if tail:
            kgain = kb.sb("kgain_sb", [128, 2], F32)
            w0, w1 = wB.bufs[0].t, wB.bufs[1].t

            def f32view(t, j0):
                return t[:, j0:j0 + 2, :].bitcast(F32).rearrange("p a b -> p (a b)")
            Rq = Ring([Buf(f32view(w0, 0)), Buf(f32view(w0, 2))])
            GB = Buf(f32view(w0, 4))
            OB = Ring([Buf(w1[:, j, :]) for j in range(3)])
            tail_alias = [(wB.bufs[0], Rq.bufs + [GB]), (wB.bufs[1], OB.bufs)]

        c_ones = p.op("dve", X("memset", ones[:], 1.0))
        kb.eps = kb.sb("eps_sb", [128, 1], F32)
        kb.eps_op = p.op("dve", X("memset", kb.eps[:], EPS))
        d_g = p.dma("sp", gains[:], g_d[:, :], ring="misc", nslots=4)
        if tail:
            kng = kb.sb("kng_sb", [128, 3], F32)
            qng = kb.sb("qng_sb", [128, 3], F32)
            d_k1 = p.dma("sp", kng[:], kng_d[:, :], ring="misc", nslots=4)
            d_k2 = p.dma("sp", qng[:], qng_d[:, :], ring="misc", nslots=4)
            o_ = p.op("dve", X("tensor_tensor", out=kgain[:], in0=kng[:, 1:3], in1=qng[:, 1:3], op=ALU.mult), deps=[d_k1, d_k2])
            d_kg = p.op("dve", X("tensor_scalar", out=kgain[:], in0=kgain[:], scalar1=HD ** -0.5, scalar2=None, op0=ALU.mult), deps=[o_])

        w_o_v = w_o.rearrange("(k p) n -> p k n", p=128)
        w_up_v = w_up.rearrange("(k p) n -> p k n", p=128)
        w_dn_v = w_dn.rearrange("(j p) n -> p j n", p=128)
        xT_v = xT_d.rearrange("(k p) t -> p k t", p=128)
        aT_v = aT_d.rearrange("(k p) t -> p k t", p=128)
        xo_v = xo_d.rearrange("(k p) t -> p k t", p=128)

        def load_wA(src_v, c0, ncols=256):
            s = wA.next()
            wd = s.wdeps()
            kh = KD // 2
            ws.load(s.t[:, 0:kh, 0:ncols], src_v[:, 0:kh, c0:c0 + ncols], kh, ncols, wd, s, True)
            ws.load(s.t[:, kh:KD, 0:ncols], src_v[:, kh:KD, c0:c0 + ncols], kh, ncols, wd, s, False)
            return s

        out_dmas = []
        for ps_ in range(NP):
            tsl = slice(ps_ * 512, (ps_ + 1) * 512)
            ld = []
            for q in range(4):
                ks = slice(q * KD // 4, (q + 1) * KD // 4)
                ld.append(p.dma("sp", XT.t[:, ks, :], xT_v[:, ks, tsl], deps=XT.wdeps(), ring="xin", nslots=4))
            XT.wrote(ld[0])
            for o_ in ld[1:]:
                XT.also_wrote(o_)
            xt_ready = list(ld)
            la = []
            for q in range(2):
                ks = slice(q * KD // 2, (q + 1) * KD // 2)
                la.append(p.dma("sp", XN.t[:, ks, :], aT_v[:, ks, tsl], deps=XN.wdeps(), ring="ain", nslots=2))
            XN.wrote(la[0])
            XN.also_wrote(la[1])
            xn_ready = list(la)
            adds = []
            for mg in range(KD // 2):
                s = load_wA(w_o_v, mg * 256)
                for mi in range(2):
                    m = 2 * mg + mi
                    bank = psA.next()
                    mm = mm_group(kb, bank, [(s.t[:, k, mi * 128:(mi + 1) * 128], XN.t[:, k, :], (s.rdeps() + xn_ready + [c_ones]) if k == 0 else []) for k in range(KD)])
                    s.read(mm)
                    XN.read(mm)
                    o = p.op("dve", X("tensor_tensor", out=XT.t[:, m, :], in0=XT.t[:, m, :], in1=bank.t[:], op=ALU.add), deps=[mm] + xt_ready)
                    bank.read(o)
                    adds.append(o)
            XT.wrote(adds[-1])

            def norm_into_XN(goff, xt_dep):
                r_op = rms_stats(kb, XT, KD, D, ones, sqr, psS, R, xt_dep)
                last = None
                for k in range(KD):
                    eng = "dve"
                    last_ = p.op(eng, X("scalar_tensor_tensor", out=XN.t[:, k, :], in0=XT.t[:, k, :], scalar=gains[:, goff + k:goff + k + 1], in1=R.t[:], op0=ALU.mult, op1=ALU.mult),
                                 deps=[r_op, d_g] + XN.wdeps() + xt_dep)
                    last = last_
                R.read(last)
                return [last, last]

            xn_ops = norm_into_XN(0, [adds[-1]])
            XN.wrote(xn_ops[0])
            xn_ready = list(xn_ops)

            hts = [None] * NR
            dn_adds = [adds[-1]]

            def up_round(r):
                ht = HT.next()
                evs = []
                for jg in range(JR // 2):
                    s = load_wA(w_up_v, (r * JR + 2 * jg) * 128)
                    for ji in range(2):
                        jl = 2 * jg + ji
                        bank = psA.next()
                        mm = mm_group(kb, bank, [(s.t[:, k, ji * 128:(ji + 1) * 128], XN.t[:, k, :], (s.rdeps() + xn_ready) if k == 0 else []) for k in range(KD)])
                        s.read(mm)
                        XN.read(mm)
                        sf = sqf.next()
                        o0 = p.op("act", X("activation", out=sf.t[:], in_=bank.t[:], func=AF.Square), deps=[mm] + sf.wdeps())
                        sf.wrote(o0)
                        o = p.op("dve", X("scalar_tensor_tensor", out=ht.t[:, jl, :], in0=bank.t[:], scalar=0.0, in1=sf.t[:], op0=ALU.is_gt, op1=ALU.mult),
                                 deps=[mm, o0] + ht.wdeps())
                        sf.read(o)
                        bank.read(o0)
                        bank.read(o)
                        evs.append(o)
                ht.wrote(evs[-1])
                hts[r] = ht

            def down_round(r):
                ht = hts[r]
                for mg in range(KD // 4):
                    s = wB.next()
                    ws.load(s.t[:], w_dn_v[:, r * JR:(r + 1) * JR, mg * 512:(mg + 1) * 512], JR, 512, s.wdeps(), s, True)
                    banks = [psD.next() for _ in range(4)]
                    last = None
                    for jl in range(JR):
                        for mi in range(4):
                            d = []
                            if jl == 0:
                                d = banks[mi].wdeps()
                                if mi == 0:
                                    d = d + s.rdeps() + ht.rdeps()
                            last = p.op("pe", X("matmul", banks[mi].t[:], lhsT=s.t[:, jl, mi * 128:(mi + 1) * 128], rhs=ht.t[:, jl, :], start=(jl == 0), stop=(jl == JR - 1)), deps=d)
                    s.read(last)
                    ht.read(last)
                    for mi in range(4):
                        banks[mi].wrote(last)
                        m = 4 * mg + mi
                        eng = "dve" if mi % 2 == 0 else "pool"
                        eng = "dve"
                        o = p.op(eng, X("tensor_tensor", out=XT.t[:, m, :], in0=XT.t[:, m, :], in1=banks[mi].t[:], op=ALU.add), deps=[last] + dn_adds[-1:])
                        banks[mi].read(o)
                        dn_adds.append(o)

            for r in range(NR + 1):
                if r < NR:
                    up_round(r)
                if r >= 1:
                    down_round(r - 1)
            XT.wrote(dn_adds[-1])
            xt_ready = [dn_adds[-1]]
            XN.read(dn_adds[-1])
            for q in range(4):
                ks = slice(q * KD // 4, (q + 1) * KD // 4)
                o = p.dma("sp", xo_v[:, ks, tsl], XT.t[:, ks, :], deps=xt_ready, ring="xout", nslots=4)
                XT.read(o)
                out_dmas.append(o)

            if tail:
                for wslot, als in tail_alias:
                    for al in als:
                        for op_ in wslot.wdeps():
                            al.read(op_)
                KVD = NKV * HD
                w_qg_v = w_qg.rearrange("(k p) n -> p k n", p=128)
                w_kv_v = w_kv.rearrange("(k p) n -> p k n", p=128)
                xn_ops = norm_into_XN(KD, xt_ready)
                XN.wrote(xn_ops[0])
                xn_ready = list(xn_ops)
                pend = None

                def finish_q(item):
                    bank, sq, hd, o_sq = item
                    mm2 = mm_group(kb, psS, [(ones[:], sq.t[:], [o_sq])])
                    sq.read(mm2)
                    rq = Rq.next()
                    o2 = rstd_from(kb, psS, rq, 1.0 / HD, [mm2])
                    ob = OB.next()
                    o3 = p.op("dve", X("tensor_tensor", out=ob.t[:], in0=bank.t[:], in1=rq.t[:], op=ALU.mult), deps=[o2] + ob.wdeps())
                    rq.wrote(o3)
                    bank.read(o3)
                    ob.wrote(o3)
                    o4 = p.dma("sp", qT_o[hd * 128:(hd + 1) * 128, tsl], ob.t[:], deps=[o3], ring="oq", nslots=3)
                    ob.read(o4)
                    out_dmas.append(o4)

                for hg in range(NQH // 2):
                    s = load_wA(w_qg_v, hg * 256)
                    for hi in range(2):
                        hd = 2 * hg + hi
                        bank = psA.next()
                        mm = mm_group(kb, bank, [(s.t[:, k, hi * 128:(hi + 1) * 128], XN.t[:, k, :], (s.rdeps() + xn_ready) if k == 0 else []) for k in range(KD)])
                        s.read(mm)
                        XN.read(mm)
                        sq = sqr.next()
                        o_sq = p.op("act", X("activation", out=sq.t[:], in_=bank.t[:], func=AF.Square), deps=[mm] + sq.wdeps())
                        sq.wrote(o_sq)
                        bank.read(o_sq)
                        if pend is not None:
                            finish_q(pend)
                        pend = (bank, sq, hd, o_sq)
                s = load_wA(w_qg_v, D, NG)
                bank = psA.next()
                mm = p_last = None
                n = KD
                d0 = bank.wdeps() + s.rdeps() + xn_ready
                for k in range(KD):
                    mm = p.op("pe", X("matmul", bank.t[0:NG, :], lhsT=s.t[:, k, 0:NG], rhs=XN.t[:, k, :], start=(k == 0), stop=(k == n - 1)), deps=d0 if k == 0 else [])
                bank.wrote(mm)
                s.read(mm)
                XN.read(mm)
                finish_q(pend)
                o = p.op("act", X("activation", out=GB.t[0:NG, :], in_=bank.t[0:NG, :], func=AF.Sigmoid), deps=[mm] + GB.wdeps())
                bank.read(o)
                GB.wrote(o)
                o2 = p.dma("sp", gT_o[:, tsl], GB.t[0:NG, :], deps=[o], ring="og", nslots=1)
                GB.read(o2)
                out_dmas.append(o2)
                last = None
                for k in range(KD):
                    eng = "dve"
                    o = p.op(eng, X("scalar_tensor_tensor", out=XN.t[:, k, :], in0=XT.t[:, k, :], scalar=gains[:, 2 * KD + k:2 * KD + k + 1], in1=R.t[:], op0=ALU.mult, op1=ALU.mult),
                             deps=XN.wdeps() + xt_ready + R.rdeps())
                    last = o
                XN.wrote(last)
                R.read(last)
                xn_ready = [last]
                pend = None

                def finish_k(item):
                    bank, sq, o_sq, dst, gi, row0 = item
                    mm2 = mm_group(kb, psS, [(ones[:], sq.t[:], [o_sq])])
                    sq.read(mm2)
                    rq = Rq.next()
                    o2 = rstd_from(kb, psS, rq, 1.0 / HD, [mm2])
                    ob = OB.next()
                    o3 = p.op("dve", X("scalar_tensor_tensor", out=ob.t[:], in0=bank.t[:], scalar=kgain[:, gi:gi + 1], in1=rq.t[:], op0=ALU.mult, op1=ALU.mult), deps=[o2, d_kg] + ob.wdeps())
                    rq.wrote(o3)
                    bank.read(o3)
                    ob.wrote(o3)
                    o4 = p.dma("sp", dst[row0:row0 + 128, tsl], ob.t[:], deps=[o3], ring="oq", nslots=3)
                    ob.read(o4)
                    out_dmas.append(o4)

                for (j, dst, gi) in ((0, kcT_o, None), (1, vcT_o, None), (2, ksT_o, 0), (4, kwT_o, 1)):
                    for gg in range(NKV // 2):
                        s = load_wA(w_kv_v, j * KVD + gg * 256)
                        for gi2 in range(2):
                            g = 2 * gg + gi2
                            bank = psA.next()
                            mm = mm_group(kb, bank, [(s.t[:, k, gi2 * 128:(gi2 + 1) * 128], XN.t[:, k, :], (s.rdeps() + xn_ready) if k == 0 else []) for k in range(KD)])
                            s.read(mm)
                            XN.read(mm)
                            if gi is None:
                                ob = OB.next()
                                o3 = p.op("act", X("copy", out=ob.t[:], in_=bank.t[:]), deps=[mm] + ob.wdeps())
                                bank.read(o3)
                                ob.wrote(o3)
                                o4 = p.dma("sp", dst[g * 128:(g + 1) * 128, tsl], ob.t[:], deps=[o3], ring="oq", nslots=3)
                                ob.read(o4)
                                out_dmas.append(o4)
                            else:
                                sq = sqr.next()
                                o_sq = p.op("act", X("activation", out=sq.t[:], in_=bank.t[:], func=AF.Square), deps=[mm] + sq.wdeps())
                                sq.wrote(o_sq)
                                bank.read(o_sq)
                                if pend is not None:
                                    finish_k(pend)
                                pend = (bank, sq, o_sq, dst, gi, g * 128)
                if pend is not None:
                    finish_k(pend)
                for (j, dst) in ((3, vs_o), (5, vw_o)):
                    for half in range(KVD // 256):
                        s = load_wA(w_kv_v, j * KVD + half * 256)
                        for tt in range(4):
                            bank = psA.next()
                            mm = mm_group(kb, bank, [(XN.t[:, k, tt * 128:(tt + 1) * 128], s.t[:, k, :], (s.rdeps() + xn_ready) if k == 0 else []) for k in range(KD)], out_ap=bank.t[:, 0:256])
                            s.read(mm)
                            XN.read(mm)
                            ob = OB.next()
                            o3 = p.op("act", X("copy", out=ob.t[:, 0:256], in_=bank.t[:, 0:256]), deps=[mm] + ob.wdeps())
                            bank.read(o3)
                            ob.wrote(o3)
                            o4 = p.dma("sp", dst[ps_ * 512 + tt * 128:ps_ * 512 + (tt + 1) * 128, half * 256:(half + 1) * 256], ob.t[:, 0:256], deps=[o3], ring="oq", nslots=3)
                            ob.read(o4)
                            out_dmas.append(o4)
            if tail:
                for wslot, als in tail_alias:
                    for al in als:
                        for op_ in al.wdeps():
                            wslot.read(op_)
        p.wait("sp", out_dmas)
        p.emit(ctx)
    return nc


def build_A(S, D, DK=256, DV=512, GATE_CAP=15.0):
    nc = bass.Bass("TRN2", target_bir_lowering=False)
    KD = D // 128
    NB = S // 512
    NH = 2
    DKC = DK // 128
    DVC = DV // 128
    QC = NH * DKC
    OC = NH * DVC
    with ExitStack() as ctx:
        kb = KB(nc, ctx)
        p = kb.p
        xT_d = kb.din("xT", [D, S], F32)
        ga_d = kb.din("ga", [128, KD], F32)
        w_q = kb.din("w_q", [D, NH * DK], F32)
        w_k = kb.din("w_k", [D, NH * DK], F32)
        w_v = kb.din("w_v", [D, NH * DV], F32)
        w_og = kb.din("w_og", [D, NH * DV], F32)
        w_g = kb.din("w_g", [D, 4], F32)
        bg_d = kb.din("bg", [2, 2], F32)
        hg_d = kb.din("hgain", [128, OC], F32)
        cc_d = kb.din("c_causal", [128, 64], F32)
        cs_d = kb.din("c_seg", [2, 1024], F32)
        csel_d = kb.din("c_sel", [2, 256], F32)
        ci2_d = kb.din("c_i2", [2, 2], F32)
        out_d = kb.dout("hgT", [NH * DV, S], BF16)

        XTr = kb.sbring("XTr", 2, [128, 4, 512], F32)
        XN = Buf(kb.sb("XN", [128, KD, 512], BF16))
        wA = kb.sbring("wA", 2, [128, KD, 256], BF16)
        ws = WStream(kb, 2, max(4096, (KD // 2) * 256))
        sqr = kb.sbring("SQ", 3, [128, 512], BF16)
        R = Buf(kb.sb("R", [128, 512], F32))
        ga = kb.sb("ga_sb", [128, KD], F32)
        hgain = kb.sb("hgain_sb", [128, OC], F32)
        ones = kb.sb("ones", [128, 128], BF16)
        causal = kb.sb("causal", [128, 64], F32)
        cseg = kb.sb("cseg", [2, 512], F32)
        csel = kb.sb("csel", [2, 256], F32)
        ci2 = kb.sb("ci2", [2, 2], F32)
        bg = kb.sb("bg_sb", [2, 2], F32)
        bg15 = kb.sb("bg15", [2, 2], F32)
        one1 = kb.sb("one1", [2, 1], F32)
        QT = Buf(kb.sb("QT", [128, QC, 512], BF16))
        KT = Buf(kb.sb("KT", [128, QC, 512], BF16))
        KG = Buf(kb.sb("KG", [128, 4, NH, DK], BF16))
        VT = Buf(kb.sb("VT", [128, 4, NH, DV], BF16))
        OG = Buf(kb.sb("OG", [128, OC, 512], BF16))
        HG = Buf(kb.sb("HG", [128, OC, 512], BF16))
        OT = kb.sbring("OT", 2, [128, 512], F32)
        C32 = [Buf(kb.sb("C32_%d" % i, [128, 512], F32)) for i in range(QC)]
        Cd = [Buf(kb.sb("Cd_%d" % i, [128, 512], F32)) for i in range(QC)]
        Cb = [Buf(kb.sb("Cb_%d" % i, [128, 512], BF16)) for i in range(QC)]
        n32 = [Buf(kb.sb("n32_%d" % i, [128, 2], F32)) for i in range(QC)]
        nrep = [Buf(kb.sb("nrep_%d" % i, [128, 128], BF16)) for i in range(QC)]
        EMWb = Buf(kb.sb("EMWb", [128, NH, 512], F32))
        G128 = Buf(kb.sb("G128", [128, 4, NH], F32))
        SC = Buf(kb.sb("SCk", [128, 4, NH], F32))
        DECb = Buf(kb.sb("DECb", [128, 8, NH], F32))
        rcol = Buf(kb.sb("rcol", [128, 8], F32))
        SW = kb.sbring("SW", 3, [128, 64], BF16)
        DN = kb.sbring("DN", 3, [128, 64], F32)
        HTt = kb.sbring("HTt", 3, [128, DVC, 64], F32)
        HSQ = kb.sbring("HSQ", 3, [128, DVC, 64], BF16)
        RS = kb.sbring("RS", 3, [128, 64], F32)
        GV = [Buf(kb.sb("GV%d" % i, [2, 512], F32)) for i in range(6)]
        gs = Buf(kb.sb("gsmall", [2, 64], F32))
        mprev = Buf(kb.sb("mprev", [2, 1], F32))

        psA = kb.psring("psA", 3)
        psS = Buf(kb.psum("psS", [128, 512]))
        psc = kb.psring("psc", 2)
        psC = kb.psring("psC", 2)

        c_ones = p.op("dve", X("memset", ones[:], 1.0))
        kb.eps = kb.sb("eps_sb", [128, 1], F32)
        kb.eps_op = p.op("dve", X("memset", kb.eps[:], EPS))
        c_one1 = p.op("dve", X("memset", one1[:], 1.0))
        c_mp = p.op("dve", X("memset", mprev.t[:], 0.0))
        mprev.wrote(c_mp)
        cinit = []
        for i in range(QC):
            o = p.op("pool", X("memset", C32[i].t[:], 0.0))
            C32[i].wrote(o)
            o = p.op("pool", X("memset", Cb[i].t[:], 0.0))
            Cb[i].wrote(o)
            o = p.op("pool", X("memset", n32[i].t[:], 0.0))
            n32[i].wrote(o)
            o = p.op("pool", X("memset", nrep[i].t[:], 0.0))
            nrep[i].wrote(o)
        dc = [p.dma("sp", ga[:], ga_d[:, :], ring="misc", nslots=8),
              p.dma("sp", hgain[:], hg_d[:, :], ring="misc", nslots=8),
              p.dma("sp", causal[:], cc_d[:, :], ring="misc", nslots=8),
              p.dma("sp", cseg[:], cs_d[:, 0:512], ring="misc", nslots=8),
              p.dma("sp", csel[:], csel_d[:, :], ring="misc", nslots=8),
              p.dma("sp", ci2[:], ci2_d[:, :], ring="misc", nslots=8),
              p.dma("sp", bg[:], bg_d[:, :], ring="misc", nslots=8)]
        c_bg15 = p.op("dve", X("tensor_scalar", out=bg15[:], in0=bg[:], scalar1=1.0 / GATE_CAP, scalar2=None, op0=ALU.mult), deps=[dc[6]])

        xT_v = xT_d.rearrange("(k p) t -> p k t", p=128)
        wv = {n: a.rearrange("(k p) n -> p k n", p=128) for n, a in (("q", w_q), ("k", w_k), ("v", w_v), ("o", w_og), ("g", w_g))}
        out_v = out_d.rearrange("(c p) t -> p c t", p=128)

        def load_wA(name, c0, ncols=256):
            s = wA.next()
            wd = s.wdeps()
            kh = KD // 2
            ws.load(s.t[:, 0:kh, 0:ncols], wv[name][:, 0:kh, c0:c0 + ncols], kh, ncols, wd, s, True)
            ws.load(s.t[:, kh:KD, 0:ncols], wv[name][:, kh:KD, c0:c0 + ncols], kh, ncols, wd, s, False)
            return s

        def proj_fm(s, c0, ncols=128, M=None):
            bank = psA.next()
            d0 = bank.wdeps() + s.rdeps() + XN.rdeps() + [c_ones]
            mm = None
            for k in range(KD):
                mm = p.op("pe", X("matmul", bank.t[0:ncols, :], lhsT=s.t[:, k, c0:c0 + ncols], rhs=XN.t[:, k, :], start=(k == 0), stop=(k == KD - 1)), deps=d0 if k == 0 else [])
            bank.wrote(mm)
            s.read(mm)
            XN.read(mm)
            return bank, mm

        def proj_tm(s, tt):
            bank = psA.next()
            d0 = bank.wdeps() + s.rdeps() + XN.rdeps()
            mm = None
            for k in range(KD):
                mm = p.op("pe", X("matmul", bank.t[:, 0:256], lhsT=XN.t[:, k, tt * 128:(tt + 1) * 128], rhs=s.t[:, k, :], start=(k == 0), stop=(k == KD - 1)), deps=d0 if k == 0 else [])
            bank.wrote(mm)
            s.read(mm)
            XN.read(mm)
            return bank, mm

        out_dmas = []
        pend = [None]

        for bi in range(NB):
            tsl = slice(bi * 512, (bi + 1) * 512)
            sq_ops = []
            first = True
            mm = None
            for grp in range(KD // 4):
                xs = XTr.next()
                ld = p.dma("sp", xs.t[:], xT_v[:, 4 * grp:4 * grp + 4, tsl], deps=xs.wdeps(), ring="xin", nslots=2)
                xs.wrote(ld)
                for kk in range(4):
                    k = 4 * grp + kk
                    sq = sqr.next()
                    o = p.op("act", X("activation", out=sq.t[:], in_=xs.t[:, kk, :], func=AF.Square), deps=sq.wdeps() + [ld])
                    sq.wrote(o)
                    xs.read(o)
                    d = [o, c_ones]
                    if k == 0:
                        d += psS.wdeps()
                    mm = p.op("pe", X("matmul", psS.t[:], lhsT=ones[:], rhs=sq.t[:], start=(k == 0), stop=(k == KD - 1)), deps=d)
                    sq.read(mm)
                    o2 = p.op("dve", X("tensor_scalar", out=XN.t[:, k, :], in0=xs.t[:, kk, :], scalar1=ga[:, k:k + 1], scalar2=None, op0=ALU.mult), deps=[ld, dc[0]] + XN.wdeps())
                    xs.read(o2)
                    if first:
                        XN.wrote(o2)
                        first = False
                    else:
                        XN.also_wrote(o2)
            psS.wrote(mm)
            r_op = rstd_from(kb, psS, R, 1.0 / D, [mm])
            R.wrote(r_op)
            bank = psA.next()
            d0 = bank.wdeps() + [r_op, c_one1]
            for tt in range(4):
                mm = p.op("pe", X("matmul", bank.t[:, tt:tt + 1], lhsT=R.t[0:1, tt * 128:(tt + 1) * 128], rhs=one1[0:1, 0:1], start=True, stop=True), deps=d0 if tt == 0 else [])
            bank.wrote(mm)
            R.read(mm)
            o = p.op("dve", X("tensor_copy", out=rcol.t[:, 0:4], in_=bank.t[:, 0:4]), deps=[mm] + rcol.wdeps())
            bank.read(o)
            o = p.op("dve", X("tensor_scalar", out=rcol.t[:, 4:8], in0=rcol.t[:, 0:4], scalar1=DK ** -0.5, scalar2=None, op0=ALU.mult), deps=[o])
            rcol.wrote(o)

            s = load_wA("g", 0, 4)
            bank_i, mm_i = proj_fm(s, 0, 2)
            bank_f, mm_f = proj_fm(s, 2, 2)
            V = GV
            o = p.op("dve", X("tensor_tensor", out=V[0].t[:], in0=bank_i.t[0:2, :], in1=R.t[0:2, :], op=ALU.mult), deps=[mm_i, r_op] + V[0].wdeps())
            bank_i.read(o)
            o = p.op("act", X("activation", out=V[0].t[:], in_=V[0].t[:], func=AF.Tanh, scale=1.0 / GATE_CAP, bias=bg15[:, 0:1]), deps=[o, c_bg15])
            o_li = p.op("dve", X("tensor_scalar", out=V[0].t[:], in0=V[0].t[:], scalar1=GATE_CAP, scalar2=None, op0=ALU.mult), deps=[o])
            V[0].wrote(o_li)
            o = p.op("dve", X("tensor_tensor", out=V[1].t[:], in0=bank_f.t[0:2, :], in1=R.t[0:2, :], op=ALU.mult), deps=[mm_f, r_op] + V[1].wdeps())
            bank_f.read(o)
            R.read(o)
            o = p.op("act", X("activation", out=V[1].t[:], in_=V[1].t[:], func=AF.Tanh, scale=1.0 / GATE_CAP, bias=bg15[:, 1:2]), deps=[o, c_bg15])
            o = p.op("act", X("activation", out=V[1].t[:], in_=V[1].t[:], func=AF.Exp, scale=-GATE_CAP), deps=[o])
            o = p.op("act", X("activation", out=V[1].t[:], in_=V[1].t[:], func=AF.Ln, bias=one1[:, 0:1]), deps=[o, c_one1])
            o_lf = p.op("dve", X("tensor_scalar", out=V[1].t[:], in0=V[1].t[:], scalar1=-1.0, scalar2=None, op0=ALU.mult), deps=[o])
            V[1].wrote(o_lf)
            o_b = p.op("dve", X("tensor_tensor_scan", out=V[2].t[:], data0=cseg[:, 0:512], data1=V[1].t[:], initial=0.0, op0=ALU.mult, op1=ALU.add), deps=[o_lf, dc[3]] + V[2].wdeps())
            V[2].wrote(o_b)
            o_a = p.op("dve", X("tensor_tensor", out=V[3].t[:], in0=V[0].t[:], in1=V[2].t[:], op=ALU.subtract), deps=[o_li, o_b] + V[3].wdeps())
            V[3].wrote(o_a)
            b3 = V[2].t[:].rearrange("p (c s) -> p c s", s=64)
            a3 = V[3].t[:].rearrange("p (c s) -> p c s", s=64)
            o = p.op("dve", X("tensor_copy", out=gs.t[:, 0:8], in_=b3[:, :, 63]), deps=[o_b] + gs.wdeps())
            o = p.op("dve", X("tensor_reduce", out=gs.t[:, 8:16], in_=a3, axis=AX.X, op=ALU.max), deps=[o_a])
            o = p.op("dve", X("tensor_tensor", out=gs.t[:, 8:16], in0=gs.t[:, 8:16], in1=gs.t[:, 0:8], op=ALU.add), deps=[o])
            o = p.op("dve", X("tensor_tensor_scan", out=gs.t[:, 16:24], data0=gs.t[:, 0:8], data1=gs.t[:, 8:16], initial=mprev.t[:, 0:1], op0=ALU.add, op1=ALU.max), deps=[o] + mprev.rdeps())
            o = p.op("dve", X("tensor_copy", out=gs.t[:, 24:25], in_=mprev.t[:, 0:1]), deps=[o])
            o = p.op("dve", X("tensor_copy", out=gs.t[:, 25:32], in_=gs.t[:, 16:23]), deps=[o])
            o_mp = p.op("dve", X("tensor_copy", out=mprev.t[:, 0:1], in_=gs.t[:, 23:24]), deps=[o])
            mprev.wrote(o_mp)
            o = p.op("dve", X("tensor_tensor", out=gs.t[:, 32:40], in0=gs.t[:, 0:8], in1=gs.t[:, 24:32], op=ALU.add), deps=[o_mp])
            o = p.op("dve", X("tensor_tensor", out=gs.t[:, 32:40], in0=gs.t[:, 32:40], in1=gs.t[:, 16:24], op=ALU.subtract), deps=[o])
            o_dec = p.op("act", X("activation", out=gs.t[:, 32:40], in_=gs.t[:, 32:40], func=AF.Exp), deps=[o])
            gs.wrote(o_dec)
            mcur_b = gs.t[:, 24:32].unsqueeze(2).to_broadcast([2, 8, 64])
            g3 = V[4].t[:].rearrange("p (c s) -> p c s", s=64)
            o = p.op("dve", X("tensor_tensor", out=g3, in0=a3, in1=mcur_b, op=ALU.subtract), deps=[o_a, o_mp] + V[4].wdeps())
            o_g = p.op("act", X("activation", out=V[4].t[:], in_=V[4].t[:], func=AF.Exp), deps=[o])
            V[4].wrote(o_g)
            e3 = V[5].t[:].rearrange("p (c s) -> p c s", s=64)
            o = p.op("dve", X("tensor_tensor", out=e3, in0=b3, in1=mcur_b, op=ALU.add), deps=[o_b, o_mp] + V[5].wdeps())
            o_e = p.op("act", X("activation", out=V[5].t[:], in_=V[5].t[:], func=AF.Exp, scale=-1.0), deps=[o])
            V[5].wrote(o_e)
            bank = psA.next()
            d0 = bank.wdeps() + [o_g, dc[5]]
            for tt in range(4):
                mm = p.op("pe", X("matmul", bank.t[:, 2 * tt:2 * tt + 2], lhsT=V[4].t[0:2, tt * 128:(tt + 1) * 128], rhs=ci2[0:2, 0:2], start=True, stop=True), deps=d0 if tt == 0 else [])
            V[4].read(mm)
            for h in range(NH):
                mm = p.op("pe", X("matmul", bank.t[:, 16 + 8 * h:24 + 8 * h], lhsT=csel[0:2, h * 128:(h + 1) * 128], rhs=gs.t[0:2, 32:40], start=True, stop=True), deps=[o_dec, dc[4]])
            gs.read(mm)
            bank.wrote(mm)
            o = p.op("act", X("copy", out=G128.t[:].rearrange("p t h -> p (t h)"), in_=bank.t[:, 0:8]), deps=[mm] + G128.wdeps())
            G128.wrote(o)
            o2 = p.op("act", X("copy", out=DECb.t[:].rearrange("p c h -> p h c"), in_=bank.t[:, 16:32].rearrange("p (h c) -> p h c", h=NH)), deps=[mm] + DECb.wdeps())
            DECb.wrote(o2)
            bank.read(o2)
            o = p.op("dve", X("tensor_tensor", out=SC.t[:], in0=G128.t[:], in1=rcol.t[:, 4:8].unsqueeze(2).to_broadcast([128, 4, NH]), op=ALU.mult), deps=[o] + rcol.rdeps() + SC.wdeps())
            SC.wrote(o)
            G128.read(o)
            first = True
            for h in range(NH):
                bank = psA.next()
                mm = p.op("pe", X("matmul", bank.t[:], lhsT=csel[0:2, h * 128:(h + 1) * 128], rhs=V[5].t[0:2, :], start=True, stop=True), deps=bank.wdeps() + [o_e, dc[4]])
                bank.wrote(mm)
                V[5].read(mm)
                o = p.op("act", X("copy", out=EMWb.t[:, h, :], in_=bank.t[:]), deps=[mm] + EMWb.wdeps())
                bank.read(o)
                if first:
                    EMWb.wrote(o)
                    first = False
                else:
                    EMWb.also_wrote(o)

            fq = fk = fkg = fv = fo = True
            for h in range(NH):
                s = load_wA("q", h * 256)
                for dci in range(DKC):
                    bank, mm = proj_fm(s, dci * 128)
                    o = p.op("dve", X("tensor_tensor", out=QT.t[:, h * DKC + dci, :], in0=bank.t[:], in1=R.t[:], op=ALU.mult), deps=[mm, r_op] + QT.wdeps())
                    bank.read(o)
                    R.read(o)
                    QT.wrote(o) if fq else QT.also_wrote(o)
                    fq = False
            for h in range(NH):
                s = load_wA("k", h * 256)
                for dci in range(DKC):
                    bank, mm = proj_fm(s, dci * 128)
                    o = p.op("dve", X("scalar_tensor_tensor", out=KT.t[:, h * DKC + dci, :], in0=bank.t[:], scalar=DK ** -0.5, in1=R.t[:], op0=ALU.mult, op1=ALU.mult), deps=[mm, r_op] + KT.wdeps())
                    bank.read(o)
                    R.read(o)
                    KT.wrote(o) if fk else KT.also_wrote(o)
                    fk = False
                for tt in range(4):
                    bank, mm = proj_tm(s, tt)
                    o = p.op("act", X("activation", out=KG.t[:, tt, h, :], in_=bank.t[:, 0:256], func=AF.Copy, scale=SC.t[:, tt, h:h + 1]), deps=[mm] + SC.rdeps() + KG.wdeps())
                    bank.read(o)
                    SC.read(o)
                    KG.wrote(o) if fkg else KG.also_wrote(o)
                    fkg = False
            for h in range(NH):
                for half in range(DV // 256):
                    s = load_wA("v", h * DV + half * 256)
                    for tt in range(4):
                        bank, mm = proj_tm(s, tt)
                        o = p.op("act", X("activation", out=VT.t[:, tt, h, half * 256:(half + 1) * 256], in_=bank.t[:, 0:256], func=AF.Copy, scale=rcol.t[:, tt:tt + 1]), deps=[mm] + rcol.rdeps() + VT.wdeps())
                        bank.read(o)
                        rcol.read(o)
                        VT.wrote(o) if fv else VT.also_wrote(o)
                        fv = False
            for h in range(NH):
                for half in range(DV // 256):
                    s = load_wA("o", h * DV + half * 256)
                    for ci in range(2):
                        ec = half * 2 + ci
                        bank, mm = proj_fm(s, ci * 128)
                        ot = OT.next()
                        o1 = p.op("dve", X("tensor_tensor", out=ot.t[:], in0=bank.t[:], in1=R.t[:], op=ALU.mult), deps=[mm, r_op] + ot.wdeps())
                        bank.read(o1)
                        R.read(o1)
                        o2 = p.op("act", X("activation", out=ot.t[:], in_=ot.t[:], func=AF.Sigmoid), deps=[o1])
                        o3 = p.op("pool", X("tensor_scalar", out=OG.t[:, h * DVC + ec, :], in0=ot.t[:], scalar1=hgain[:, h * DVC + ec:h * DVC + ec + 1], scalar2=None, op0=ALU.mult), deps=[o2, dc[1]] + OG.wdeps())
                        ot.wrote(o3)
                        OG.wrote(o3) if fo else OG.also_wrote(o3)
                        fo = False

            fhg = [True]

            def finish(item):
                (pc, ht, h, c, o_ht) = item
                hs = HSQ.next()
                o_sq = p.op("act", X("activation", out=hs.t[:], in_=ht.t[:], func=AF.Square), deps=[o_ht] + hs.wdeps())
                hs.wrote(o_sq)
                mm = None
                for ec in range(DVC):
                    mm = p.op("pe", X("matmul", pc.t[:, 384:448], lhsT=ones[:], rhs=hs.t[:, ec, :], start=(ec == 0), stop=(ec == DVC - 1)), deps=([o_sq] + pc.wdeps()) if ec == 0 else [])
                hs.read(mm)
                pc.also_wrote(mm)
                rs = RS.next()
                o1 = p.op("act", X("activation", out=rs.t[:], in_=pc.t[:, 384:448], func=AF.Sqrt, scale=1.0 / DV, bias=kb.eps[:, 0:1]), deps=[mm, kb.eps_op] + rs.wdeps())
                pc.read(o1)
                o2 = p.op("dve", X("reciprocal", out=rs.t[:], in_=rs.t[:]), deps=[o1])
                o3 = p.op("dve", X("tensor_tensor", out=ht.t[:], in0=ht.t[:], in1=rs.t[:].unsqueeze(1).to_broadcast([128, DVC, 64]), op=ALU.mult), deps=[o2, o_sq])
                rs.wrote(o3)
                o4 = p.op("dve", X("tensor_tensor", out=HG.t[:, h * DVC:(h + 1) * DVC, c * 64:(c + 1) * 64], in0=ht.t[:], in1=OG.t[:, h * DVC:(h + 1) * DVC, c * 64:(c + 1) * 64], op=ALU.mult), deps=[o3] + OG.rdeps() + HG.wdeps())
                ht.wrote(o4)
                OG.read(o4)
                HG.wrote(o4) if fhg[0] else HG.also_wrote(o4)
                fhg[0] = False

            for c in range(8):
                tt = c // 2
                pb = 64 * (c % 2)
                csl = slice(c * 64, (c + 1) * 64)
                tl = slice(tt * 128, (tt + 1) * 128)
                for h in range(NH):
                    pc = psc.next()
                    qs = [h * DKC + i for i in range(DKC)]
                    d0 = pc.wdeps() + QT.rdeps() + KT.rdeps()
                    mm = None
                    for i, qi in enumerate(qs):
                        mm = p.op("pe", X("matmul", pc.t[:, 0:64], lhsT=KT.t[:, qi, tl], rhs=QT.t[:, qi, csl], start=(i == 0), stop=(i == DKC - 1)), deps=d0 if i == 0 else [])
                    pc.wrote(mm)
                    QT.read(mm)
                    KT.read(mm)
                    sw = SW.next()
                    o_sw = p.op("dve", X("scalar_tensor_tensor", out=sw.t[pb:pb + 64, :], in0=pc.t[pb:pb + 64, 0:64], scalar=G128.t[pb:pb + 64, tt, h:h + 1], in1=causal[pb:pb + 64, :], op0=ALU.mult, op1=ALU.mult),
                                deps=[mm, dc[2]] + G128.rdeps() + sw.wdeps())
                    sw.wrote(o_sw)
                    pc.read(o_sw)
                    d0 = [o_sw] + VT.rdeps() + Cb[qs[0]].rdeps() + Cb[qs[1]].rdeps() + nrep[qs[0]].rdeps() + nrep[qs[1]].rdeps()
                    for ec in range(DVC):
                        for i, qi in enumerate(qs):
                            mm = p.op("pe", X("matmul", pc.t[:, 64 + 64 * ec:128 + 64 * ec], lhsT=Cb[qi].t[:, ec * 128:(ec + 1) * 128], rhs=QT.t[:, qi, csl], start=(i == 0), stop=False), deps=d0 if (ec == 0 and i == 0) else [])
                        mm = p.op("pe", X("matmul", pc.t[:, 64 + 64 * ec:128 + 64 * ec], lhsT=VT.t[pb:pb + 64, tt, h, ec * 128:(ec + 1) * 128], rhs=sw.t[pb:pb + 64, :], start=False, stop=True))
                    for i, qi in enumerate(qs):
                        mm = p.op("pe", X("matmul", pc.t[:, 320:384], lhsT=nrep[qi].t[:], rhs=QT.t[:, qi, csl], start=(i == 0), stop=False))
                    mm_nd = p.op("pe", X("matmul", pc.t[:, 320:384], lhsT=ones[pb:pb + 64, :], rhs=sw.t[pb:pb + 64, :], start=False, stop=True))
                    pc.also_wrote(mm_nd)
                    sw.read(mm_nd)
                    for qi in qs:
                        Cb[qi].read(mm_nd)
                        nrep[qi].read(mm_nd)
                    VT.read(mm_nd)
                    QT.read(mm_nd)
                    mmcs = []
                    bcs = []
                    for i, qi in enumerate(qs):
                        bc = psC.next()
                        d0 = bc.wdeps() + KG.rdeps() + VT.rdeps()
                        mmc = p.op("pe", X("matmul", bc.t[:], lhsT=KG.t[pb:pb + 64, tt, h, i * 128:(i + 1) * 128], rhs=VT.t[pb:pb + 64, tt, h, :], start=True, stop=True), deps=d0)
                        bc.wrote(mmc)
                        mmcs.append(mmc)
                        bcs.append(bc)
                    mmn = None
                    for i, qi in enumerate(qs):
                        mmn = p.op("pe", X("matmul", pc.t[:, 448 + i:449 + i], lhsT=KG.t[pb:pb + 64, tt, h, i * 128:(i + 1) * 128], rhs=ones[pb:pb + 64, 0:1], start=True, stop=True))
                    pc.also_wrote(mmn)
                    KG.read(mmn)
                    VT.read(mmn)
                    for i, qi in enumerate(qs):
                        bc = bcs[i]
                        mmc = mmcs[i]
                        dcol = DECb.t[:, c, h:h + 1]
                        o_cd = p.op("act", X("activation", out=Cd[qi].t[:], in_=C32[qi].t[:], func=AF.Copy, scale=dcol), deps=Cd[qi].wdeps() + C32[qi].rdeps() + DECb.rdeps())
                        Cd[qi].wrote(o_cd)
                        o_c = p.op("dve", X("scalar_tensor_tensor", out=C32[qi].t[:], in0=bc.t[:], scalar=dcol, in1=Cd[qi].t[:], op0=ALU.mult, op1=ALU.add), deps=[mmc, o_cd] + DECb.rdeps() + C32[qi].wdeps())
                        bc.read(o_c)
                        Cd[qi].read(o_c)
                        C32[qi].wrote(o_c)
                        o_cb = p.op("pool", X("tensor_copy", out=Cb[qi].t[:], in_=C32[qi].t[:]), deps=[o_c] + Cb[qi].wdeps())
                        Cb[qi].wrote(o_cb)
                        C32[qi].read(o_cb)
                        o_n1 = p.op("dve", X("tensor_tensor", out=n32[qi].t[:, 0:1], in0=n32[qi].t[:, 0:1], in1=pc.t[:, 448 + i:449 + i], op=ALU.add), deps=[mmn] + n32[qi].wdeps())
                        o_n2 = p.op("dve", X("tensor_tensor", out=n32[qi].t[:, 0:1], in0=n32[qi].t[:, 0:1], in1=dcol, op=ALU.mult), deps=[o_n1])
                        n32[qi].wrote(o_n2)
                        pc.read(o_n1)
                        o_nr = p.op("pool", X("tensor_copy", out=nrep[qi].t[:], in_=n32[qi].t[:, 0:1].to_broadcast([128, 128])), deps=[o_n2] + nrep[qi].wdeps())
                        nrep[qi].wrote(o_nr)
                        n32[qi].read(o_nr)
                    DECb.read(o_c)
                    dn = DN.next()
                    o0 = p.op("act", X("activation", out=dn.t[:], in_=pc.t[:, 320:384], func=AF.Abs), deps=[mm_nd, mmn] + dn.wdeps())
                    pc.read(o0)
                    o1 = p.op("dve", X("tensor_tensor", out=dn.t[:], in0=dn.t[:], in1=EMWb.t[:, h, csl], op=ALU.max), deps=[o0] + EMWb.rdeps())
                    EMWb.read(o1)
                    o2 = p.op("dve", X("reciprocal", out=dn.t[:], in_=dn.t[:]), deps=[o1])
                    ht = HTt.next()
                    o_ht = p.op("dve", X("tensor_tensor", out=ht.t[:], in0=pc.t[:, 64:320].rearrange("p (e t) -> p e t", e=DVC), in1=dn.t[:].unsqueeze(1).to_broadcast([128, DVC, 64]), op=ALU.mult), deps=[o2, mm_nd, mmn] + ht.wdeps())
                    dn.wrote(o_ht)
                    ht.wrote(o_ht)
                    pc.read(o_ht)
                    if pend[0] is not None:
                        finish(pend[0])
                    pend[0] = (pc, ht, h, c, o_ht)
            finish(pend[0])
            pend[0] = None
            o = p.dma("sp", out_v[:, :, tsl], HG.t[:], deps=HG.rdeps(), ring="hgout", nslots=1)
            HG.read(o)
            out_dmas.append(o)
        p.wait("sp", out_dmas)
        p.emit(ctx)
    return nc


def build_C(S, HD=128, HPG=8, CMP_BLOCK=32, CMP_STRIDE=16, CMP_HID=512, SEL_BLOCK=64, N_SEL=16, WINDOW=512):
    nc = bass.Bass("TRN2", target_bir_lowering=False)
    NQB = S // 64
    NCMP = (S - CMP_BLOCK) // CMP_STRIDE + 1
    NCC = (NCMP + 127) // 128
    NSB = S // SEL_BLOCK
    NKC = S // 128
    HC = CMP_HID // 128
    scale = HD ** -0.5
    with ExitStack() as ctx:
        kb = KB(nc, ctx)
        p = kb.p
        qT_d = kb.din("qT", [HPG * HD, S], BF16)
        gT_d = kb.din("gT", [3 * HPG, S], F32)
        kcT_d = kb.din("kcT", [HD, S], BF16)
        vcT_d = kb.din("vcT", [HD, S], BF16)
        ksT_d = kb.din("ksT", [HD, S], BF16)
        kwT_d = kb.din("kwT", [HD, S], BF16)
        vs_d = kb.din("vs", [S, HD], BF16)
        vw_d = kb.din("vw", [S, HD], BF16)
        w1_d = kb.din("cmp_w1", [2, CMP_BLOCK * HD, CMP_HID], F32)
        w2_d = kb.din("cmp_w2", [2, CMP_HID, HD], F32)
        posT_d = kb.din("posT", [HD, 2 * CMP_BLOCK], F32)
        kng_d = kb.din("kng", [HD, 3], F32)
        qng_d = kb.din("qng", [HD, 3], F32)
        ov_d = kb.din("c_ov", [128, NCC * NSB], F32)
        ex_d = kb.din("c_ex", [64, NKC * 128], F32)
        selg_d = kb.din("c_selg", [3 * HPG, 3 * HPG * 128], F32)
        i64_d = kb.din("c_i64", [64, 64], F32)
        o_d = kb.dout("oT", [HPG * HD, S], BF16)

        Q = Buf(kb.sb("Q", [128, HPG, S], BF16))
        GT = Buf(kb.sb("GT", [3 * HPG, S], F32))
        KS = Buf(kb.sb("KS", [128, S], BF16))
        KW = Buf(kb.sb("KW", [128, S], BF16))
        VS = Buf(kb.sb("VS", [128, NKC, HD], BF16))
        VW = Buf(kb.sb("VW", [128, NKC, HD], BF16))
        KC = Buf(kb.sb("KC", [128, NCC * 128], BF16))
        VC = Buf(kb.sb("VC", [128, NCC, HD], BF16))
        RAW = kb.sbring("RAW", 1, [128, S], BF16)
        if HPG * S >= 2 * CMP_BLOCK * CMP_HID:
            hq = HPG // 2
            W1 = Buf(Q.t[:, hq:HPG, :].rearrange("p h s -> p (h s)")[:, 0:CMP_BLOCK * CMP_HID].rearrange("p (i n) -> p i n", i=CMP_BLOCK))
            w1_alias = True
        else:
            W1 = Buf(kb.sb("W1", [128, CMP_BLOCK, CMP_HID], BF16))
            w1_alias = False
        W2 = Buf(kb.sb("W2", [128, 2, HC, HD], BF16))
        posT = kb.sb("posT_sb", [128, 2 * CMP_BLOCK], F32)
        posTb = kb.sb("posT_bf", [128, 2 * CMP_BLOCK], BF16)
        kng = kb.sb("kng_sb", [128, 3], F32)
        qng = kb.sb("qng_sb", [128, 3], F32)
        kcg = kb.sb("kcg", [128, 1], F32)
        ov = kb.sb("ov", [128, NCC * NSB], BF16)
        ex = kb.sb("ex", [64, NKC * 128], BF16)
        selg = kb.sb("selg", [3 * HPG, 3 * HPG * 128], F32)
        i64 = kb.sb("i64", [64, 64], BF16)
        ones = kb.sb("ones", [128, 128], BF16)
        tiny = kb.sb("tiny", [128, 1], F32)
        HID = kb.sbring("HID", 2, [128, HC, 256], BF16)
        Z = kb.sbring("Z", 2, [128, 256], F32)
        Z2 = kb.sbring("Z2", 2, [128, 256], F32)
        pbias = Buf(kb.sb("pbias", [128, 2 * HC], F32))
        PT = kb.sbring("PT", 4, [128, HPG, 64], BF16)
        PN = kb.sbring("PN", 2, [128, HPG, 64], BF16)
        RD = kb.sbring("RD", 2, [128, 512], F32)
        CF = kb.sbring("CF", 2, [128, 512], F32)
        OACC = Buf(kb.sb("OACC", [128, 512], F32))
        OB = kb.sbring("OBo", 2, [128, HPG, 64], BF16)
        PM = kb.sbring("PMx", 2, [64, 64], F32)
        PM2 = kb.sbring("PM2", 2, [64, 64], F32)
        M8 = kb.sbring("M8", 2, [64, 16], F32)
        SEL = kb.sbring("SEL", 2, [64, 64], BF16)
        SELT = kb.sbring("SELT", 2, [64, 64], BF16)
        sqb = kb.sbring("SQc", 2, [128, 256], BF16)
        RK = Buf(kb.sb("RK", [128, 256], F32))

        psS = kb.psring("psS", 2)
        psN = kb.psring("psN", 2)
        psD = kb.psring("psD", 2)
        psM = Buf(kb.psum("psM", [128, 512]))
        psX = Buf(kb.psum("psX", [128, 512]))
        psB = psX

        c_ones = p.op("dve", X("memset", ones[:], 1.0))
        kb.eps = kb.sb("eps_sb", [128, 1], F32)
        kb.eps_op = p.op("dve", X("memset", kb.eps[:], EPS))
        c_tiny = p.op("dve", X("memset", tiny[:], 1e-30))
        dq = []
        for h in range(HPG // 2 if w1_alias else HPG):
            dq.append(p.dma("sp", Q.t[:, h, :], qT_d[h * 128:(h + 1) * 128, :], ring="ld", nslots=24))
        Q.wrote(dq[0])
        for o in dq[1:]:
            Q.also_wrote(o)
        o = p.dma("sp", GT.t[:], gT_d[:, :], ring="ld", nslots=24); GT.wrote(o)
        o = p.dma("sp", KS.t[:], ksT_d[:, :], ring="ld", nslots=24); KS.wrote(o)
        o = p.dma("sp", KW.t[:], kwT_d[:, :], ring="ld", nslots=24); KW.wrote(o)
        o = p.dma("sp", VS.t[:], vs_d.rearrange("(c p) d -> p c d", p=128), ring="ld", nslots=24); VS.wrote(o)
        o = p.dma("sp", VW.t[:], vw_d.rearrange("(c p) d -> p c d", p=128), ring="ld", nslots=24); VW.wrote(o)
        d_pos = p.dma("sp", posT[:], posT_d[:, :], ring="ld", nslots=24)
        d_kng = p.dma("sp", kng[:], kng_d[:, :], ring="ld", nslots=24)
        d_qng = p.dma("sp", qng[:], qng_d[:, :], ring="ld", nslots=24)
        d_selg = p.dma("sp", selg[:], selg_d[:, :], ring="ld", nslots=24)
        c_ov = p.dma("pool", ov[:], ov_d[:, :], ring="ldc", nslots=4)
        c_i64 = p.dma("pool", i64[:], i64_d[:, :], ring="ldc", nslots=4)
        c_ex = p.dma("pool", ex[:], ex_d[:, :], ring="ldc", nslots=4)
        c_pos = p.op("dve", X("tensor_copy", out=posTb[:], in_=posT[:]), deps=[d_pos])
        o = p.op("dve", X("tensor_tensor", out=kcg[:], in0=kng[:, 0:1], in1=qng[:, 0:1], op=ALU.mult), deps=[d_kng, d_qng])
        c_kcg = p.op("dve", X("tensor_scalar", out=kcg[:], in0=kcg[:], scalar1=scale, scalar2=None, op0=ALU.mult), deps=[o])
        z0 = p.op("pool", X("memset", KC.t[:], 0.0)); KC.wrote(z0)
        z1 = p.op("pool", X("memset", VC.t[:], 0.0)); VC.wrote(z1)
        w2l = p.dma("pool", W2.t[:], w2_d.rearrange("j (c p) d -> p j c d", p=128), ring="w2", nslots=1)
        W2.wrote(w2l)

        for j in range(2):
            raw = RAW.next()
            dr = p.dma("sp", raw.t[:], (kcT_d if j == 0 else vcT_d)[:, :], deps=raw.wdeps(), ring="ldr", nslots=2)
            raw.wrote(dr)
            dw = p.dma("pool", W1.t[:], w1_d[j].rearrange("(i p) n -> p i n", p=128), deps=W1.wdeps(), ring="w1", nslots=1)
            W1.wrote(dw)
            bank = psB
            d0 = bank.wdeps() + [dw, c_pos]
            mm = None
            for hc in range(HC):
                for i in range(CMP_BLOCK):
                    mm = p.op("pe", X("matmul", bank.t[:, hc:hc + 1], lhsT=W1.t[:, i, hc * 128:(hc + 1) * 128], rhs=posTb[:, j * CMP_BLOCK + i:j * CMP_BLOCK + i + 1], start=(i == 0), stop=(i == CMP_BLOCK - 1)), deps=d0 if (hc == 0 and i == 0) else [])
            bank.wrote(mm)
            o_pb = p.op("dve", X("tensor_copy", out=pbias.t[:, j * HC:(j + 1) * HC], in_=bank.t[:, 0:HC]), deps=[mm] + pbias.wdeps())
            bank.read(o_pb)
            pbias.wrote(o_pb)
            hid = HID.next()
            first = True
            for hc in range(HC):
                for c0 in range(0, NCMP, 256):
                    ncol = min(256, NCMP - c0)
                    bank = psS.next()
                    d0 = bank.wdeps() + [dw, dr]
                    for i in range(CMP_BLOCK):
                        t0 = CMP_STRIDE * c0 + i
                        mm = p.op("pe", X("matmul", bank.t[:, 0:ncol], lhsT=W1.t[:, i, hc * 128:(hc + 1) * 128], rhs=raw.t[:, t0:t0 + CMP_STRIDE * (ncol - 1) + 1:CMP_STRIDE], start=(i == 0), stop=(i == CMP_BLOCK - 1)), deps=d0 if i == 0 else [])
                    bank.wrote(mm)
                    W1.read(mm)
                    raw.read(mm)
                    z = Z.next()
                    z2 = Z2.next()
                    o1 = p.op("act", X("activation", out=z.t[:, 0:ncol], in_=bank.t[:, 0:ncol], func=AF.Identity, bias=pbias.t[:, j * HC + hc:j * HC + hc + 1]), deps=[mm, o_pb] + z.wdeps())
                    bank.read(o1)
                    o2 = p.op("dve", X("tensor_tensor", out=z2.t[:, 0:ncol], in0=z.t[:, 0:ncol], in1=z.t[:, 0:ncol], op=ALU.mult), deps=[o1] + z2.wdeps())
                    o3 = p.op("dve", X("tensor_scalar", out=z2.t[:, 0:ncol], in0=z2.t[:, 0:ncol], scalar1=0.044715, scalar2=1.0, op0=ALU.mult, op1=ALU.add), deps=[o2])
                    o4 = p.op("dve", X("tensor_tensor", out=z2.t[:, 0:ncol], in0=z2.t[:, 0:ncol], in1=z.t[:, 0:ncol], op=ALU.mult), deps=[o3])
                    o5 = p.op("act", X("activation", out=z2.t[:, 0:ncol], in_=z2.t[:, 0:ncol], func=AF.Sigmoid, scale=1.5957691216057308), deps=[o4])
                    o6 = p.op("dve", X("tensor_tensor", out=hid.t[:, hc, c0:c0 + ncol] if NCMP <= 256 else hid.t[:, hc, 0:ncol], in0=z.t[:, 0:ncol], in1=z2.t[:, 0:ncol], op=ALU.mult), deps=[o5] + hid.wdeps())
                    z.wrote(o6)
                    z2.wrote(o6)
                    hid.wrote(o6) if first else hid.also_wrote(o6)
                    first = False
            assert NCMP <= 256
            if j == 0:
                bank = psS.next()
                d0 = bank.wdeps() + hid.rdeps() + [w2l]
                for hc in range(HC):
                    mm = p.op("pe", X("matmul", bank.t[:, 0:NCMP], lhsT=W2.t[:, 0, hc, :], rhs=hid.t[:, hc, 0:NCMP], start=(hc == 0), stop=(hc == HC - 1)), deps=d0 if hc == 0 else [])
                bank.wrote(mm)
                hid.read(mm)
                sq = sqb.next()
                o_sq = p.op("act", X("activation", out=sq.t[:, 0:NCMP], in_=bank.t[:, 0:NCMP], func=AF.Square), deps=[mm] + sq.wdeps())
                sq.wrote(o_sq)
                bank.read(o_sq)
                mm2 = p.op("pe", X("matmul", psB.t[:, 0:NCMP], lhsT=ones[:], rhs=sq.t[:, 0:NCMP], start=True, stop=True), deps=[o_sq, c_ones] + psB.wdeps())
                psB.wrote(mm2)
                sq.read(mm2)
                o1 = p.op("act", X("activation", out=RK.t[:, 0:NCMP], in_=psB.t[:, 0:NCMP], func=AF.Sqrt, scale=1.0 / HD, bias=kb.eps[:, 0:1]), deps=[mm2, kb.eps_op] + RK.wdeps())
                psB.read(o1)
                o2 = p.op("dve", X("reciprocal", out=RK.t[:, 0:NCMP], in_=RK.t[:, 0:NCMP]), deps=[o1])
                o3 = p.op("dve", X("scalar_tensor_tensor", out=KC.t[:, 0:NCMP], in0=bank.t[:, 0:NCMP], scalar=kcg[:, 0:1], in1=RK.t[:, 0:NCMP], op0=ALU.mult, op1=ALU.mult), deps=[o2, c_kcg, mm] + KC.wdeps())
                bank.read(o3)
                RK.wrote(o3)
                KC.wrote(o3)
            else:
                for cc in range(NCC):
                    ncol = min(128, NCMP - cc * 128)
                    bank = psS.next()
                    d0 = bank.wdeps() + hid.rdeps() + [w2l]
                    for hc in range(HC):
                        mm = p.op("pe", X("matmul", bank.t[0:ncol, 0:HD], lhsT=hid.t[:, hc, cc * 128:cc * 128 + ncol], rhs=W2.t[:, 1, hc, :], start=(hc == 0), stop=(hc == HC - 1)), deps=d0 if hc == 0 else [])
                    bank.wrote(mm)
                    hid.read(mm)
                    o = p.op("act", X("copy", out=VC.t[0:ncol, cc, :], in_=bank.t[0:ncol, 0:HD]), deps=[mm] + VC.wdeps())
                    bank.read(o)
                    VC.wrote(o) if cc == 0 else VC.also_wrote(o)

        if w1_alias:
            for h in range(HPG // 2, HPG):
                o = p.dma("sp", Q.t[:, h, :], qT_d[h * 128:(h + 1) * 128, :], deps=W1.wdeps(), ring="ld", nslots=24)
                Q.also_wrote(o)
        out_dmas = []
        o_v = o_d.rearrange("(h p) t -> p h t", p=128)
        for qb in range(NQB):
            qsl = slice(qb * 64, (qb + 1) * 64)
            t0 = qb * 64
            rhsq = Q.t[:, :, qsl]
            need_topk = (qb + 1) > N_SEL
            kc_last = qb // 2
            st = {"selT": None, "pts": [], "rd": None}

            def stage1(job):
                (br, kT_ap, v_ap, kdeps, first, last, mask_fn) = job
                bank = psS.next()
                mm = p.op("pe", X("matmul", bank.t[:], lhsT=kT_ap, rhs=rhsq, start=True, stop=True), deps=bank.wdeps() + Q.rdeps() + kdeps)
                bank.wrote(mm)
                pt = PT.next()
                o = p.op("act", X("activation", out=pt.t[:].rearrange("p h q -> p (h q)"), in_=bank.t[:], func=AF.Exp), deps=[mm] + pt.wdeps())
                bank.read(o)
                pt.wrote(o)
                o = mask_fn(pt, o)
                return (pt, o)

            def stage2(job, s1, acc):
                (br, kT_ap, v_ap, kdeps, first, last, mask_fn) = job
                pt, o = s1
                bN, bD = acc
                d0 = [o]
                if first:
                    d0 = d0 + bN.wdeps() + bD.wdeps()
                ptf = pt.t[:].rearrange("p h q -> p (h q)")
                m1 = p.op("pe", X("matmul", bN.t[:], lhsT=v_ap, rhs=ptf, start=first, stop=last), deps=d0 + kdeps)
                m2 = p.op("pe", X("matmul", bD.t[:], lhsT=ones[:], rhs=ptf, start=first, stop=last), deps=[c_ones])
                pt.read(m2)
                if last:
                    bN.wrote(m1)
                    bD.wrote(m2)
                return m2

            def finish_branch(r, m_last, first_branch, acc):
                bN, bD = acc
                d0 = psX.wdeps() + GT.rdeps() + [d_selg]
                mm = None
                for h in range(HPG):
                    row = r * HPG + h
                    mm = p.op("pe", X("matmul", psX.t[:, h * 64:(h + 1) * 64], lhsT=selg[:, row * 128:(row + 1) * 128], rhs=GT.t[:, qsl], start=True, stop=True), deps=d0 if h == 0 else [])
                psX.wrote(mm)
                rd = RD.next()
                o1 = p.op("dve", X("tensor_scalar", out=rd.t[:], in0=bD.t[:], scalar1=tiny[:, 0:1], scalar2=None, op0=ALU.max), deps=[m_last, c_tiny] + rd.wdeps())
                bD.read(o1)
                o2 = p.op("dve", X("reciprocal", out=rd.t[:], in_=rd.t[:]), deps=[o1])
                rd.wrote(o2)
                cf = CF.next()
                o3 = p.op("dve", X("tensor_tensor", out=cf.t[:], in0=psX.t[:], in1=rd.t[:], op=ALU.mult), deps=[o2, mm] + cf.wdeps())
                psX.read(o3)
                rd.read(o3)
                if first_branch:
                    o4 = p.op("dve", X("tensor_tensor", out=OACC.t[:], in0=bN.t[:], in1=cf.t[:], op=ALU.mult), deps=[o3, m_last] + OACC.wdeps())
                else:
                    o4a = p.op("dve", X("tensor_tensor", out=cf.t[:], in0=bN.t[:], in1=cf.t[:], op=ALU.mult), deps=[o3, m_last])
                    o4 = p.op("dve", X("tensor_tensor", out=OACC.t[:], in0=OACC.t[:], in1=cf.t[:], op=ALU.add), deps=[o4a] + OACC.wdeps())
                cf.wrote(o4)
                bN.read(o4)
                OACC.wrote(o4)
                return rd, o2

            def topk(rd, o_rd, pts):
                pns = []
                for cc, pt in enumerate(pts):
                    pn = PN.next()
                    o = p.op("dve", X("tensor_tensor", out=pn.t[:].rearrange("p h q -> p (h q)"), in0=pt.t[:].rearrange("p h q -> p (h q)"), in1=rd.t[:], op=ALU.mult), deps=[o_rd] + pt.rdeps() + pn.wdeps())
                    pn.wrote(o)
                    pt.read(o)
                    rd.read(o)
                    pns.append((pn, o))
                d0 = psX.wdeps() + [c_ov]
                n_mm = len(pns) * HPG
                i = 0
                mm = None
                for cc, (pn, o) in enumerate(pns):
                    for h in range(HPG):
                        mm = p.op("pe", X("matmul", psX.t[0:64, 0:NSB], lhsT=pn.t[:, h, :], rhs=ov[:, cc * NSB:(cc + 1) * NSB], start=(i == 0), stop=(i == n_mm - 1)), deps=(d0 if i == 0 else []) + ([o] if h == 0 else []))
                        i += 1
                    pn.read(mm)
                psX.wrote(mm)
                pm = PM.next()
                o1 = p.op("dve", X("tensor_copy", out=pm.t[:, 0:NSB], in_=psX.t[0:64, 0:NSB]), deps=[mm] + pm.wdeps())
                psX.read(o1)
                o2 = p.op("dve", X("memset", pm.t[:, 0:1], 1e30), deps=[o1])
                o3 = p.op("dve", X("memset", pm.t[:, qb:qb + 1], 1e30), deps=[o2])
                if qb + 1 < NSB:
                    o3 = p.op("dve", X("memset", pm.t[:, qb + 1:NSB], -1.0), deps=[o3])
                m8 = M8.next()
                o4 = p.op("dve", X("max", out=m8.t[:, 0:8], in_=pm.t[:, 0:NSB]), deps=[o3] + m8.wdeps())
                pm2 = PM2.next()
                o5 = p.op("dve", X("match_replace", out=pm2.t[:, 0:NSB], in_to_replace=m8.t[:, 0:8], in_values=pm.t[:, 0:NSB], imm_value=-2.0), deps=[o4] + pm2.wdeps())
                o6 = p.op("dve", X("max", out=m8.t[:, 8:16], in_=pm2.t[:, 0:NSB]), deps=[o5])
                pm2.wrote(o6)
                sel = SEL.next()
                o7 = p.op("dve", X("tensor_scalar", out=sel.t[:, 0:NSB], in0=pm.t[:, 0:NSB], scalar1=m8.t[:, 15:16], scalar2=None, op0=ALU.is_ge), deps=[o6] + sel.wdeps())
                m8.wrote(o7)
                pm.wrote(o7)
                sel.wrote(o7)
                mmT = p.op("pe", X("matmul", psX.t[0:NSB, 64:128], lhsT=sel.t[:, 0:NSB], rhs=i64[:], start=True, stop=True), deps=[o7, c_i64] + psX.wdeps())
                psX.wrote(mmT)
                sel.read(mmT)
                selT = SELT.next()
                o8 = p.op("act", X("copy", out=selT.t[0:NSB, :], in_=psX.t[0:NSB, 64:128]), deps=[mmT] + selT.wdeps())
                psX.read(o8)
                selT.wrote(o8)
                st["selT"] = selT

            jobs = []
            ncv = min(NCMP, (t0 + 63 - (CMP_BLOCK - 1)) // CMP_STRIDE + 1)
            ncc_q = max(1, (ncv + 127) // 128)
            for cc in range(ncc_q):
                def mask_cmp(pt, o, cc=cc):
                    o2 = p.op("pool", X("affine_select", out=pt.t[:], in_=pt.t[:], pattern=[[0, HPG], [1, 64]], compare_op=ALU.is_ge, fill=0.0,
                                        base=t0 - (CMP_BLOCK - 1) - CMP_STRIDE * 128 * cc, channel_multiplier=-CMP_STRIDE), deps=[o])
                    pt.wrote(o2)
                    st["pts"].append(pt)
                    return o2
                jobs.append((0, KC.t[:, cc * 128:(cc + 1) * 128], VC.t[:, cc, :], KC.rdeps() + VC.rdeps(), cc == 0, cc == ncc_q - 1, mask_cmp))
            kc_first = max(0, (t0 - (WINDOW - 1)) // 128)
            for kc in range(kc_first, kc_last + 1):
                def mask_win(pt, o, kc=kc):
                    if 128 * kc < t0 + 63 - (WINDOW - 1):
                        o = p.op("pool", X("affine_select", out=pt.t[:], in_=pt.t[:], pattern=[[0, HPG], [-1, 64]], compare_op=ALU.is_ge, fill=0.0,
                                           base=128 * kc - t0 + WINDOW - 1, channel_multiplier=1), deps=[o])
                        pt.wrote(o)
                    if 128 * kc + 127 > t0:
                        o = p.op("pool", X("affine_select", out=pt.t[:], in_=pt.t[:], pattern=[[0, HPG], [1, 64]], compare_op=ALU.is_ge, fill=0.0,
                                           base=t0 - 128 * kc, channel_multiplier=-1), deps=[o])
                        pt.wrote(o)
                    return o
                jobs.append((2, KW.t[:, kc * 128:(kc + 1) * 128], VW.t[:, kc, :], KW.rdeps() + VW.rdeps(), kc == kc_first, kc == kc_last, mask_win))
            for kc in range(kc_last + 1):
                def mask_sel(pt, o, kc=kc):
                    if need_topk:
                        selT = st["selT"]
                        bm = psM
                        mmx = p.op("pe", X("matmul", bm.t[:, 0:64], lhsT=ex[0:NSB, kc * 128:(kc + 1) * 128], rhs=selT.t[0:NSB, :], start=True, stop=True), deps=bm.wdeps() + selT.rdeps() + [c_ex])
                        bm.wrote(mmx)
                        selT.read(mmx)
                        o = p.op("dve", X("tensor_tensor", out=pt.t[:], in0=pt.t[:], in1=bm.t[:, 0:64].unsqueeze(1).to_broadcast([128, HPG, 64]), op=ALU.mult), deps=[o, mmx])
                        bm.read(o)
                        pt.wrote(o)
                    if kc == kc_last:
                        o = p.op("pool", X("affine_select", out=pt.t[:], in_=pt.t[:], pattern=[[0, HPG], [1, 64]], compare_op=ALU.is_ge, fill=0.0,
                                           base=t0 - 128 * kc, channel_multiplier=-1), deps=[o])
                        pt.wrote(o)
                    return o
                jobs.append((1, KS.t[:, kc * 128:(kc + 1) * 128], VS.t[:, kc, :], KS.rdeps() + VS.rdeps(), kc == 0, kc == kc_last, mask_sel))

            nj = len(jobs)
            accs = {}
            s1s = [None] * nj
            def can_issue(i):
                return not (jobs[i][0] == 1 and need_topk and st["selT"] is None)
            if can_issue(0):
                s1s[0] = stage1(jobs[0])
            first_branch = True
            for i in range(nj):
                if i + 1 < nj and s1s[i + 1] is None and can_issue(i + 1):
                    s1s[i + 1] = stage1(jobs[i + 1])
                if s1s[i] is None:
                    s1s[i] = stage1(jobs[i])
                br = jobs[i][0]
                if jobs[i][4]:
                    accs[br] = (psN.next(), psD.next())
                m2 = stage2(jobs[i], s1s[i], accs[br])
                if jobs[i][5]:
                    rd, o_rd = finish_branch(br, m2, first_branch, accs[br])
                    first_branch = False
                    if br == 0 and need_topk:
                        topk(rd, o_rd, st["pts"])

            ob = OB.next()
            o = p.op("act", X("copy", out=ob.t[:].rearrange("p h q -> p (h q)"), in_=OACC.t[:]), deps=OACC.rdeps() + ob.wdeps())
            OACC.read(o)
            ob.wrote(o)
            od = p.dma("sp", o_v[:, :, qsl], ob.t[:], deps=[o], ring="oout", nslots=2)
            ob.read(od)
            out_dmas.append(od)
        p.wait("sp", out_dmas)
        p.emit(ctx)
    return nc


D_MODEL = 4096
SEQ = 4096
BATCH = 2
D_FF = 4 * D_MODEL
M_HEADS = 8
M_DV = 512
M_DK = 256
N_KV = 4
HPG = 8
HEAD_DIM = 128
KV_DIM = 512
NCORES = 8


def _lay(g):
    return np.ascontiguousarray(np.asarray(g, np.float32).reshape(-1, 128).T)


def _consts_A():
    cc = np.tril(np.ones((64, 64), np.float32)).T
    cc = np.ascontiguousarray(np.concatenate([cc, cc], 0))
    seg = np.ones((2, 512), np.float32)
    seg[:, ::64] = 0
    neg = np.zeros((2, 512), np.float32)
    neg[:, ::64] = -1e30
    sel = np.zeros((2, 256), np.float32)
    sel[0, :128] = 1
    sel[1, 128:] = 1
    return {"c_causal": cc, "c_seg": np.concatenate([seg, neg], 1), "c_sel": sel, "c_i2": np.eye(2, dtype=np.float32)}


def _consts_C(S):
    NCMP = (S - 32) // 16 + 1
    NCC = (NCMP + 127) // 128
    NSB = S // 64
    NKC = S // 128
    c0 = np.arange(NCC * 128)[:, None] * 16
    s0 = np.arange(NSB)[None, :] * 64
    ovm = np.maximum(np.minimum(c0 + 32, s0 + 64) - np.maximum(c0, s0), 0) / 32.0
    ovm[NCMP:] = 0
    ov = np.ascontiguousarray(ovm.reshape(NCC, 128, NSB).transpose(1, 0, 2).reshape(128, NCC * NSB).astype(np.float32))
    ex = np.zeros((64, NKC * 128), np.float32)
    pp = np.arange(NKC * 128)
    jj = 2 * (pp // 128) + (pp % 128) // 64
    ok = jj < 64
    ex[jj[ok], pp[ok]] = 1
    selg = np.zeros((24, 24 * 128), np.float32)
    for r in range(24):
        selg[r, r * 128:(r + 1) * 128] = 1
    return {"c_ov": ov, "c_ex": ex, "c_selg": selg, "c_i64": np.eye(64, dtype=np.float32)}


_NC_CACHE = {}


def _get(name, fn):
    if name not in _NC_CACHE:
        _NC_CACHE[name] = fn()
    return _NC_CACHE[name]


def _run(nc, in_maps):
    res = run_bass_kernel_spmd(nc, in_maps, core_ids=list(range(len(in_maps))))
    return res.results


def kernel(x, attn_norm_g, mlp_norm_g, m_w_in, m_b_gate, m_head_g, m_w_out, kv_norm_g, w_kv,
           k_norm_g, cmp_pos, cmp_w1, cmp_w2, n_w_qg, q_norm_g, n_w_out, mlp_w_up, mlp_w_down):
    f32 = np.float32
    x = np.asarray(x, f32)
    B, S, D = x.shape
    T = S // 4
    xT = [np.ascontiguousarray(x[b].T) for b in range(B)]
    m_w_in = np.asarray(m_w_in, f32)[0]
    QK = M_HEADS * M_DK
    ncA = _get("A", lambda: build_A(S, D, M_DK, M_DV))
    cA = _consts_A()
    bgate = np.asarray(m_b_gate, f32)[0]
    hg_all = np.asarray(m_head_g, f32)[0]
    ga0 = _lay(np.asarray(attn_norm_g, f32)[0])
    mapsA = []
    for c in range(NCORES):
        b, hp = c // 4, c % 4
        h0 = 2 * hp
        cq = slice(h0 * M_DK, (h0 + 2) * M_DK)
        cv = slice(h0 * M_DV, (h0 + 2) * M_DV)
        gcols = [2 * QK + 2 * D + h0, 2 * QK + 2 * D + h0 + 1, 2 * QK + 2 * D + M_HEADS + h0, 2 * QK + 2 * D + M_HEADS + h0 + 1]
        m = {"xT": xT[b], "ga": ga0,
             "w_q": np.ascontiguousarray(m_w_in[:, 0:QK][:, cq]),
             "w_k": np.ascontiguousarray(m_w_in[:, QK:2 * QK][:, cq]),
             "w_v": np.ascontiguousarray(m_w_in[:, 2 * QK:2 * QK + D][:, cv]),
             "w_og": np.ascontiguousarray(m_w_in[:, 2 * QK + D:2 * QK + 2 * D][:, cv]),
             "w_g": np.ascontiguousarray(m_w_in[:, gcols]),
             "bg": np.ascontiguousarray(np.stack([bgate[[h0, h0 + 1]], bgate[[M_HEADS + h0, M_HEADS + h0 + 1]]], axis=1)),
             "hgain": _lay(hg_all[h0 * M_DV:(h0 + 2) * M_DV])}
        m.update(cA)
        mapsA.append(m)
    rA = _run(ncA, mapsA)
    aT = [np.concatenate([np.asarray(rA[b * 4 + hp]["hgT"]) for hp in range(4)], axis=0) for b in range(B)]
    del rA, mapsA

    ncB = _get("B", lambda: build_BD(T, D, D_FF, True))
    gainsB = np.ascontiguousarray(np.concatenate([_lay(np.asarray(mlp_norm_g, f32)[0]), _lay(np.asarray(attn_norm_g, f32)[1]), _lay(np.asarray(kv_norm_g, f32))], axis=1))
    kngT = np.ascontiguousarray(np.asarray(k_norm_g, f32).T)
    qngT = np.ascontiguousarray(np.asarray(q_norm_g, f32)[0].T)
    w_o0 = np.asarray(m_w_out, f32)[0]
    w_up0 = np.asarray(mlp_w_up, f32)[0]
    w_dn0 = np.asarray(mlp_w_down, f32)[0]
    w_qg = np.asarray(n_w_qg, f32)[0]
    w_kv_ = np.asarray(w_kv, f32)
    mapsB = []
    for c in range(NCORES):
        b, r = c // 4, c % 4
        ts = slice(r * T, (r + 1) * T)
        mapsB.append({"xT": np.ascontiguousarray(xT[b][:, ts]), "aT": np.ascontiguousarray(aT[b][:, ts]),
                      "w_o": w_o0, "w_up": w_up0, "w_dn": w_dn0, "gains": gainsB,
                      "w_qg": w_qg, "w_kv": w_kv_, "kng": kngT, "qng": qngT})
    rB = _run(ncB, mapsB)
    del mapsB

    def catT(name, b):
        return np.concatenate([np.asarray(rB[b * 4 + r][name]) for r in range(4)], axis=1)

    def catR(name, b):
        return np.concatenate([np.asarray(rB[b * 4 + r][name]) for r in range(4)], axis=0)

    x2T = [catT("xoT", b) for b in range(B)]
    ncC = _get("C", lambda: build_C(S))
    cC = _consts_C(S)
    posT = np.ascontiguousarray(np.asarray(cmp_pos, f32).transpose(2, 0, 1).reshape(HEAD_DIM, -1))
    w1 = np.asarray(cmp_w1, f32)
    w2 = np.asarray(cmp_w2, f32)
    mapsC = []
    for b in range(B):
        qT = catT("qT", b)
        gT = catT("gT", b)
        kcT, vcT, ksT, kwT = catT("kcT", b), catT("vcT", b), catT("ksT", b), catT("kwT", b)
        vs, vw = catR("vs", b), catR("vw", b)
        for g in range(N_KV):
            gr = slice(g * 128, (g + 1) * 128)
            grow = np.concatenate([np.arange(r_ * 32 + g * 8, r_ * 32 + g * 8 + 8) for r_ in range(3)])
            m = {"qT": np.ascontiguousarray(qT[g * 1024:(g + 1) * 1024]), "gT": np.ascontiguousarray(gT[grow]),
                 "kcT": np.ascontiguousarray(kcT[gr]), "vcT": np.ascontiguousarray(vcT[gr]),
                 "ksT": np.ascontiguousarray(ksT[gr]), "kwT": np.ascontiguousarray(kwT[gr]),
                 "vs": np.ascontiguousarray(vs[:, gr]), "vw": np.ascontiguousarray(vw[:, gr]),
                 "cmp_w1": w1, "cmp_w2": w2, "posT": posT, "kng": kngT, "qng": qngT}
            m.update(cC)
            mapsC.append(m)
    del rB
    rC = _run(ncC, mapsC)
    aT2 = [np.concatenate([np.asarray(rC[b * 4 + g]["oT"]) for g in range(4)], axis=0) for b in range(B)]
    del rC, mapsC

    ncD = _get("D", lambda: build_BD(T, D, D_FF, False))
    gainsD = np.ascontiguousarray(np.concatenate([_lay(np.asarray(mlp_norm_g, f32)[1])] * 3, axis=1))
    w_o1 = np.asarray(n_w_out, f32)[0]
    w_up1 = np.asarray(mlp_w_up, f32)[1]
    w_dn1 = np.asarray(mlp_w_down, f32)[1]
    mapsD = []
    for c in range(NCORES):
        b, r = c // 4, c % 4
        ts = slice(r * T, (r + 1) * T)
        mapsD.append({"xT": np.ascontiguousarray(x2T[b][:, ts]), "aT": np.ascontiguousarray(aT2[b][:, ts]),
                      "w_o": w_o1, "w_up": w_up1, "w_dn": w_dn1, "gains": gainsD})
    rD = _run(ncD, mapsD)
    out = np.empty((B, S, D), f32)
    for c in range(NCORES):
        b, r = c // 4, c % 4
        out[b, r * T:(r + 1) * T, :] = np.asarray(rD[c]["xoT"]).T
    return out
```

```python
import numpy as np
from contextlib import ExitStack
import ml_dtypes
import concourse.bass as bass
import concourse.mybir as mybir
from concourse.bass_utils import run_bass_kernel_spmd

F32 = mybir.dt.float32
BF16 = mybir.dt.bfloat16
ALU = mybir.AluOpType
AF = mybir.ActivationFunctionType
AX = mybir.AxisListType
NPBF = ml_dtypes.bfloat16

EPS = 1e-6
ENGS = ("pe", "act", "dve", "pool", "sp")


class Op:
    __slots__ = ("eng", "fn", "deps", "is_dma", "sem", "val", "signal", "ring", "slot")

    def __init__(self, eng, fn, deps, is_dma=False):
        self.eng = eng
        self.fn = fn
        self.deps = [d for d in deps if d is not None]
        self.is_dma = is_dma
        self.sem = None
        self.val = None
        self.signal = False
        self.ring = None
        self.slot = None


def X(name, *a, **kw):
    return (name, a, kw)


class Prog:
    def __init__(self, nc):
        self.nc = nc
        self.ops = {e: [] for e in ENGS}
        self.rings = {}

    def op(self, eng, fn, deps=()):
        o = Op(eng, fn, deps)
        self.ops[eng].append(o)
        return o

    def dma(self, q, out, in_, deps=(), ring="dflt", nslots=4, **kw):
        def fn(e, out=out, in_=in_, kw=kw):
            return e.dma_start(out=out, in_=in_, **kw)
        o = Op(q, fn, deps, is_dma=True)
        self.ops[q].append(o)
        r = self.rings.setdefault(ring, {"n": nslots, "count": 0, "uses": {}, "last": {}})
        s = r["count"] % r["n"]
        r["count"] += 1
        r["uses"][s] = r["uses"].get(s, 0) + 1
        if s in r["last"]:
            o.deps.append(r["last"][s])
        r["last"][s] = o
        o.ring = ring
        o.slot = s
        o.val = 16 * r["uses"][s]
        return o

    def wait(self, eng, deps):
        o = Op(eng, None, deps)
        self.ops[eng].append(o)
        return o

    def emit(self, ctx):
        nc = self.nc
        esem = {e: ctx.enter_context(nc.semaphore("s_" + e)) for e in ("pe", "act", "dve", "pool")}
        rsem = {}
        for name, r in self.rings.items():
            rsem[name] = [ctx.enter_context(nc.semaphore("r_%s_%d" % (name, i))) for i in range(len(r["uses"]))]
        for e in ENGS:
            for o in self.ops[e]:
                for d in o.deps:
                    if (not d.is_dma) and (d.eng != e or e != "pe"):
                        d.signal = True
        for e in ("pe", "act", "dve", "pool"):
            c = 0
            for o in self.ops[e]:
                if o.is_dma:
                    continue
                if o.signal:
                    c += 1
                    o.sem = esem[e]
                    o.val = c
        for e in ENGS:
            for o in self.ops[e]:
                if o.is_dma:
                    o.sem = rsem[o.ring][o.slot]
        blk = ctx.enter_context(nc.Block())

        def run(e, engobj):
            waited = {}
            for o in self.ops[e]:
                for d in o.deps:
                    if (not d.is_dma) and d.eng == e and e == "pe":
                        continue
                    key = id(d.sem)
                    if waited.get(key, 0) >= d.val:
                        continue
                    engobj.wait_ge(d.sem, d.val)
                    waited[key] = d.val
                if o.fn is None:
                    continue
                if isinstance(o.fn, tuple):
                    ins = getattr(engobj, o.fn[0])(*o.fn[1], **o.fn[2])
                else:
                    ins = o.fn(engobj)
                if o.is_dma:
                    ins.then_inc(o.sem, 16)
                elif o.signal:
                    ins.then_inc(o.sem, 1)

        @blk.tensor
        def _(eng):
            run("pe", eng)

        @blk.scalar
        def _(eng):
            run("act", eng)

        @blk.vector
        def _(eng):
            run("dve", eng)

        @blk.gpsimd
        def _(eng):
            run("pool", eng)

        @blk.sync
        def _(eng):
            run("sp", eng)


class Buf:
    def __init__(self, t):
        self.t = t
        self.ws = {}
        self.rs = {}
        self.dw = []
        self.dr = []

    @staticmethod
    def _add(d, lst, op):
        if op.is_dma:
            lst.append(op)
        else:
            d[op.eng] = op

    def wdeps(self):
        return list(self.ws.values()) + self.dw + list(self.rs.values()) + self.dr

    def rdeps(self):
        return list(self.ws.values()) + self.dw

    def wrote(self, op):
        self.ws = {}
        self.rs = {}
        self.dw = []
        self.dr = []
        self._add(self.ws, self.dw, op)

    def also_wrote(self, op):
        self._add(self.ws, self.dw, op)

    def read(self, op):
        self._add(self.rs, self.dr, op)


class Ring:
    def __init__(self, bufs):
        self.bufs = bufs
        self.i = 0

    def next(self):
        b = self.bufs[self.i % len(self.bufs)]
        self.i += 1
        return b


class WStream:
    def __init__(self, kb, nstg=2, elems=4096, name="stg"):
        self.kb = kb
        self.n = nstg
        self.elems = elems
        self.name = name
        self.ring = kb.sbring(name, nstg, [128, elems], F32)
        self.i = 0

    def load(self, dst_ap, src_ap, a, b, wd, dstbuf, first):
        p = self.kb.p
        stg = self.ring.next()
        view = stg.t[:, 0:a * b].rearrange("p (a b) -> p a b", a=a)
        d = p.dma("sp", view, src_ap, deps=stg.wdeps(), ring=self.name, nslots=self.n)
        stg.wrote(d)
        c = p.op("act", X("copy", out=dst_ap, in_=view), deps=[d] + list(wd))
        self.i += 1
        stg.read(c)
        if first:
            dstbuf.wrote(c)
        else:
            dstbuf.also_wrote(c)
        return c


class KB:
    def __init__(self, nc, ctx):
        self.nc = nc
        self.ctx = ctx
        self.p = Prog(nc)
        self.nps = 0

    def sb(self, name, shape, dt):
        return self.ctx.enter_context(self.nc.sbuf_tensor(name, list(shape), dt))

    def psum(self, name, shape, dt=F32):
        return self.ctx.enter_context(self.nc.psum_tensor(name, list(shape), dt))

    def sbring(self, name, n, shape, dt):
        return Ring([Buf(self.sb("%s%d" % (name, i), shape, dt)) for i in range(n)])

    def psring(self, name, n, shape=(128, 512), dt=F32):
        return Ring([Buf(self.psum("%s%d" % (name, i), shape, dt)) for i in range(n)])

    def din(self, name, shape, dt):
        return self.nc.dram_tensor(name, list(shape), dt, kind="ExternalInput").ap()

    def dout(self, name, shape, dt):
        return self.nc.dram_tensor(name, list(shape), dt, kind="ExternalOutput").ap()


def mm_group(kb, bank, pairs, extra_deps=(), out_ap=None):
    p = kb.p
    n = len(pairs)
    last = None
    out = bank.t[:] if out_ap is None else out_ap
    for i, (l, r, deps) in enumerate(pairs):
        d = list(deps)
        if i == 0:
            d += bank.wdeps() + list(extra_deps)
        last = p.op("pe", X("matmul", out, lhsT=l, rhs=r, start=(i == 0), stop=(i == n - 1)), deps=d)
    bank.wrote(last)
    return last


def rms_stats(kb, XT, KD, D, ones, sqring, ssbank, R, xt_ready):
    p = kb.p
    last = None
    for k in range(KD):
        sq = sqring.next()
        o = p.op("act", X("activation", out=sq.t[:], in_=XT.t[:, k, :], func=AF.Square), deps=sq.wdeps() + xt_ready)
        sq.wrote(o)
        d = [o]
        if k == 0:
            d += ssbank.wdeps()
        last = p.op("pe", X("matmul", ssbank.t[:], lhsT=ones[:], rhs=sq.t[:], start=(k == 0), stop=(k == KD - 1)), deps=d)
        sq.read(last)
    ssbank.wrote(last)
    o2 = rstd_from(kb, ssbank, R, 1.0 / D, [last])
    R.wrote(o2)
    return o2


def rstd_from(kb, bank, dst, inv_n, deps, eps_ap=None):
    p = kb.p
    o1 = p.op("act", X("activation", out=dst.t[:], in_=bank.t[:], func=AF.Sqrt, scale=inv_n, bias=kb.eps[:, 0:1]), deps=list(deps) + dst.wdeps() + [kb.eps_op])
    bank.read(o1)
    o2 = p.op("dve", X("reciprocal", out=dst.t[:], in_=dst.t[:]), deps=[o1])
    return o2


def build_BD(T, D, DFF, tail, HD=128, NKV=4, NQH=32, NG=96):
    nc = bass.Bass("TRN2", target_bir_lowering=False)
    KD = D // 128
    KF = DFF // 128
    NP = T // 512
    JR = 8
    NR = KF // JR
    with ExitStack() as ctx:
        kb = KB(nc, ctx)
        p = kb.p
        xT_d = kb.din("xT", [D, T], F32)
        aT_d = kb.din("aT", [D, T], BF16)
        w_o = kb.din("w_o", [D, D], F32)
        w_up = kb.din("w_up", [D, DFF], F32)
        w_dn = kb.din("w_dn", [DFF, D], F32)
        g_d = kb.din("gains", [128, 3 * KD], F32)
        xo_d = kb.dout("xoT", [D, T], F32)
        if tail:
            KVD = NKV * HD
            w_qg = kb.din("w_qg", [D, D + NG], F32)
            w_kv = kb.din("w_kv", [D, 6 * KVD], F32)
            kng_d = kb.din("kng", [128, 3], F32)
            qng_d = kb.din("qng", [128, 3], F32)
            qT_o = kb.dout("qT", [D, T], BF16)
            gT_o = kb.dout("gT", [NG, T], F32)
            kcT_o = kb.dout("kcT", [KVD, T], BF16)
            vcT_o = kb.dout("vcT", [KVD, T], BF16)
            ksT_o = kb.dout("ksT", [KVD, T], BF16)
            kwT_o = kb.dout("kwT", [KVD, T], BF16)
            vs_o = kb.dout("vs", [T, KVD], BF16)
            vw_o = kb.dout("vw", [T, KVD], BF16)

        XT = Buf(kb.sb("XT", [128, KD, 512], F32))
        XN = Buf(kb.sb("XN", [128, KD, 512], BF16))
        HT = kb.sbring("HT", 2, [128, JR, 512], BF16)
        wA = kb.sbring("wA", 2, [128, KD, 256], BF16)
        wB = kb.sbring("wB", 2, [128, JR, 512], BF16)
        ws = WStream(kb, 2, max(4096, JR * 512, (KD // 2) * 256))
        sqr = kb.sbring("SQ", 3, [128, 512], BF16)
        sqf = kb.sbring("SQF", 2, [128, 512], F32)
        R = Buf(kb.sb("R", [128, 512], F32))
        gains = kb.sb("gains_sb", [128, 3 * KD], F32)
        ones = kb.sb("ones", [128, 128], BF16)
        psA = kb.psring("psA", 3)
        psD = kb.psring("psD", 4)
        psS = Buf(kb.psum("psS", [128, 512]))
        if tail:
            kgain = kb.sb("kgain_sb", [128, 2], F32)
            w0, w1 = wB.bufs[0].t, wB.bufs[1].t

            def f32view(t, j0):
                return t[:, j0:j0 + 2, :].bitcast(F32).rearrange("p a b -> p (a b)")
            Rq = Ring([Buf(f32view(w0, 0)), Buf(f32view(w0, 2))])
            GB = Buf(f32view(w0, 4))
            OB = Ring([Buf(w1[:, j, :]) for j in range(3)])
            tail_alias = [(wB.bufs[0], Rq.bufs + [GB]), (wB.bufs[1], OB.bufs)]

        c_ones = p.op("dve", X("memset", ones[:], 1.0))
        kb.eps = kb.sb("eps_sb", [128, 1], F32)
        kb.eps_op = p.op("dve", X("memset", kb.eps[:], EPS))
        d_g = p.dma("sp", gains[:], g_d[:, :], ring="misc", nslots=4)
        if tail:
            kng = kb.sb("kng_sb", [128, 3], F32)
            qng = kb.sb("qng_sb", [128, 3], F32)
            d_k1 = p.dma("sp", kng[:], kng_d[:, :], ring="misc", nslots=4)
            d_k2 = p.dma("sp", qng[:], qng_d[:, :], ring="misc", nslots=4)
            o_ = p.op("dve", X("tensor_tensor", out=kgain[:], in0=kng[:, 1:3], in1=qng[:, 1:3], op=ALU.mult), deps=[d_k1, d_k2])
            d_kg = p.op("dve", X("tensor_scalar", out=kgain[:], in0=kgain[:], scalar1=HD ** -0.5, scalar2=None, op0=ALU.mult), deps=[o_])

        w_o_v = w_o.rearrange("(k p) n -> p k n", p=128)
        w_up_v = w_up.rearrange("(k p) n -> p k n", p=128)
        w_dn_v = w_dn.rearrange("(j p) n -> p j n", p=128)
        xT_v = xT_d.rearrange("(k p) t -> p k t", p=128)
        aT_v = aT_d.rearrange("(k p) t -> p k t", p=128)
        xo_v = xo_d.rearrange("(k p) t -> p k t", p=128)

        def load_wA(src_v, c0, ncols=256):
            s = wA.next()
            wd = s.wdeps()
            kh = KD // 2
            ws.load(s.t[:, 0:kh, 0:ncols], src_v[:, 0:kh, c0:c0 + ncols], kh, ncols, wd, s, True)
            ws.load(s.t[:, kh:KD, 0:ncols], src_v[:, kh:KD, c0:c0 + ncols], kh, ncols, wd, s, False)
            return s

        out_dmas = []
        for ps_ in range(NP):
            tsl = slice(ps_ * 512, (ps_ + 1) * 512)
            ld = []
            for q in range(4):
                ks = slice(q * KD // 4, (q + 1) * KD // 4)
                ld.append(p.dma("sp", XT.t[:, ks, :], xT_v[:, ks, tsl], deps=XT.wdeps(), ring="xin", nslots=4))
            XT.wrote(ld[0])
            for o_ in ld[1:]:
                XT.also_wrote(o_)
            xt_ready = list(ld)
            la = []
            for q in range(2):
                ks = slice(q * KD // 2, (q + 1) * KD // 2)
                la.append(p.dma("sp", XN.t[:, ks, :], aT_v[:, ks, tsl], deps=XN.wdeps(), ring="ain", nslots=2))
            XN.wrote(la[0])
            XN.also_wrote(la[1])
            xn_ready = list(la)
            adds = []
            for mg in range(KD // 2):
                s = load_wA(w_o_v, mg * 256)
                for mi in range(2):
                    m = 2 * mg + mi
                    bank = psA.next()
                    mm = mm_group(kb, bank, [(s.t[:, k, mi * 128:(mi + 1) * 128], XN.t[:, k, :], (s.rdeps() + xn_ready + [c_ones]) if k == 0 else []) for k in range(KD)])
                    s.read(mm)
                    XN.read(mm)
                    o = p.op("dve", X("tensor_tensor", out=XT.t[:, m, :], in0=XT.t[:, m, :], in1=bank.t[:], op=ALU.add), deps=[mm] + xt_ready)
                    bank.read(o)
                    adds.append(o)
            XT.wrote(adds[-1])

            def norm_into_XN(goff, xt_dep):
                r_op = rms_stats(kb, XT, KD, D, ones, sqr, psS, R, xt_dep)
                last = None
                for k in range(KD):
                    eng = "dve"
                    last_ = p.op(eng, X("scalar_tensor_tensor", out=XN.t[:, k, :], in0=XT.t[:, k, :], scalar=gains[:, goff + k:goff + k + 1], in1=R.t[:], op0=ALU.mult, op1=ALU.mult),
                                 deps=[r_op, d_g] + XN.wdeps() + xt_dep)
                    last = last_
                R.read(last)
                return [last, last]

            xn_ops = norm_into_XN(0, [adds[-1]])
            XN.wrote(xn_ops[0])
            xn_ready = list(xn_ops)

            hts = [None] * NR
            dn_adds = [adds[-1]]

            def up_round(r):
                ht = HT.next()
                evs = []
                for jg in range(JR // 2):
                    s = load_wA(w_up_v, (r * JR + 2 * jg) * 128)
                    for ji in range(2):
                        jl = 2 * jg + ji
                        bank = psA.next()
                        mm = mm_group(kb, bank, [(s.t[:, k, ji * 128:(ji + 1) * 128], XN.t[:, k, :], (s.rdeps() + xn_ready) if k == 0 else []) for k in range(KD)])
                        s.read(mm)
                        XN.read(mm)
                        sf = sqf.next()
                        o0 = p.op("act", X("activation", out=sf.t[:], in_=bank.t[:], func=AF.Square), deps=[mm] + sf.wdeps())
                        sf.wrote(o0)
                        o = p.op("dve", X("scalar_tensor_tensor", out=ht.t[:, jl, :], in0=bank.t[:], scalar=0.0, in1=sf.t[:], op0=ALU.is_gt, op1=ALU.mult),
                                 deps=[mm, o0] + ht.wdeps())
                        sf.read(o)
                        bank.read(o0)
                        bank.read(o)
                        evs.append(o)
                ht.wrote(evs[-1])
                hts[r] = ht

            def down_round(r):
                ht = hts[r]
                for mg in range(KD // 4):
                    s = wB.next()
                    ws.load(s.t[:], w_dn_v[:, r * JR:(r + 1) * JR, mg * 512:(mg + 1) * 512], JR, 512, s.wdeps(), s, True)
                    banks = [psD.next() for _ in range(4)]
                    last = None
                    for jl in range(JR):
                        for mi in range(4):
                            d = []
                            if jl == 0:
                                d = banks[mi].wdeps()
                                if mi == 0:
                                    d = d + s.rdeps() + ht.rdeps()
                            last = p.op("pe", X("matmul", banks[mi].t[:], lhsT=s.t[:, jl, mi * 128:(mi + 1) * 128], rhs=ht.t[:, jl, :], start=(jl == 0), stop=(jl == JR - 1)), deps=d)
                    s.read(last)
                    ht.read(last)
                    for mi in range(4):
                        banks[mi].wrote(last)
                        m = 4 * mg + mi
                        eng = "dve" if mi % 2 == 0 else "pool"
                        eng = "dve"
                        o = p.op(eng, X("tensor_tensor", out=XT.t[:, m, :], in0=XT.t[:, m, :], in1=banks[mi].t[:], op=ALU.add), deps=[last] + dn_adds[-1:])
                        banks[mi].read(o)
                        dn_adds.append(o)

            for r in range(NR + 1):
                if r < NR:
                    up_round(r)
                if r >= 1:
                    down_round(r - 1)
            XT.wrote(dn_adds[-1])
            xt_ready = [dn_adds[-1]]
            XN.read(dn_adds[-1])
            for q in range(4):
                ks = slice(q * KD // 4, (q + 1) * KD // 4)
                o = p.dma("sp", xo_v[:, ks, tsl], XT.t[:, ks, :], deps=xt_ready, ring="xout", nslots=4)
                XT.read(o)
                out_dmas.append(o)

            if tail:
                for wslot, als in tail_alias:
                    for al in als:
                        for op_ in wslot.wdeps():
                            al.read(op_)
                KVD = NKV * HD
                w_qg_v = w_qg.rearrange("(k p) n -> p k n", p=128)
                w_kv_v = w_kv.rearrange("(k p) n -> p k n", p=128)
                xn_ops = norm_into_XN(KD, xt_ready)
                XN.wrote(xn_ops[0])
                xn_ready = list(xn_ops)
                pend = None

                def finish_q(item):
                    bank, sq, hd, o_sq = item
                    mm2 = mm_group(kb, psS, [(ones[:], sq.t[:], [o_sq])])
                    sq.read(mm2)
                    rq = Rq.next()
                    o2 = rstd_from(kb, psS, rq, 1.0 / HD, [mm2])
                    ob = OB.next()
                    o3 = p.op("dve", X("tensor_tensor", out=ob.t[:], in0=bank.t[:], in1=rq.t[:], op=ALU.mult), deps=[o2] + ob.wdeps())
                    rq.wrote(o3)
                    bank.read(o3)
                    ob.wrote(o3)
                    o4 = p.dma("sp", qT_o[hd * 128:(hd + 1) * 128, tsl], ob.t[:], deps=[o3], ring="oq", nslots=3)
                    ob.read(o4)
                    out_dmas.append(o4)

                for hg in range(NQH // 2):
                    s = load_wA(w_qg_v, hg * 256)
                    for hi in range(2):
                        hd = 2 * hg + hi
                        bank = psA.next()
                        mm = mm_group(kb, bank, [(s.t[:, k, hi * 128:(hi + 1) * 128], XN.t[:, k, :], (s.rdeps() + xn_ready) if k == 0 else []) for k in range(KD)])
                        s.read(mm)
                        XN.read(mm)
                        sq = sqr.next()
                        o_sq = p.op("act", X("activation", out=sq.t[:], in_=bank.t[:], func=AF.Square), deps=[mm] + sq.wdeps())
                        sq.wrote(o_sq)
                        bank.read(o_sq)
                        if pend is not None:
                            finish_q(pend)
                        pend = (bank, sq, hd, o_sq)
                s = load_wA(w_qg_v, D, NG)
                bank = psA.next()
                mm = p_last = None
                n = KD
                d0 = bank.wdeps() + s.rdeps() + xn_ready
                for k in range(KD):
                    mm = p.op("pe", X("matmul", bank.t[0:NG, :], lhsT=s.t[:, k, 0:NG], rhs=XN.t[:, k, :], start=(k == 0), stop=(k == n - 1)), deps=d0 if k == 0 else [])
                bank.wrote(mm)
                s.read(mm)
                XN.read(mm)
                finish_q(pend)
                o = p.op("act", X("activation", out=GB.t[0:NG, :], in_=bank.t[0:NG, :], func=AF.Sigmoid), deps=[mm] + GB.wdeps())
                bank.read(o)
                GB.wrote(o)
                o2 = p.dma("sp", gT_o[:, tsl], GB.t[0:NG, :], deps=[o], ring="og", nslots=1)
                GB.read(o2)
                out_dmas.append(o2)
                last = None
                for k in range(KD):
                    eng = "dve"
                    o = p.op(eng, X("scalar_tensor_tensor", out=XN.t[:, k, :], in0=XT.t[:, k, :], scalar=gains[:, 2 * KD + k:2 * KD + k + 1], in1=R.t[:], op0=ALU.mult, op1=ALU.mult),
                             deps=XN.wdeps() + xt_ready + R.rdeps())
                    last = o
                XN.wrote(last)
                R.read(last)
                xn_ready = [last]
                pend = None

                def finish_k(item):
                    bank, sq, o_sq, dst, gi, row0 = item
                    mm2 = mm_group(kb, psS, [(ones[:], sq.t[:], [o_sq])])
                    sq.read(mm2)
                    rq = Rq.next()
                    o2 = rstd_from(kb, psS, rq, 1.0 / HD, [mm2])
                    ob = OB.next()
                    o3 = p.op("dve", X("scalar_tensor_tensor", out=ob.t[:], in0=bank.t[:], scalar=kgain[:, gi:gi + 1], in1=rq.t[:], op0=ALU.mult, op1=ALU.mult), deps=[o2, d_kg] + ob.wdeps())
                    rq.wrote(o3)
                    bank.read(o3)
                    ob.wrote(o3)
                    o4 = p.dma("sp", dst[row0:row0 + 128, tsl], ob.t[:], deps=[o3], ring="oq", nslots=3)
                    ob.read(o4)
                    out_dmas.append(o4)

                for (j, dst, gi) in ((0, kcT_o, None), (1, vcT_o, None), (2, ksT_o, 0), (4, kwT_o, 1)):
                    for gg in range(NKV // 2):
                        s = load_wA(w_kv_v, j * KVD + gg * 256)
                        for gi2 in range(2):
                            g = 2 * gg + gi2
                            bank = psA.next()
                            mm = mm_group(kb, bank, [(s.t[:, k, gi2 * 128:(gi2 + 1) * 128], XN.t[:, k, :], (s.rdeps() + xn_ready) if k == 0 else []) for k in range(KD)])
                            s.read(mm)
                            XN.read(mm)
                            if gi is None:
                                ob = OB.next()
                                o3 = p.op("act", X("copy", out=ob.t[:], in_=bank.t[:]), deps=[mm] + ob.wdeps())
                                bank.read(o3)
                                ob.wrote(o3)
                                o4 = p.dma("sp", dst[g * 128:(g + 1) * 128, tsl], ob.t[:], deps=[o3], ring="oq", nslots=3)
                                ob.read(o4)
                                out_dmas.append(o4)
                            else:
                                sq = sqr.next()
                                o_sq = p.op("act", X("activation", out=sq.t[:], in_=bank.t[:], func=AF.Square), deps=[mm] + sq.wdeps())
                                sq.wrote(o_sq)
                                bank.read(o_sq)
                                if pend is not None:
                                    finish_k(pend)
                                pend = (bank, sq, o_sq, dst, gi, g * 128)
                if pend is not None:
                    finish_k(pend)
                for (j, dst) in ((3, vs_o), (5, vw_o)):
                    for half in range(KVD // 256):
                        s = load_wA(w_kv_v, j * KVD + half * 256)
                        for tt in range(4):
                            bank = psA.next()
                            mm = mm_group(kb, bank, [(XN.t[:, k, tt * 128:(tt + 1) * 128], s.t[:, k, :], (s.rdeps() + xn_ready) if k == 0 else []) for k in range(KD)], out_ap=bank.t[:, 0:256])
                            s.read(mm)
                            XN.read(mm)
                            ob = OB.next()
                            o3 = p.op("act", X("copy", out=ob.t[:, 0:256], in_=bank.t[:, 0:256]), deps=[mm] + ob.wdeps())
                            bank.read(o3)
                            ob.wrote(o3)
                            o4 = p.dma("sp", dst[ps_ * 512 + tt * 128:ps_ * 512 + (tt + 1) * 128, half * 256:(half + 1) * 256], ob.t[:, 0:256], deps=[o3], ring="oq", nslots=3)
                            ob.read(o4)
                            out_dmas.append(o4)
            if tail:
                for wslot, als in tail_alias:
                    for al in als:
                        for op_ in al.wdeps():
                            wslot.read(op_)
        p.wait("sp", out_dmas)
        p.emit(ctx)
    return nc


def build_A(S, D, DK=256, DV=512, GATE_CAP=15.0):
    nc = bass.Bass("TRN2", target_bir_lowering=False)
    KD = D // 128
    NB = S // 512
    NH = 2
    DKC = DK // 128
    DVC = DV // 128
    QC = NH * DKC
    OC = NH * DVC
    with ExitStack() as ctx:
        kb = KB(nc, ctx)
        p = kb.p
        xT_d = kb.din("xT", [D, S], F32)
        ga_d = kb.din("ga", [128, KD], F32)
        w_q = kb.din("w_q", [D, NH * DK], F32)
        w_k = kb.din("w_k", [D, NH * DK], F32)
        w_v = kb.din("w_v", [D, NH * DV], F32)
        w_og = kb.din("w_og", [D, NH * DV], F32)
        w_g = kb.din("w_g", [D, 4], F32)
        bg_d = kb.din("bg", [2, 2], F32)
        hg_d = kb.din("hgain", [128, OC], F32)
        cc_d = kb.din("c_causal", [128, 64], F32)
        cs_d = kb.din("c_seg", [2, 1024], F32)
        csel_d = kb.din("c_sel", [2, 256], F32)
        ci2_d = kb.din("c_i2", [2, 2], F32)
        out_d = kb.dout("hgT", [NH * DV, S], BF16)

        XTr = kb.sbring("XTr", 2, [128, 4, 512], F32)
        XN = Buf(kb.sb("XN", [128, KD, 512], BF16))
        wA = kb.sbring("wA", 2, [128, KD, 256], BF16)
        ws = WStream(kb, 2, max(4096, (KD // 2) * 256))
        sqr = kb.sbring("SQ", 3, [128, 512], BF16)
        R = Buf(kb.sb("R", [128, 512], F32))
        ga = kb.sb("ga_sb", [128, KD], F32)
        hgain = kb.sb("hgain_sb", [128, OC], F32)
        ones = kb.sb("ones", [128, 128], BF16)
        causal = kb.sb("causal", [128, 64], F32)
        cseg = kb.sb("cseg", [2, 512], F32)
        csel = kb.sb("csel", [2, 256], F32)
        ci2 = kb.sb("ci2", [2, 2], F32)
        bg = kb.sb("bg_sb", [2, 2], F32)
        bg15 = kb.sb("bg15", [2, 2], F32)
        one1 = kb.sb("one1", [2, 1], F32)
        QT = Buf(kb.sb("QT", [128, QC, 512], BF16))
        KT = Buf(kb.sb("KT", [128, QC, 512], BF16))
        KG = Buf(kb.sb("KG", [128, 4, NH, DK], BF16))
        VT = Buf(kb.sb("VT", [128, 4, NH, DV], BF16))
        OG = Buf(kb.sb("OG", [128, OC, 512], BF16))
        HG = Buf(kb.sb("HG", [128, OC, 512], BF16))
        OT = kb.sbring("OT", 2, [128, 512], F32)
        C32 = [Buf(kb.sb("C32_%d" % i, [128, 512], F32)) for i in range(QC)]
        Cd = [Buf(kb.sb("Cd_%d" % i, [128, 512], F32)) for i in range(QC)]
        Cb = [Buf(kb.sb("Cb_%d" % i, [128, 512], BF16)) for i in range(QC)]
        n32 = [Buf(kb.sb("n32_%d" % i, [128, 2], F32)) for i in range(QC)]
        nrep = [Buf(kb.sb("nrep_%d" % i, [128, 128], BF16)) for i in range(QC)]
        EMWb = Buf(kb.sb("EMWb", [128, NH, 512], F32))
        G128 = Buf(kb.sb("G128", [128, 4, NH], F32))
        SC = Buf(kb.sb("SCk", [128, 4, NH], F32))
        DECb = Buf(kb.sb("DECb", [128, 8, NH], F32))
        rcol = Buf(kb.sb("rcol", [128, 8], F32))
        SW = kb.sbring("SW", 3, [128, 64], BF16)
        DN = kb.sbring("DN", 3, [128, 64], F32)
        HTt = kb.sbring("HTt", 3, [128, DVC, 64], F32)
        HSQ = kb.sbring("HSQ", 3, [128, DVC, 64], BF16)
        RS = kb.sbring("RS", 3, [128, 64], F32)
        GV = [Buf(kb.sb("GV%d" % i, [2, 512], F32)) for i in range(6)]
        gs = Buf(kb.sb("gsmall", [2, 64], F32))
        mprev = Buf(kb.sb("mprev", [2, 1], F32))

        psA = kb.psring("psA", 3)
        psS = Buf(kb.psum("psS", [128, 512]))
        psc = kb.psring("psc", 2)
        psC = kb.psring("psC", 2)

        c_ones = p.op("dve", X("memset", ones[:], 1.0))
        kb.eps = kb.sb("eps_sb", [128, 1], F32)
        kb.eps_op = p.op("dve", X("memset", kb.eps[:], EPS))
        c_one1 = p.op("dve", X("memset", one1[:], 1.0))
        c_mp = p.op("dve", X("memset", mprev.t[:], 0.0))
        mprev.wrote(c_mp)
        cinit = []
        for i in range(QC):
            o = p.op("pool", X("memset", C32[i].t[:], 0.0))
            C32[i].wrote(o)
            o = p.op("pool", X("memset", Cb[i].t[:], 0.0))
            Cb[i].wrote(o)
            o = p.op("pool", X("memset", n32[i].t[:], 0.0))
            n32[i].wrote(o)
            o = p.op("pool", X("memset", nrep[i].t[:], 0.0))
            nrep[i].wrote(o)
        dc = [p.dma("sp", ga[:], ga_d[:, :], ring="misc", nslots=8),
              p.dma("sp", hgain[:], hg_d[:, :], ring="misc", nslots=8),
              p.dma("sp", causal[:], cc_d[:, :], ring="misc", nslots=8),
              p.dma("sp", cseg[:], cs_d[:, 0:512], ring="misc", nslots=8),
              p.dma("sp", csel[:], csel_d[:, :], ring="misc", nslots=8),
              p.dma("sp", ci2[:], ci2_d[:, :], ring="misc", nslots=8),
              p.dma("sp", bg[:], bg_d[:, :], ring="misc", nslots=8)]
        c_bg15 = p.op("dve", X("tensor_scalar", out=bg15[:], in0=bg[:], scalar1=1.0 / GATE_CAP, scalar2=None, op0=ALU.mult), deps=[dc[6]])

        xT_v = xT_d.rearrange("(k p) t -> p k t", p=128)
        wv = {n: a.rearrange("(k p) n -> p k n", p=128) for n, a in (("q", w_q), ("k", w_k), ("v", w_v), ("o", w_og), ("g", w_g))}
        out_v = out_d.rearrange("(c p) t -> p c t", p=128)

        def load_wA(name, c0, ncols=256):
            s = wA.next()
            wd = s.wdeps()
            kh = KD // 2
            ws.load(s.t[:, 0:kh, 0:ncols], wv[name][:, 0:kh, c0:c0 + ncols], kh, ncols, wd, s, True)
            ws.load(s.t[:, kh:KD, 0:ncols], wv[name][:, kh:KD, c0:c0 + ncols], kh, ncols, wd, s, False)
            return s

        def proj_fm(s, c0, ncols=128, M=None):
            bank = psA.next()
            d0 = bank.wdeps() + s.rdeps() + XN.rdeps() + [c_ones]
            mm = None
            for k in range(KD):
                mm = p.op("pe", X("matmul", bank.t[0:ncols, :], lhsT=s.t[:, k, c0:c0 + ncols], rhs=XN.t[:, k, :], start=(k == 0), stop=(k == KD - 1)), deps=d0 if k == 0 else [])
            bank.wrote(mm)
            s.read(mm)
            XN.read(mm)
            return bank, mm

        def proj_tm(s, tt):
            bank = psA.next()
            d0 = bank.wdeps() + s.rdeps() + XN.rdeps()
            mm = None
            for k in range(KD):
                mm = p.op("pe", X("matmul", bank.t[:, 0:256], lhsT=XN.t[:, k, tt * 128:(tt + 1) * 128], rhs=s.t[:, k, :], start=(k == 0), stop=(k == KD - 1)), deps=d0 if k == 0 else [])
            bank.wrote(mm)
            s.read(mm)
            XN.read(mm)
            return bank, mm

        out_dmas = []
        pend = [None]

        for bi in range(NB):
            tsl = slice(bi * 512, (bi + 1) * 512)
            sq_ops = []
            first = True
            mm = None
            for grp in range(KD // 4):
                xs = XTr.next()
                ld = p.dma("sp", xs.t[:], xT_v[:, 4 * grp:4 * grp + 4, tsl], deps=xs.wdeps(), ring="xin", nslots=2)
                xs.wrote(ld)
                for kk in range(4):
                    k = 4 * grp + kk
                    sq = sqr.next()
                    o = p.op("act", X("activation", out=sq.t[:], in_=xs.t[:, kk, :], func=AF.Square), deps=sq.wdeps() + [ld])
                    sq.wrote(o)
                    xs.read(o)
                    d = [o, c_ones]
                    if k == 0:
                        d += psS.wdeps()
                    mm = p.op("pe", X("matmul", psS.t[:], lhsT=ones[:], rhs=sq.t[:], start=(k == 0), stop=(k == KD - 1)), deps=d)
                    sq.read(mm)
                    o2 = p.op("dve", X("tensor_scalar", out=XN.t[:, k, :], in0=xs.t[:, kk, :], scalar1=ga[:, k:k + 1], scalar2=None, op0=ALU.mult), deps=[ld, dc[0]] + XN.wdeps())
                    xs.read(o2)
                    if first:
                        XN.wrote(o2)
                        first = False
                    else:
                        XN.also_wrote(o2)
            psS.wrote(mm)
            r_op = rstd_from(kb, psS, R, 1.0 / D, [mm])
            R.wrote(r_op)
            bank = psA.next()
            d0 = bank.wdeps() + [r_op, c_one1]
            for tt in range(4):
                mm = p.op("pe", X("matmul", bank.t[:, tt:tt + 1], lhsT=R.t[0:1, tt * 128:(tt + 1) * 128], rhs=one1[0:1, 0:1], start=True, stop=True), deps=d0 if tt == 0 else [])
            bank.wrote(mm)
            R.read(mm)
            o = p.op("dve", X("tensor_copy", out=rcol.t[:, 0:4], in_=bank.t[:, 0:4]), deps=[mm] + rcol.wdeps())
            bank.read(o)
            o = p.op("dve", X("tensor_scalar", out=rcol.t[:, 4:8], in0=rcol.t[:, 0:4], scalar1=DK ** -0.5, scalar2=None, op0=ALU.mult), deps=[o])
            rcol.wrote(o)

            s = load_wA("g", 0, 4)
            bank_i, mm_i = proj_fm(s, 0, 2)
            bank_f, mm_f = proj_fm(s, 2, 2)
            V = GV
            o = p.op("dve", X("tensor_tensor", out=V[0].t[:], in0=bank_i.t[0:2, :], in1=R.t[0:2, :], op=ALU.mult), deps=[mm_i, r_op] + V[0].wdeps())
            bank_i.read(o)
            o = p.op("act", X("activation", out=V[0].t[:], in_=V[0].t[:], func=AF.Tanh, scale=1.0 / GATE_CAP, bias=bg15[:, 0:1]), deps=[o, c_bg15])
            o_li = p.op("dve", X("tensor_scalar", out=V[0].t[:], in0=V[0].t[:], scalar1=GATE_CAP, scalar2=None, op0=ALU.mult), deps=[o])
            V[0].wrote(o_li)
            o = p.op("dve", X("tensor_tensor", out=V[1].t[:], in0=bank_f.t[0:2, :], in1=R.t[0:2, :], op=ALU.mult), deps=[mm_f, r_op] + V[1].wdeps())
            bank_f.read(o)
            R.read(o)
            o = p.op("act", X("activation", out=V[1].t[:], in_=V[1].t[:], func=AF.Tanh, scale=1.0 / GATE_CAP, bias=bg15[:, 1:2]), deps=[o, c_bg15])
            o = p.op("act", X("activation", out=V[1].t[:], in_=V[1].t[:], func=AF.Exp, scale=-GATE_CAP), deps=[o])
            o = p.op("act", X("activation", out=V[1].t[:], in_=V[1].t[:], func=AF.Ln, bias=one1[:, 0:1]), deps=[o, c_one1])
            o_lf = p.op("dve", X("tensor_scalar", out=V[1].t[:], in0=V[1].t[:], scalar1=-1.0, scalar2=None, op0=ALU.mult), deps=[o])
            V[1].wrote(o_lf)
            o_b = p.op("dve", X("tensor_tensor_scan", out=V[2].t[:], data0=cseg[:, 0:512], data1=V[1].t[:], initial=0.0, op0=ALU.mult, op1=ALU.add), deps=[o_lf, dc[3]] + V[2].wdeps())
            V[2].wrote(o_b)
            o_a = p.op("dve", X("tensor_tensor", out=V[3].t[:], in0=V[0].t[:], in1=V[2].t[:], op=ALU.subtract), deps=[o_li, o_b] + V[3].wdeps())
            V[3].wrote(o_a)
            b3 = V[2].t[:].rearrange("p (c s) -> p c s", s=64)
            a3 = V[3].t[:].rearrange("p (c s) -> p c s", s=64)
            o = p.op("dve", X("tensor_copy", out=gs.t[:, 0:8], in_=b3[:, :, 63]), deps=[o_b] + gs.wdeps())
            o = p.op("dve", X("tensor_reduce", out=gs.t[:, 8:16], in_=a3, axis=AX.X, op=ALU.max), deps=[o_a])
            o = p.op("dve", X("tensor_tensor", out=gs.t[:, 8:16], in0=gs.t[:, 8:16], in1=gs.t[:, 0:8], op=ALU.add), deps=[o])
            o = p.op("dve", X("tensor_tensor_scan", out=gs.t[:, 16:24], data0=gs.t[:, 0:8], data1=gs.t[:, 8:16], initial=mprev.t[:, 0:1], op0=ALU.add, op1=ALU.max), deps=[o] + mprev.rdeps())
            o = p.op("dve", X("tensor_copy", out=gs.t[:, 24:25], in_=mprev.t[:, 0:1]), deps=[o])
            o = p.op("dve", X("tensor_copy", out=gs.t[:, 25:32], in_=gs.t[:, 16:23]), deps=[o])
            o_mp = p.op("dve", X("tensor_copy", out=mprev.t[:, 0:1], in_=gs.t[:, 23:24]), deps=[o])
            mprev.wrote(o_mp)
            o = p.op("dve", X("tensor_tensor", out=gs.t[:, 32:40], in0=gs.t[:, 0:8], in1=gs.t[:, 24:32], op=ALU.add), deps=[o_mp])
            o = p.op("dve", X("tensor_tensor", out=gs.t[:, 32:40], in0=gs.t[:, 32:40], in1=gs.t[:, 16:24], op=ALU.subtract), deps=[o])
            o_dec = p.op("act", X("activation", out=gs.t[:, 32:40], in_=gs.t[:, 32:40], func=AF.Exp), deps=[o])
            gs.wrote(o_dec)
            mcur_b = gs.t[:, 24:32].unsqueeze(2).to_broadcast([2, 8, 64])
            g3 = V[4].t[:].rearrange("p (c s) -> p c s", s=64)
            o = p.op("dve", X("tensor_tensor", out=g3, in0=a3, in1=mcur_b, op=ALU.subtract), deps=[o_a, o_mp] + V[4].wdeps())
            o_g = p.op("act", X("activation", out=V[4].t[:], in_=V[4].t[:], func=AF.Exp), deps=[o])
            V[4].wrote(o_g)
            e3 = V[5].t[:].rearrange("p (c s) -> p c s", s=64)
            o = p.op("dve", X("tensor_tensor", out=e3, in0=b3, in1=mcur_b, op=ALU.add), deps=[o_b, o_mp] + V[5].wdeps())
            o_e = p.op("act", X("activation", out=V[5].t[:], in_=V[5].t[:], func=AF.Exp, scale=-1.0), deps=[o])
            V[5].wrote(o_e)
            bank = psA.next()
            d0 = bank.wdeps() + [o_g, dc[5]]
            for tt in range(4):
                mm = p.op("pe", X("matmul", bank.t[:, 2 * tt:2 * tt + 2], lhsT=V[4].t[0:2, tt * 128:(tt + 1) * 128], rhs=ci2[0:2, 0:2], start=True, stop=True), deps=d0 if tt == 0 else [])
            V[4].read(mm)
            for h in range(NH):
                mm = p.op("pe", X("matmul", bank.t[:, 16 + 8 * h:24 + 8 * h], lhsT=csel[0:2, h * 128:(h + 1) * 128], rhs=gs.t[0:2, 32:40], start=True, stop=True), deps=[o_dec, dc[4]])
            gs.read(mm)
            bank.wrote(mm)
            o = p.op("act", X("copy", out=G128.t[:].rearrange("p t h -> p (t h)"), in_=bank.t[:, 0:8]), deps=[mm] + G128.wdeps())
            G128.wrote(o)
            o2 = p.op("act", X("copy", out=DECb.t[:].rearrange("p c h -> p h c"), in_=bank.t[:, 16:32].rearrange("p (h c) -> p h c", h=NH)), deps=[mm] + DECb.wdeps())
            DECb.wrote(o2)
            bank.read(o2)
            o = p.op("dve", X("tensor_tensor", out=SC.t[:], in0=G128.t[:], in1=rcol.t[:, 4:8].unsqueeze(2).to_broadcast([128, 4, NH]), op=ALU.mult), deps=[o] + rcol.rdeps() + SC.wdeps())
            SC.wrote(o)
            G128.read(o)
            first = True
            for h in range(NH):
                bank = psA.next()
                mm = p.op("pe", X("matmul", bank.t[:], lhsT=csel[0:2, h * 128:(h + 1) * 128], rhs=V[5].t[0:2, :], start=True, stop=True), deps=bank.wdeps() + [o_e, dc[4]])
                bank.wrote(mm)
                V[5].read(mm)
                o = p.op("act", X("copy", out=EMWb.t[:, h, :], in_=bank.t[:]), deps=[mm] + EMWb.wdeps())
                bank.read(o)
                if first:
                    EMWb.wrote(o)
                    first = False
                else:
                    EMWb.also_wrote(o)

            fq = fk = fkg = fv = fo = True
            for h in range(NH):
                s = load_wA("q", h * 256)
                for dci in range(DKC):
                    bank, mm = proj_fm(s, dci * 128)
                    o = p.op("dve", X("tensor_tensor", out=QT.t[:, h * DKC + dci, :], in0=bank.t[:], in1=R.t[:], op=ALU.mult), deps=[mm, r_op] + QT.wdeps())
                    bank.read(o)
                    R.read(o)
                    QT.wrote(o) if fq else QT.also_wrote(o)
                    fq = False
            for h in range(NH):
                s = load_wA("k", h * 256)
                for dci in range(DKC):
                    bank, mm = proj_fm(s, dci * 128)
                    o = p.op("dve", X("scalar_tensor_tensor", out=KT.t[:, h * DKC + dci, :], in0=bank.t[:], scalar=DK ** -0.5, in1=R.t[:], op0=ALU.mult, op1=ALU.mult), deps=[mm, r_op] + KT.wdeps())
                    bank.read(o)
                    R.read(o)
                    KT.wrote(o) if fk else KT.also_wrote(o)
                    fk = False
                for tt in range(4):
                    bank, mm = proj_tm(s, tt)
                    o = p.op("act", X("activation", out=KG.t[:, tt, h, :], in_=bank.t[:, 0:256], func=AF.Copy, scale=SC.t[:, tt, h:h + 1]), deps=[mm] + SC.rdeps() + KG.wdeps())
                    bank.read(o)
                    SC.read(o)
                    KG.wrote(o) if fkg else KG.also_wrote(o)
                    fkg = False
            for h in range(NH):
                for half in range(DV // 256):
                    s = load_wA("v", h * DV + half * 256)
                    for tt in range(4):
                        bank, mm = proj_tm(s, tt)
                        o = p.op("act", X("activation", out=VT.t[:, tt, h, half * 256:(half + 1) * 256], in_=bank.t[:, 0:256], func=AF.Copy, scale=rcol.t[:, tt:tt + 1]), deps=[mm] + rcol.rdeps() + VT.wdeps())
                        bank.read(o)
                        rcol.read(o)
                        VT.wrote(o) if fv else VT.also_wrote(o)
                        fv = False
            for h in range(NH):
                for half in range(DV // 256):
                    s = load_wA("o", h * DV + half * 256)
                    for ci in range(2):
                        ec = half * 2 + ci
                        bank, mm = proj_fm(s, ci * 128)
                        ot = OT.next()
                        o1 = p.op("dve", X("tensor_tensor", out=ot.t[:], in0=bank.t[:], in1=R.t[:], op=ALU.mult), deps=[mm, r_op] + ot.wdeps())
                        bank.read(o1)
                        R.read(o1)
                        o2 = p.op("act", X("activation", out=ot.t[:], in_=ot.t[:], func=AF.Sigmoid), deps=[o1])
                        o3 = p.op("pool", X("tensor_scalar", out=OG.t[:, h * DVC + ec, :], in0=ot.t[:], scalar1=hgain[:, h * DVC + ec:h * DVC + ec + 1], scalar2=None, op0=ALU.mult), deps=[o2, dc[1]] + OG.wdeps())
                        ot.wrote(o3)
                        OG.wrote(o3) if fo else OG.also_wrote(o3)
                        fo = False

            fhg = [True]

            def finish(item):
                (pc, ht, h, c, o_ht) = item
                hs = HSQ.next()
                o_sq = p.op("act", X("activation", out=hs.t[:], in_=ht.t[:], func=AF.Square), deps=[o_ht] + hs.wdeps())
                hs.wrote(o_sq)
                mm = None
                for ec in range(DVC):
                    mm = p.op("pe", X("matmul", pc.t[:, 384:448], lhsT=ones[:], rhs=hs.t[:, ec, :], start=(ec == 0), stop=(ec == DVC - 1)), deps=([o_sq] + pc.wdeps()) if ec == 0 else [])
                hs.read(mm)
                pc.also_wrote(mm)
                rs = RS.next()
                o1 = p.op("act", X("activation", out=rs.t[:], in_=pc.t[:, 384:448], func=AF.Sqrt, scale=1.0 / DV, bias=kb.eps[:, 0:1]), deps=[mm, kb.eps_op] + rs.wdeps())
                pc.read(o1)
                o2 = p.op("dve", X("reciprocal", out=rs.t[:], in_=rs.t[:]), deps=[o1])
                o3 = p.op("dve", X("tensor_tensor", out=ht.t[:], in0=ht.t[:], in1=rs.t[:].unsqueeze(1).to_broadcast([128, DVC, 64]), op=ALU.mult), deps=[o2, o_sq])
                rs.wrote(o3)
                o4 = p.op("dve", X("tensor_tensor", out=HG.t[:, h * DVC:(h + 1) * DVC, c * 64:(c + 1) * 64], in0=ht.t[:], in1=OG.t[:, h * DVC:(h + 1) * DVC, c * 64:(c + 1) * 64], op=ALU.mult), deps=[o3] + OG.rdeps() + HG.wdeps())
                ht.wrote(o4)
                OG.read(o4)
                HG.wrote(o4) if fhg[0] else HG.also_wrote(o4)
                fhg[0] = False

            for c in range(8):
                tt = c // 2
                pb = 64 * (c % 2)
                csl = slice(c * 64, (c + 1) * 64)
                tl = slice(tt * 128, (tt + 1) * 128)
                for h in range(NH):
                    pc = psc.next()
                    qs = [h * DKC + i for i in range(DKC)]
                    d0 = pc.wdeps() + QT.rdeps() + KT.rdeps()
                    mm = None
                    for i, qi in enumerate(qs):
                        mm = p.op("pe", X("matmul", pc.t[:, 0:64], lhsT=KT.t[:, qi, tl], rhs=QT.t[:, qi, csl], start=(i == 0), stop=(i == DKC - 1)), deps=d0 if i == 0 else [])
                    pc.wrote(mm)
                    QT.read(mm)
                    KT.read(mm)
                    sw = SW.next()
                    o_sw = p.op("dve", X("scalar_tensor_tensor", out=sw.t[pb:pb + 64, :], in0=pc.t[pb:pb + 64, 0:64], scalar=G128.t[pb:pb + 64, tt, h:h + 1], in1=causal[pb:pb + 64, :], op0=ALU.mult, op1=ALU.mult),
                                deps=[mm, dc[2]] + G128.rdeps() + sw.wdeps())
                    sw.wrote(o_sw)
                    pc.read(o_sw)
                    d0 = [o_sw] + VT.rdeps() + Cb[qs[0]].rdeps() + Cb[qs[1]].rdeps() + nrep[qs[0]].rdeps() + nrep[qs[1]].rdeps()
                    for ec in range(DVC):
                        for i, qi in enumerate(qs):
                            mm = p.op("pe", X("matmul", pc.t[:, 64 + 64 * ec:128 + 64 * ec], lhsT=Cb[qi].t[:, ec * 128:(ec + 1) * 128], rhs=QT.t[:, qi, csl], start=(i == 0), stop=False), deps=d0 if (ec == 0 and i == 0) else [])
                        mm = p.op("pe", X("matmul", pc.t[:, 64 + 64 * ec:128 + 64 * ec], lhsT=VT.t[pb:pb + 64, tt, h, ec * 128:(ec + 1) * 128], rhs=sw.t[pb:pb + 64, :], start=False, stop=True))
                    for i, qi in enumerate(qs):
                        mm = p.op("pe", X("matmul", pc.t[:, 320:384], lhsT=nrep[qi].t[:], rhs=QT.t[:, qi, csl], start=(i == 0), stop=False))
                    mm_nd = p.op("pe", X("matmul", pc.t[:, 320:384], lhsT=ones[pb:pb + 64, :], rhs=sw.t[pb:pb + 64, :], start=False, stop=True))
                    pc.also_wrote(mm_nd)
                    sw.read(mm_nd)
                    for qi in qs:
                        Cb[qi].read(mm_nd)
                        nrep[qi].read(mm_nd)
                    VT.read(mm_nd)
                    QT.read(mm_nd)
                    mmcs = []
                    bcs = []
                    for i, qi in enumerate(qs):
                        bc = psC.next()
                        d0 = bc.wdeps() + KG.rdeps() + VT.rdeps()
                        mmc = p.op("pe", X("matmul", bc.t[:], lhsT=KG.t[pb:pb + 64, tt, h, i * 128:(i + 1) * 128], rhs=VT.t[pb:pb + 64, tt, h, :], start=True, stop=True), deps=d0)
                        bc.wrote(mmc)
                        mmcs.append(mmc)
                        bcs.append(bc)
                    mmn = None
                    for i, qi in enumerate(qs):
                        mmn = p.op("pe", X("matmul", pc.t[:, 448 + i:449 + i], lhsT=KG.t[pb:pb + 64, tt, h, i * 128:(i + 1) * 128], rhs=ones[pb:pb + 64, 0:1], start=True, stop=True))
                    pc.also_wrote(mmn)
                    KG.read(mmn)
                    VT.read(mmn)
                    for i, qi in enumerate(qs):
                        bc = bcs[i]
                        mmc = mmcs[i]
                        dcol = DECb.t[:, c, h:h + 1]
                        o_cd = p.op("act", X("activation", out=Cd[qi].t[:], in_=C32[qi].t[:], func=AF.Copy, scale=dcol), deps=Cd[qi].wdeps() + C32[qi].rdeps() + DECb.rdeps())
                        Cd[qi].wrote(o_cd)
                        o_c = p.op("dve", X("scalar_tensor_tensor", out=C32[qi].t[:], in0=bc.t[:], scalar=dcol, in1=Cd[qi].t[:], op0=ALU.mult, op1=ALU.add), deps=[mmc, o_cd] + DECb.rdeps() + C32[qi].wdeps())
                        bc.read(o_c)
                        Cd[qi].read(o_c)
                        C32[qi].wrote(o_c)
                        o_cb = p.op("pool", X("tensor_copy", out=Cb[qi].t[:], in_=C32[qi].t[:]), deps=[o_c] + Cb[qi].wdeps())
                        Cb[qi].wrote(o_cb)
                        C32[qi].read(o_cb)
                        o_n1 = p.op("dve", X("tensor_tensor", out=n32[qi].t[:, 0:1], in0=n32[qi].t[:, 0:1], in1=pc.t[:, 448 + i:449 + i], op=ALU.add), deps=[mmn] + n32[qi].wdeps())
                        o_n2 = p.op("dve", X("tensor_tensor", out=n32[qi].t[:, 0:1], in0=n32[qi].t[:, 0:1], in1=dcol, op=ALU.mult), deps=[o_n1])
                        n32[qi].wrote(o_n2)
                        pc.read(o_n1)
                        o_nr = p.op("pool", X("tensor_copy", out=nrep[qi].t[:], in_=n32[qi].t[:, 0:1].to_broadcast([128, 128])), deps=[o_n2] + nrep[qi].wdeps())
                        nrep[qi].wrote(o_nr)
                        n32[qi].read(o_nr)
                    DECb.read(o_c)
                    dn = DN.next()
                    o0 = p.op("act", X("activation", out=dn.t[:], in_=pc.t[:, 320:384], func=AF.Abs), deps=[mm_nd, mmn] + dn.wdeps())
                    pc.read(o0)
                    o1 = p.op("dve", X("tensor_tensor", out=dn.t[:], in0=dn.t[:], in1=EMWb.t[:, h, csl], op=ALU.max), deps=[o0] + EMWb.rdeps())
                    EMWb.read(o1)
                    o2 = p.op("dve", X("reciprocal", out=dn.t[:], in_=dn.t[:]), deps=[o1])
                    ht = HTt.next()
                    o_ht = p.op("dve", X("tensor_tensor", out=ht.t[:], in0=pc.t[:, 64:320].rearrange("p (e t) -> p e t", e=DVC), in1=dn.t[:].unsqueeze(1).to_broadcast([128, DVC, 64]), op=ALU.mult), deps=[o2, mm_nd, mmn] + ht.wdeps())
                    dn.wrote(o_ht)
                    ht.wrote(o_ht)
                    pc.read(o_ht)
                    if pend[0] is not None:
                        finish(pend[0])
                    pend[0] = (pc, ht, h, c, o_ht)
            finish(pend[0])
            pend[0] = None
            o = p.dma("sp", out_v[:, :, tsl], HG.t[:], deps=HG.rdeps(), ring="hgout", nslots=1)
            HG.read(o)
            out_dmas.append(o)
        p.wait("sp", out_dmas)
        p.emit(ctx)
    return nc


def build_C(S, HD=128, HPG=8, CMP_BLOCK=32, CMP_STRIDE=16, CMP_HID=512, SEL_BLOCK=64, N_SEL=16, WINDOW=512):
    nc = bass.Bass("TRN2", target_bir_lowering=False)
    NQB = S // 64
    NCMP = (S - CMP_BLOCK) // CMP_STRIDE + 1
    NCC = (NCMP + 127) // 128
    NSB = S // SEL_BLOCK
    NKC = S // 128
    HC = CMP_HID // 128
    scale = HD ** -0.5
    with ExitStack() as ctx:
        kb = KB(nc, ctx)
        p = kb.p
        qT_d = kb.din("qT", [HPG * HD, S], BF16)
        gT_d = kb.din("gT", [3 * HPG, S], F32)
        kcT_d = kb.din("kcT", [HD, S], BF16)
        vcT_d = kb.din("vcT", [HD, S], BF16)
        ksT_d = kb.din("ksT", [HD, S], BF16)
        kwT_d = kb.din("kwT", [HD, S], BF16)
        vs_d = kb.din("vs", [S, HD], BF16)
        vw_d = kb.din("vw", [S, HD], BF16)
        w1_d = kb.din("cmp_w1", [2, CMP_BLOCK * HD, CMP_HID], F32)
        w2_d = kb.din("cmp_w2", [2, CMP_HID, HD], F32)
        posT_d = kb.din("posT", [HD, 2 * CMP_BLOCK], F32)
        kng_d = kb.din("kng", [HD, 3], F32)
        qng_d = kb.din("qng", [HD, 3], F32)
        ov_d = kb.din("c_ov", [128, NCC * NSB], F32)
        ex_d = kb.din("c_ex", [64, NKC * 128], F32)
        selg_d = kb.din("c_selg", [3 * HPG, 3 * HPG * 128], F32)
        i64_d = kb.din("c_i64", [64, 64], F32)
        o_d = kb.dout("oT", [HPG * HD, S], BF16)

        Q = Buf(kb.sb("Q", [128, HPG, S], BF16))
        GT = Buf(kb.sb("GT", [3 * HPG, S], F32))
        KS = Buf(kb.sb("KS", [128, S], BF16))
        KW = Buf(kb.sb("KW", [128, S], BF16))
        VS = Buf(kb.sb("VS", [128, NKC, HD], BF16))
        VW = Buf(kb.sb("VW", [128, NKC, HD], BF16))
        KC = Buf(kb.sb("KC", [128, NCC * 128], BF16))
        VC = Buf(kb.sb("VC", [128, NCC, HD], BF16))
        RAW = kb.sbring("RAW", 1, [128, S], BF16)
        if HPG * S >= 2 * CMP_BLOCK * CMP_HID:
            hq = HPG // 2
            W1 = Buf(Q.t[:, hq:HPG, :].rearrange("p h s -> p (h s)")[:, 0:CMP_BLOCK * CMP_HID].rearrange("p (i n) -> p i n", i=CMP_BLOCK))
            w1_alias = True
        else:
            W1 = Buf(kb.sb("W1", [128, CMP_BLOCK, CMP_HID], BF16))
            w1_alias = False
        W2 = Buf(kb.sb("W2", [128, 2, HC, HD], BF16))
        posT = kb.sb("posT_sb", [128, 2 * CMP_BLOCK], F32)
        posTb = kb.sb("posT_bf", [128, 2 * CMP_BLOCK], BF16)
        kng = kb.sb("kng_sb", [128, 3], F32)
        qng = kb.sb("qng_sb", [128, 3], F32)
        kcg = kb.sb("kcg", [128, 1], F32)
        ov = kb.sb("ov", [128, NCC * NSB], BF16)
        ex = kb.sb("ex", [64, NKC * 128], BF16)
        selg = kb.sb("selg", [3 * HPG, 3 * HPG * 128], F32)
        i64 = kb.sb("i64", [64, 64], BF16)
        ones = kb.sb("ones", [128, 128], BF16)
        tiny = kb.sb("tiny", [128, 1], F32)
        HID = kb.sbring("HID", 2, [128, HC, 256], BF16)
        Z = kb.sbring("Z", 2, [128, 256], F32)
        Z2 = kb.sbring("Z2", 2, [128, 256], F32)
        pbias = Buf(kb.sb("pbias", [128, 2 * HC], F32))
        PT = kb.sbring("PT", 4, [128, HPG, 64], BF16)
        PN = kb.sbring("PN", 2, [128, HPG, 64], BF16)
        RD = kb.sbring("RD", 2, [128, 512], F32)
        CF = kb.sbring("CF", 2, [128, 512], F32)
        OACC = Buf(kb.sb("OACC", [128, 512], F32))
        OB = kb.sbring("OBo", 2, [128, HPG, 64], BF16)
        PM = kb.sbring("PMx", 2, [64, 64], F32)
        PM2 = kb.sbring("PM2", 2, [64, 64], F32)
        M8 = kb.sbring("M8", 2, [64, 16], F32)
        SEL = kb.sbring("SEL", 2, [64, 64], BF16)
        SELT = kb.sbring("SELT", 2, [64, 64], BF16)
        sqb = kb.sbring("SQc", 2, [128, 256], BF16)
        RK = Buf(kb.sb("RK", [128, 256], F32))

        psS = kb.psring("psS", 2)
        psN = kb.psring("psN", 2)
        psD = kb.psring("psD", 2)
        psM = Buf(kb.psum("psM", [128, 512]))
        psX = Buf(kb.psum("psX", [128, 512]))
        psB = psX

        c_ones = p.op("dve", X("memset", ones[:], 1.0))
        kb.eps = kb.sb("eps_sb", [128, 1], F32)
        kb.eps_op = p.op("dve", X("memset", kb.eps[:], EPS))
        c_tiny = p.op("dve", X("memset", tiny[:], 1e-30))
        dq = []
        for h in range(HPG // 2 if w1_alias else HPG):
            dq.append(p.dma("sp", Q.t[:, h, :], qT_d[h * 128:(h + 1) * 128, :], ring="ld", nslots=24))
        Q.wrote(dq[0])
        for o in dq[1:]:
            Q.also_wrote(o)
        o = p.dma("sp", GT.t[:], gT_d[:, :], ring="ld", nslots=24); GT.wrote(o)
        o = p.dma("sp", KS.t[:], ksT_d[:, :], ring="ld", nslots=24); KS.wrote(o)
        o = p.dma("sp", KW.t[:], kwT_d[:, :], ring="ld", nslots=24); KW.wrote(o)
        o = p.dma("sp", VS.t[:], vs_d.rearrange("(c p) d -> p c d", p=128), ring="ld", nslots=24); VS.wrote(o)
        o = p.dma("sp", VW.t[:], vw_d.rearrange("(c p) d -> p c d", p=128), ring="ld", nslots=24); VW.wrote(o)
        d_pos = p.dma("sp", posT[:], posT_d[:, :], ring="ld", nslots=24)
        d_kng = p.dma("sp", kng[:], kng_d[:, :], ring="ld", nslots=24)
        d_qng = p.dma("sp", qng[:], qng_d[:, :], ring="ld", nslots=24)
        d_selg = p.dma("sp", selg[:], selg_d[:, :], ring="ld", nslots=24)
        c_ov = p.dma("pool", ov[:], ov_d[:, :], ring="ldc", nslots=4)
        c_i64 = p.dma("pool", i64[:], i64_d[:, :], ring="ldc", nslots=4)
        c_ex = p.dma("pool", ex[:], ex_d[:, :], ring="ldc", nslots=4)
        c_pos = p.op("dve", X("tensor_copy", out=posTb[:], in_=posT[:]), deps=[d_pos])
        o = p.op("dve", X("tensor_tensor", out=kcg[:], in0=kng[:, 0:1], in1=qng[:, 0:1], op=ALU.mult), deps=[d_kng, d_qng])
        c_kcg = p.op("dve", X("tensor_scalar", out=kcg[:], in0=kcg[:], scalar1=scale, scalar2=None, op0=ALU.mult), deps=[o])
        z0 = p.op("pool", X("memset", KC.t[:], 0.0)); KC.wrote(z0)
        z1 = p.op("pool", X("memset", VC.t[:], 0.0)); VC.wrote(z1)
        w2l = p.dma("pool", W2.t[:], w2_d.rearrange("j (c p) d -> p j c d", p=128), ring="w2", nslots=1)
        W2.wrote(w2l)

        for j in range(2):
            raw = RAW.next()
            dr = p.dma("sp", raw.t[:], (kcT_d if j == 0 else vcT_d)[:, :], deps=raw.wdeps(), ring="ldr", nslots=2)
            raw.wrote(dr)
            dw = p.dma("pool", W1.t[:], w1_d[j].rearrange("(i p) n -> p i n", p=128), deps=W1.wdeps(), ring="w1", nslots=1)
            W1.wrote(dw)
            bank = psB
            d0 = bank.wdeps() + [dw, c_pos]
            mm = None
            for hc in range(HC):
                for i in range(CMP_BLOCK):
                    mm = p.op("pe", X("matmul", bank.t[:, hc:hc + 1], lhsT=W1.t[:, i, hc * 128:(hc + 1) * 128], rhs=posTb[:, j * CMP_BLOCK + i:j * CMP_BLOCK + i + 1], start=(i == 0), stop=(i == CMP_BLOCK - 1)), deps=d0 if (hc == 0 and i == 0) else [])
            bank.wrote(mm)
            o_pb = p.op("dve", X("tensor_copy", out=pbias.t[:, j * HC:(j + 1) * HC], in_=bank.t[:, 0:HC]), deps=[mm] + pbias.wdeps())
            bank.read(o_pb)
            pbias.wrote(o_pb)
            hid = HID.next()
            first = True
            for hc in range(HC):
                for c0 in range(0, NCMP, 256):
                    ncol = min(256, NCMP - c0)
                    bank = psS.next()
                    d0 = bank.wdeps() + [dw, dr]
                    for i in range(CMP_BLOCK):
                        t0 = CMP_STRIDE * c0 + i
                        mm = p.op("pe", X("matmul", bank.t[:, 0:ncol], lhsT=W1.t[:, i, hc * 128:(hc + 1) * 128], rhs=raw.t[:, t0:t0 + CMP_STRIDE * (ncol - 1) + 1:CMP_STRIDE], start=(i == 0), stop=(i == CMP_BLOCK - 1)), deps=d0 if i == 0 else [])
                    bank.wrote(mm)
                    W1.read(mm)
                    raw.read(mm)
                    z = Z.next()
                    z2 = Z2.next()
                    o1 = p.op("act", X("activation", out=z.t[:, 0:ncol], in_=bank.t[:, 0:ncol], func=AF.Identity, bias=pbias.t[:, j * HC + hc:j * HC + hc + 1]), deps=[mm, o_pb] + z.wdeps())
                    bank.read(o1)
                    o2 = p.op("dve", X("tensor_tensor", out=z2.t[:, 0:ncol], in0=z.t[:, 0:ncol], in1=z.t[:, 0:ncol], op=ALU.mult), deps=[o1] + z2.wdeps())
                    o3 = p.op("dve", X("tensor_scalar", out=z2.t[:, 0:ncol], in0=z2.t[:, 0:ncol], scalar1=0.044715, scalar2=1.0, op0=ALU.mult, op1=ALU.add), deps=[o2])
                    o4 = p.op("dve", X("tensor_tensor", out=z2.t[:, 0:ncol], in0=z2.t[:, 0:ncol], in1=z.t[:, 0:ncol], op=ALU.mult), deps=[o3])
                    o5 = p.op("act", X("activation", out=z2.t[:, 0:ncol], in_=z2.t[:, 0:ncol], func=AF.Sigmoid, scale=1.5957691216057308), deps=[o4])
                    o6 = p.op("dve", X("tensor_tensor", out=hid.t[:, hc, c0:c0 + ncol] if NCMP <= 256 else hid.t[:, hc, 0:ncol], in0=z.t[:, 0:ncol], in1=z2.t[:, 0:ncol], op=ALU.mult), deps=[o5] + hid.wdeps())
                    z.wrote(o6)
                    z2.wrote(o6)
                    hid.wrote(o6) if first else hid.also_wrote(o6)
                    first = False
            assert NCMP <= 256
            if j == 0:
                bank = psS.next()
                d0 = bank.wdeps() + hid.rdeps() + [w2l]
                for hc in range(HC):
                    mm = p.op("pe", X("matmul", bank.t[:, 0:NCMP], lhsT=W2.t[:, 0, hc, :], rhs=hid.t[:, hc, 0:NCMP], start=(hc == 0), stop=(hc == HC - 1)), deps=d0 if hc == 0 else [])
                bank.wrote(mm)
                hid.read(mm)
                sq = sqb.next()
                o_sq = p.op("act", X("activation", out=sq.t[:, 0:NCMP], in_=bank.t[:, 0:NCMP], func=AF.Square), deps=[mm] + sq.wdeps())
                sq.wrote(o_sq)
                bank.read(o_sq)
                mm2 = p.op("pe", X("matmul", psB.t[:, 0:NCMP], lhsT=ones[:], rhs=sq.t[:, 0:NCMP], start=True, stop=True), deps=[o_sq, c_ones] + psB.wdeps())
                psB.wrote(mm2)
                sq.read(mm2)
                o1 = p.op("act", X("activation", out=RK.t[:, 0:NCMP], in_=psB.t[:, 0:NCMP], func=AF.Sqrt, scale=1.0 / HD, bias=kb.eps[:, 0:1]), deps=[mm2, kb.eps_op] + RK.wdeps())
                psB.read(o1)
                o2 = p.op("dve", X("reciprocal", out=RK.t[:, 0:NCMP], in_=RK.t[:, 0:NCMP]), deps=[o1])
                o3 = p.op("dve", X("scalar_tensor_tensor", out=KC.t[:, 0:NCMP], in0=bank.t[:, 0:NCMP], scalar=kcg[:, 0:1], in1=RK.t[:, 0:NCMP], op0=ALU.mult, op1=ALU.mult), deps=[o2, c_kcg, mm] + KC.wdeps())
                bank.read(o3)
                RK.wrote(o3)
                KC.wrote(o3)
            else:
                for cc in range(NCC):
                    ncol = min(128, NCMP - cc * 128)
                    bank = psS.next()
                    d0 = bank.wdeps() + hid.rdeps() + [w2l]
                    for hc in range(HC):
                        mm = p.op("pe", X("matmul", bank.t[0:ncol, 0:HD], lhsT=hid.t[:, hc, cc * 128:cc * 128 + ncol], rhs=W2.t[:, 1, hc, :], start=(hc == 0), stop=(hc == HC - 1)), deps=d0 if hc == 0 else [])
                    bank.wrote(mm)
                    hid.read(mm)
                    o = p.op("act", X("copy", out=VC.t[0:ncol, cc, :], in_=bank.t[0:ncol, 0:HD]), deps=[mm] + VC.wdeps())
                    bank.read(o)
                    VC.wrote(o) if cc == 0 else VC.also_wrote(o)

        if w1_alias:
            for h in range(HPG // 2, HPG):
                o = p.dma("sp", Q.t[:, h, :], qT_d[h * 128:(h + 1) * 128, :], deps=W1.wdeps(), ring="ld", nslots=24)
                Q.also_wrote(o)
        out_dmas = []
        o_v = o_d.rearrange("(h p) t -> p h t", p=128)
        for qb in range(NQB):
            qsl = slice(qb * 64, (qb + 1) * 64)
            t0 = qb * 64
            rhsq = Q.t[:, :, qsl]
            need_topk = (qb + 1) > N_SEL
            kc_last = qb // 2
            st = {"selT": None, "pts": [], "rd": None}

            def stage1(job):
                (br, kT_ap, v_ap, kdeps, first, last, mask_fn) = job
                bank = psS.next()
                mm = p.op("pe", X("matmul", bank.t[:], lhsT=kT_ap, rhs=rhsq, start=True, stop=True), deps=bank.wdeps() + Q.rdeps() + kdeps)
                bank.wrote(mm)
                pt = PT.next()
                o = p.op("act", X("activation", out=pt.t[:].rearrange("p h q -> p (h q)"), in_=bank.t[:], func=AF.Exp), deps=[mm] + pt.wdeps())
                bank.read(o)
                pt.wrote(o)
                o = mask_fn(pt, o)
                return (pt, o)

            def stage2(job, s1, acc):
                (br, kT_ap, v_ap, kdeps, first, last, mask_fn) = job
                pt, o = s1
                bN, bD = acc
                d0 = [o]
                if first:
                    d0 = d0 + bN.wdeps() + bD.wdeps()
                ptf = pt.t[:].rearrange("p h q -> p (h q)")
                m1 = p.op("pe", X("matmul", bN.t[:], lhsT=v_ap, rhs=ptf, start=first, stop=last), deps=d0 + kdeps)
                m2 = p.op("pe", X("matmul", bD.t[:], lhsT=ones[:], rhs=ptf, start=first, stop=last), deps=[c_ones])
                pt.read(m2)
                if last:
                    bN.wrote(m1)
                    bD.wrote(m2)
                return m2

            def finish_branch(r, m_last, first_branch, acc):
                bN, bD = acc
                d0 = psX.wdeps() + GT.rdeps() + [d_selg]
                mm = None
                for h in range(HPG):
                    row = r * HPG + h
                    mm = p.op("pe", X("matmul", psX.t[:, h * 64:(h + 1) * 64], lhsT=selg[:, row * 128:(row + 1) * 128], rhs=GT.t[:, qsl], start=True, stop=True), deps=d0 if h == 0 else [])
                psX.wrote(mm)
                rd = RD.next()
                o1 = p.op("dve", X("tensor_scalar", out=rd.t[:], in0=bD.t[:], scalar1=tiny[:, 0:1], scalar2=None, op0=ALU.max), deps=[m_last, c_tiny] + rd.wdeps())
                bD.read(o1)
                o2 = p.op("dve", X("reciprocal", out=rd.t[:], in_=rd.t[:]), deps=[o1])
                rd.wrote(o2)
                cf = CF.next()
                o3 = p.op("dve", X("tensor_tensor", out=cf.t[:], in0=psX.t[:], in1=rd.t[:], op=ALU.mult), deps=[o2, mm] + cf.wdeps())
                psX.read(o3)
                rd.read(o3)
                if first_branch:
                    o4 = p.op("dve", X("tensor_tensor", out=OACC.t[:], in0=bN.t[:], in1=cf.t[:], op=ALU.mult), deps=[o3, m_last] + OACC.wdeps())
                else:
                    o4a = p.op("dve", X("tensor_tensor", out=cf.t[:], in0=bN.t[:], in1=cf.t[:], op=ALU.mult), deps=[o3, m_last])
                    o4 = p.op("dve", X("tensor_tensor", out=OACC.t[:], in0=OACC.t[:], in1=cf.t[:], op=ALU.add), deps=[o4a] + OACC.wdeps())
                cf.wrote(o4)
                bN.read(o4)
                OACC.wrote(o4)
                return rd, o2

            def topk(rd, o_rd, pts):
                pns = []
                for cc, pt in enumerate(pts):
                    pn = PN.next()
                    o = p.op("dve", X("tensor_tensor", out=pn.t[:].rearrange("p h q -> p (h q)"), in0=pt.t[:].rearrange("p h q -> p (h q)"), in1=rd.t[:], op=ALU.mult), deps=[o_rd] + pt.rdeps() + pn.wdeps())
                    pn.wrote(o)
                    pt.read(o)
                    rd.read(o)
                    pns.append((pn, o))
                d0 = psX.wdeps() + [c_ov]
                n_mm = len(pns) * HPG
                i = 0
                mm = None
                for cc, (pn, o) in enumerate(pns):
                    for h in range(HPG):
                        mm = p.op("pe", X("matmul", psX.t[0:64, 0:NSB], lhsT=pn.t[:, h, :], rhs=ov[:, cc * NSB:(cc + 1) * NSB], start=(i == 0), stop=(i == n_mm - 1)), deps=(d0 if i == 0 else []) + ([o] if h == 0 else []))
                        i += 1
                    pn.read(mm)
                psX.wrote(mm)
                pm = PM.next()
                o1 = p.op("dve", X("tensor_copy", out=pm.t[:, 0:NSB], in_=psX.t[0:64, 0:NSB]), deps=[mm] + pm.wdeps())
                psX.read(o1)
                o2 = p.op("dve", X("memset", pm.t[:, 0:1], 1e30), deps=[o1])
                o3 = p.op("dve", X("memset", pm.t[:, qb:qb + 1], 1e30), deps=[o2])
                if qb + 1 < NSB:
                    o3 = p.op("dve", X("memset", pm.t[:, qb + 1:NSB], -1.0), deps=[o3])
                m8 = M8.next()
                o4 = p.op("dve", X("max", out=m8.t[:, 0:8], in_=pm.t[:, 0:NSB]), deps=[o3] + m8.wdeps())
                pm2 = PM2.next()
                o5 = p.op("dve", X("match_replace", out=pm2.t[:, 0:NSB], in_to_replace=m8.t[:, 0:8], in_values=pm.t[:, 0:NSB], imm_value=-2.0), deps=[o4] + pm2.wdeps())
                o6 = p.op("dve", X("max", out=m8.t[:, 8:16], in_=pm2.t[:, 0:NSB]), deps=[o5])
                pm2.wrote(o6)
                sel = SEL.next()
                o7 = p.op("dve", X("tensor_scalar", out=sel.t[:, 0:NSB], in0=pm.t[:, 0:NSB], scalar1=m8.t[:, 15:16], scalar2=None, op0=ALU.is_ge), deps=[o6] + sel.wdeps())
                m8.wrote(o7)
                pm.wrote(o7)
                sel.wrote(o7)
                mmT = p.op("pe", X("matmul", psX.t[0:NSB, 64:128], lhsT=sel.t[:, 0:NSB], rhs=i64[:], start=True, stop=True), deps=[o7, c_i64] + psX.wdeps())
                psX.wrote(mmT)
                sel.read(mmT)
                selT = SELT.next()
                o8 = p.op("act", X("copy", out=selT.t[0:NSB, :], in_=psX.t[0:NSB, 64:128]), deps=[mmT] + selT.wdeps())
                psX.read(o8)
                selT.wrote(o8)
                st["selT"] = selT

            jobs = []
            ncv = min(NCMP, (t0 + 63 - (CMP_BLOCK - 1)) // CMP_STRIDE + 1)
            ncc_q = max(1, (ncv + 127) // 128)
            for cc in range(ncc_q):
                def mask_cmp(pt, o, cc=cc):
                    o2 = p.op("pool", X("affine_select", out=pt.t[:], in_=pt.t[:], pattern=[[0, HPG], [1, 64]], compare_op=ALU.is_ge, fill=0.0,
                                        base=t0 - (CMP_BLOCK - 1) - CMP_STRIDE * 128 * cc, channel_multiplier=-CMP_STRIDE), deps=[o])
                    pt.wrote(o2)
                    st["pts"].append(pt)
                    return o2
                jobs.append((0, KC.t[:, cc * 128:(cc + 1) * 128], VC.t[:, cc, :], KC.rdeps() + VC.rdeps(), cc == 0, cc == ncc_q - 1, mask_cmp))
            kc_first = max(0, (t0 - (WINDOW - 1)) // 128)
            for kc in range(kc_first, kc_last + 1):
                def mask_win(pt, o, kc=kc):
                    if 128 * kc < t0 + 63 - (WINDOW - 1):
                        o = p.op("pool", X("affine_select", out=pt.t[:], in_=pt.t[:], pattern=[[0, HPG], [-1, 64]], compare_op=ALU.is_ge, fill=0.0,
                                           base=128 * kc - t0 + WINDOW - 1, channel_multiplier=1), deps=[o])
                        pt.wrote(o)
                    if 128 * kc + 127 > t0:
                        o = p.op("pool", X("affine_select", out=pt.t[:], in_=pt.t[:], pattern=[[0, HPG], [1, 64]], compare_op=ALU.is_ge, fill=0.0,
                                           base=t0 - 128 * kc, channel_multiplier=-1), deps=[o])
                        pt.wrote(o)
                    return o
                jobs.append((2, KW.t[:, kc * 128:(kc + 1) * 128], VW.t[:, kc, :], KW.rdeps() + VW.rdeps(), kc == kc_first, kc == kc_last, mask_win))
            for kc in range(kc_last + 1):
                def mask_sel(pt, o, kc=kc):
                    if need_topk:
                        selT = st["selT"]
                        bm = psM
                        mmx = p.op("pe", X("matmul", bm.t[:, 0:64], lhsT=ex[0:NSB, kc * 128:(kc + 1) * 128], rhs=selT.t[0:NSB, :], start=True, stop=True), deps=bm.wdeps() + selT.rdeps() + [c_ex])
                        bm.wrote(mmx)
                        selT.read(mmx)
                        o = p.op("dve", X("tensor_tensor", out=pt.t[:], in0=pt.t[:], in1=bm.t[:, 0:64].unsqueeze(1).to_broadcast([128, HPG, 64]), op=ALU.mult), deps=[o, mmx])
                        bm.read(o)
                        pt.wrote(o)
                    if kc == kc_last:
                        o = p.op("pool", X("affine_select", out=pt.t[:], in_=pt.t[:], pattern=[[0, HPG], [1, 64]], compare_op=ALU.is_ge, fill=0.0,
                                           base=t0 - 128 * kc, channel_multiplier=-1), deps=[o])
                        pt.wrote(o)
                    return o
                jobs.append((1, KS.t[:, kc * 128:(kc + 1) * 128], VS.t[:, kc, :], KS.rdeps() + VS.rdeps(), kc == 0, kc == kc_last, mask_sel))

            nj = len(jobs)
            accs = {}
            s1s = [None] * nj
            def can_issue(i):
                return not (jobs[i][0] == 1 and need_topk and st["selT"] is None)
            if can_issue(0):
                s1s[0] = stage1(jobs[0])
            first_branch = True
            for i in range(nj):
                if i + 1 < nj and s1s[i + 1] is None and can_issue(i + 1):
                    s1s[i + 1] = stage1(jobs[i + 1])
                if s1s[i] is None:
                    s1s[i] = stage1(jobs[i])
                br = jobs[i][0]
                if jobs[i][4]:
                    accs[br] = (psN.next(), psD.next())
                m2 = stage2(jobs[i], s1s[i], accs[br])
                if jobs[i][5]:
                    rd, o_rd = finish_branch(br, m2, first_branch, accs[br])
                    first_branch = False
                    if br == 0 and need_topk:
                        topk(rd, o_rd, st["pts"])

            ob = OB.next()
            o = p.op("act", X("copy", out=ob.t[:].rearrange("p h q -> p (h q)"), in_=OACC.t[:]), deps=OACC.rdeps() + ob.wdeps())
            OACC.read(o)
            ob.wrote(o)
            od = p.dma("sp", o_v[:, :, qsl], ob.t[:], deps=[o], ring="oout", nslots=2)
            ob.read(od)
            out_dmas.append(od)
        p.wait("sp", out_dmas)
        p.emit(ctx)
    return nc


D_MODEL = 4096
SEQ = 4096
BATCH = 2
D_FF = 4 * D_MODEL
M_HEADS = 8
M_DV = 512
M_DK = 256
N_KV = 4
HPG = 8
HEAD_DIM = 128
KV_DIM = 512
NCORES = 8


def _lay(g):
    return np.ascontiguousarray(np.asarray(g, np.float32).reshape(-1, 128).T)


def _consts_A():
    cc = np.tril(np.ones((64, 64), np.float32)).T
    cc = np.ascontiguousarray(np.concatenate([cc, cc], 0))
    seg = np.ones((2, 512), np.float32)
    seg[:, ::64] = 0
    neg = np.zeros((2, 512), np.float32)
    neg[:, ::64] = -1e30
    sel = np.zeros((2, 256), np.float32)
    sel[0, :128] = 1
    sel[1, 128:] = 1
    return {"c_causal": cc, "c_seg": np.concatenate([seg, neg], 1), "c_sel": sel, "c_i2": np.eye(2, dtype=np.float32)}


def _consts_C(S):
    NCMP = (S - 32) // 16 + 1
    NCC = (NCMP + 127) // 128
    NSB = S // 64
    NKC = S // 128
    c0 = np.arange(NCC * 128)[:, None] * 16
    s0 = np.arange(NSB)[None, :] * 64
    ovm = np.maximum(np.minimum(c0 + 32, s0 + 64) - np.maximum(c0, s0), 0) / 32.0
    ovm[NCMP:] = 0
    ov = np.ascontiguousarray(ovm.reshape(NCC, 128, NSB).transpose(1, 0, 2).reshape(128, NCC * NSB).astype(np.float32))
    ex = np.zeros((64, NKC * 128), np.float32)
    pp = np.arange(NKC * 128)
    jj = 2 * (pp // 128) + (pp % 128) // 64
    ok = jj < 64
    ex[jj[ok], pp[ok]] = 1
    selg = np.zeros((24, 24 * 128), np.float32)
    for r in range(24):
        selg[r, r * 128:(r + 1) * 128] = 1
    return {"c_ov": ov, "c_ex": ex, "c_selg": selg, "c_i64": np.eye(64, dtype=np.float32)}


_NC_CACHE = {}


def _get(name, fn):
    if name not in _NC_CACHE:
        _NC_CACHE[name] = fn()
    return _NC_CACHE[name]


def _run(nc, in_maps):
    res = run_bass_kernel_spmd(nc, in_maps, core_ids=list(range(len(in_maps))))
    return res.results


def kernel(x, attn_norm_g, mlp_norm_g, m_w_in, m_b_gate, m_head_g, m_w_out, kv_norm_g, w_kv,
           k_norm_g, cmp_pos, cmp_w1, cmp_w2, n_w_qg, q_norm_g, n_w_out, mlp_w_up, mlp_w_down):
    f32 = np.float32
    x = np.asarray(x, f32)
    B, S, D = x.shape
    T = S // 4
    xT = [np.ascontiguousarray(x[b].T) for b in range(B)]
    m_w_in = np.asarray(m_w_in, f32)[0]
    QK = M_HEADS * M_DK
    ncA = _get("A", lambda: build_A(S, D, M_DK, M_DV))
    cA = _consts_A()
    bgate = np.asarray(m_b_gate, f32)[0]
    hg_all = np.asarray(m_head_g, f32)[0]
    ga0 = _lay(np.asarray(attn_norm_g, f32)[0])
    mapsA = []
    for c in range(NCORES):
        b, hp = c // 4, c % 4
        h0 = 2 * hp
        cq = slice(h0 * M_DK, (h0 + 2) * M_DK)
        cv = slice(h0 * M_DV, (h0 + 2) * M_DV)
        gcols = [2 * QK + 2 * D + h0, 2 * QK + 2 * D + h0 + 1, 2 * QK + 2 * D + M_HEADS + h0, 2 * QK + 2 * D + M_HEADS + h0 + 1]
        m = {"xT": xT[b], "ga": ga0,
             "w_q": np.ascontiguousarray(m_w_in[:, 0:QK][:, cq]),
             "w_k": np.ascontiguousarray(m_w_in[:, QK:2 * QK][:, cq]),
             "w_v": np.ascontiguousarray(m_w_in[:, 2 * QK:2 * QK + D][:, cv]),
             "w_og": np.ascontiguousarray(m_w_in[:, 2 * QK + D:2 * QK + 2 * D][:, cv]),
             "w_g": np.ascontiguousarray(m_w_in[:, gcols]),
             "bg": np.ascontiguousarray(np.stack([bgate[[h0, h0 + 1]], bgate[[M_HEADS + h0, M_HEADS + h0 + 1]]], axis=1)),
             "hgain": _lay(hg_all[h0 * M_DV:(h0 + 2) * M_DV])}
        m.update(cA)
        mapsA.append(m)
    rA = _run(ncA, mapsA)
    aT = [np.concatenate([np.asarray(rA[b * 4 + hp]["hgT"]) for hp in range(4)], axis=0) for b in range(B)]
    del rA, mapsA

    ncB = _get("B", lambda: build_BD(T, D, D_FF, True))
    gainsB = np.ascontiguousarray(np.concatenate([_lay(np.asarray(mlp_norm_g, f32)[0]), _lay(np.asarray(attn_norm_g, f32)[1]), _lay(np.asarray(kv_norm_g, f32))], axis=1))
    kngT = np.ascontiguousarray(np.asarray(k_norm_g, f32).T)
    qngT = np.ascontiguousarray(np.asarray(q_norm_g, f32)[0].T)
    w_o0 = np.asarray(m_w_out, f32)[0]
    w_up0 = np.asarray(mlp_w_up, f32)[0]
    w_dn0 = np.asarray(mlp_w_down, f32)[0]
    w_qg = np.asarray(n_w_qg, f32)[0]
    w_kv_ = np.asarray(w_kv, f32)
    mapsB = []
    for c in range(NCORES):
        b, r = c // 4, c % 4
        ts = slice(r * T, (r + 1) * T)
        mapsB.append({"xT": np.ascontiguousarray(xT[b][:, ts]), "aT": np.ascontiguousarray(aT[b][:, ts]),
                      "w_o": w_o0, "w_up": w_up0, "w_dn": w_dn0, "gains": gainsB,
                      "w_qg": w_qg, "w_kv": w_kv_, "kng": kngT, "qng": qngT})
    rB = _run(ncB, mapsB)
    del mapsB

    def catT(name, b):
        return np.concatenate([np.asarray(rB[b * 4 + r][name]) for r in range(4)], axis=1)

    def catR(name, b):
        return np.concatenate([np.asarray(rB[b * 4 + r][name]) for r in range(4)], axis=0)

    x2T = [catT("xoT", b) for b in range(B)]
    ncC = _get("C", lambda: build_C(S))
    cC = _consts_C(S)
    posT = np.ascontiguousarray(np.asarray(cmp_pos, f32).transpose(2, 0, 1).reshape(HEAD_DIM, -1))
    w1 = np.asarray(cmp_w1, f32)
    w2 = np.asarray(cmp_w2, f32)
    mapsC = []
    for b in range(B):
        qT = catT("qT", b)
        gT = catT("gT", b)
        kcT, vcT, ksT, kwT = catT("kcT", b), catT("vcT", b), catT("ksT", b), catT("kwT", b)
        vs, vw = catR("vs", b), catR("vw", b)
        for g in range(N_KV):
            gr = slice(g * 128, (g + 1) * 128)
            grow = np.concatenate([np.arange(r_ * 32 + g * 8, r_ * 32 + g * 8 + 8) for r_ in range(3)])
            m = {"qT": np.ascontiguousarray(qT[g * 1024:(g + 1) * 1024]), "gT": np.ascontiguousarray(gT[grow]),
                 "kcT": np.ascontiguousarray(kcT[gr]), "vcT": np.ascontiguousarray(vcT[gr]),
                 "ksT": np.ascontiguousarray(ksT[gr]), "kwT": np.ascontiguousarray(kwT[gr]),
                 "vs": np.ascontiguousarray(vs[:, gr]), "vw": np.ascontiguousarray(vw[:, gr]),
                 "cmp_w1": w1, "cmp_w2": w2, "posT": posT, "kng": kngT, "qng": qngT}
            m.update(cC)
            mapsC.append(m)
    del rB
    rC = _run(ncC, mapsC)
    aT2 = [np.concatenate([np.asarray(rC[b * 4 + g]["oT"]) for g in range(4)], axis=0) for b in range(B)]
    del rC, mapsC

    ncD = _get("D", lambda: build_BD(T, D, D_FF, False))
    gainsD = np.ascontiguousarray(np.concatenate([_lay(np.asarray(mlp_norm_g, f32)[1])] * 3, axis=1))
    w_o1 = np.asarray(n_w_out, f32)[0]
    w_up1 = np.asarray(mlp_w_up, f32)[1]
    w_dn1 = np.asarray(mlp_w_down, f32)[1]
    mapsD = []
    for c in range(NCORES):
        b, r = c // 4, c % 4
        ts = slice(r * T, (r + 1) * T)
        mapsD.append({"xT": np.ascontiguousarray(x2T[b][:, ts]), "aT": np.ascontiguousarray(aT2[b][:, ts]),
                      "w_o": w_o1, "w_up": w_up1, "w_dn": w_dn1, "gains": gainsD})
    rD = _run(ncD, mapsD)
    out = np.empty((B, S, D), f32)
    for c in range(NCORES):
        b, r = c // 4, c % 4
        out[b, r * T:(r + 1) * T, :] = np.asarray(rD[c]["xoT"]).T
    return out
```

```python
import numpy as np
from contextlib import ExitStack
import ml_dtypes
import concourse.bass as bass
import concourse.mybir as mybir
from concourse.bass_utils import run_bass_kernel_spmd

F32 = mybir.dt.float32
BF16 = mybir.dt.bfloat16
ALU = mybir.AluOpType
AF = mybir.ActivationFunctionType
AX = mybir.AxisListType
NPBF = ml_dtypes.bfloat16

EPS = 1e-6
ENGS = ("pe", "act", "dve", "pool", "sp")


class Op:
    __slots__ = ("eng", "fn", "deps", "is_dma", "sem", "val", "signal", "ring", "slot")

    def __init__(self, eng, fn, deps, is_dma=False):
        self.eng = eng
        self.fn = fn
        self.deps = [d for d in deps if d is not None]
        self.is_dma = is_dma
        self.sem = None
        self.val = None
        self.signal = False
        self.ring = None
        self.slot = None


def X(name, *a, **kw):
    return (name, a, kw)


class Prog:
    def __init__(self, nc):
        self.nc = nc
        self.ops = {e: [] for e in ENGS}
        self.rings = {}

    def op(self, eng, fn, deps=()):
        o = Op(eng, fn, deps)
        self.ops[eng].append(o)
        return o

    def dma(self, q, out, in_, deps=(), ring="dflt", nslots=4, **kw):
        def fn(e, out=out, in_=in_, kw=kw):
            return e.dma_start(out=out, in_=in_, **kw)
        o = Op(q, fn, deps, is_dma=True)
        self.ops[q].append(o)
        r = self.rings.setdefault(ring, {"n": nslots, "count": 0, "uses": {}, "last": {}})
        s = r["count"] % r["n"]
        r["count"] += 1
        r["uses"][s] = r["uses"].get(s, 0) + 1
        if s in r["last"]:
            o.deps.append(r["last"][s])
        r["last"][s] = o
        o.ring = ring
        o.slot = s
        o.val = 16 * r["uses"][s]
        return o

    def wait(self, eng, deps):
        o = Op(eng, None, deps)
        self.ops[eng].append(o)
        return o

    def emit(self, ctx):
        nc = self.nc
        esem = {e: ctx.enter_context(nc.semaphore("s_" + e)) for e in ("pe", "act", "dve", "pool")}
        rsem = {}
        for name, r in self.rings.items():
            rsem[name] = [ctx.enter_context(nc.semaphore("r_%s_%d" % (name, i))) for i in range(len(r["uses"]))]
        for e in ENGS:
            for o in self.ops[e]:
                for d in o.deps:
                    if (not d.is_dma) and (d.eng != e or e != "pe"):
                        d.signal = True
        for e in ("pe", "act", "dve", "pool"):
            c = 0
            for o in self.ops[e]:
                if o.is_dma:
                    continue
                if o.signal:
                    c += 1
                    o.sem = esem[e]
                    o.val = c
        for e in ENGS:
            for o in self.ops[e]:
                if o.is_dma:
                    o.sem = rsem[o.ring][o.slot]
        blk = ctx.enter_context(nc.Block())

        def run(e, engobj):
            waited = {}
            for o in self.ops[e]:
                for d in o.deps:
                    if (not d.is_dma) and d.eng == e and e == "pe":
                        continue
                    key = id(d.sem)
                    if waited.get(key, 0) >= d.val:
                        continue
                    engobj.wait_ge(d.sem, d.val)
                    waited[key] = d.val
                if o.fn is None:
                    continue
                if isinstance(o.fn, tuple):
                    ins = getattr(engobj, o.fn[0])(*o.fn[1], **o.fn[2])
                else:
                    ins = o.fn(engobj)
                if o.is_dma:
                    ins.then_inc(o.sem, 16)
                elif o.signal:
                    ins.then_inc(o.sem, 1)

        @blk.tensor
        def _(eng):
            run("pe", eng)

        @blk.scalar
        def _(eng):
            run("act", eng)

        @blk.vector
        def _(eng):
            run("dve", eng)

        @blk.gpsimd
        def _(eng):
            run("pool", eng)

        @blk.sync
        def _(eng):
            run("sp", eng)


class Buf:
    def __init__(self, t):
        self.t = t
        self.ws = {}
        self.rs = {}
        self.dw = []
        self.dr = []

    @staticmethod
    def _add(d, lst, op):
        if op.is_dma:
            lst.append(op)
        else:
            d[op.eng] = op

    def wdeps(self):
        return list(self.ws.values()) + self.dw + list(self.rs.values()) + self.dr

    def rdeps(self):
        return list(self.ws.values()) + self.dw

    def wrote(self, op):
        self.ws = {}
        self.rs = {}
        self.dw = []
        self.dr = []
        self._add(self.ws, self.dw, op)

    def also_wrote(self, op):
        self._add(self.ws, self.dw, op)

    def read(self, op):
        self._add(self.rs, self.dr, op)


class Ring:
    def __init__(self, bufs):
        self.bufs = bufs
        self.i = 0

    def next(self):
        b = self.bufs[self.i % len(self.bufs)]
        self.i += 1
        return b


class WStream:
    def __init__(self, kb, nstg=2, elems=4096, name="stg"):
        self.kb = kb
        self.n = nstg
        self.elems = elems
        self.name = name
        self.ring = kb.sbring(name, nstg, [128, elems], F32)
        self.i = 0

    def load(self, dst_ap, src_ap, a, b, wd, dstbuf, first):
        p = self.kb.p
        stg = self.ring.next()
        view = stg.t[:, 0:a * b].rearrange("p (a b) -> p a b", a=a)
        d = p.dma("sp", view, src_ap, deps=stg.wdeps(), ring=self.name, nslots=self.n)
        stg.wrote(d)
        c = p.op("act", X("copy", out=dst_ap, in_=view), deps=[d] + list(wd))
        self.i += 1
        stg.read(c)
        if first:
            dstbuf.wrote(c)
        else:
            dstbuf.also_wrote(c)
        return c


class KB:
    def __init__(self, nc, ctx):
        self.nc = nc
        self.ctx = ctx
        self.p = Prog(nc)
        self.nps = 0

    def sb(self, name, shape, dt):
        return self.ctx.enter_context(self.nc.sbuf_tensor(name, list(shape), dt))

    def psum(self, name, shape, dt=F32):
        return self.ctx.enter_context(self.nc.psum_tensor(name, list(shape), dt))

    def sbring(self, name, n, shape, dt):
        return Ring([Buf(self.sb("%s%d" % (name, i), shape, dt)) for i in range(n)])

    def psring(self, name, n, shape=(128, 512), dt=F32):
        return Ring([Buf(self.psum("%s%d" % (name, i), shape, dt)) for i in range(n)])

    def din(self, name, shape, dt):
        return self.nc.dram_tensor(name, list(shape), dt, kind="ExternalInput").ap()

    def dout(self, name, shape, dt):
        return self.nc.dram_tensor(name, list(shape), dt, kind="ExternalOutput").ap()


def mm_group(kb, bank, pairs, extra_deps=(), out_ap=None):
    p = kb.p
    n = len(pairs)
    last = None
    out = bank.t[:] if out_ap is None else out_ap
    for i, (l, r, deps) in enumerate(pairs):
        d = list(deps)
        if i == 0:
            d += bank.wdeps() + list(extra_deps)
        last = p.op("pe", X("matmul", out, lhsT=l, rhs=r, start=(i == 0), stop=(i == n - 1)), deps=d)
    bank.wrote(last)
    return last


def rms_stats(kb, XT, KD, D, ones, sqring, ssbank, R, xt_ready):
    p = kb.p
    last = None
    for k in range(KD):
        sq = sqring.next()
        o = p.op("act", X("activation", out=sq.t[:], in_=XT.t[:, k, :], func=AF.Square), deps=sq.wdeps() + xt_ready)
        sq.wrote(o)
        d = [o]
        if k == 0:
            d += ssbank.wdeps()
        last = p.op("pe", X("matmul", ssbank.t[:], lhsT=ones[:], rhs=sq.t[:], start=(k == 0), stop=(k == KD - 1)), deps=d)
        sq.read(last)
    ssbank.wrote(last)
    o2 = rstd_from(kb, ssbank, R, 1.0 / D, [last])
    R.wrote(o2)
    return o2


def rstd_from(kb, bank, dst, inv_n, deps, eps_ap=None):
    p = kb.p
    o1 = p.op("act", X("activation", out=dst.t[:], in_=bank.t[:], func=AF.Sqrt, scale=inv_n, bias=kb.eps[:, 0:1]), deps=list(deps) + dst.wdeps() + [kb.eps_op])
    bank.read(o1)
    o2 = p.op("dve", X("reciprocal", out=dst.t[:], in_=dst.t[:]), deps=[o1])
    return o2


def build_BD(T, D, DFF, tail, HD=128, NKV=4, NQH=32, NG=96):
    nc = bass.Bass("TRN2", target_bir_lowering=False)
    KD = D // 128
    KF = DFF // 128
    NP = T // 512
    JR = 8
    NR = KF // JR
    with ExitStack() as ctx:
        kb = KB(nc, ctx)
        p = kb.p
        xT_d = kb.din("xT", [D, T], F32)
        aT_d = kb.din("aT", [D, T], BF16)
        w_o = kb.din("w_o", [D, D], F32)
        w_up = kb.din("w_up", [D, DFF], F32)
        w_dn = kb.din("w_dn", [DFF, D], F32)
        g_d = kb.din("gains", [128, 3 * KD], F32)
        xo_d = kb.dout("xoT", [D, T], F32)
        if tail:
            KVD = NKV * HD
            w_qg = kb.din("w_qg", [D, D + NG], F32)
            w_kv = kb.din("w_kv", [D, 6 * KVD], F32)
            kng_d = kb.din("kng", [128, 3], F32)
            qng_d = kb.din("qng", [128, 3], F32)
            qT_o = kb.dout("qT", [D, T], BF16)
            gT_o = kb.dout("gT", [NG, T], F32)
            kcT_o = kb.dout("kcT", [KVD, T], BF16)
            vcT_o = kb.dout("vcT", [KVD, T], BF16)
            ksT_o = kb.dout("ksT", [KVD, T], BF16)
            kwT_o = kb.dout("kwT", [KVD, T], BF16)
            vs_o = kb.dout("vs", [T, KVD], BF16)
            vw_o = kb.dout("vw", [T, KVD], BF16)

        XT = Buf(kb.sb("XT", [128, KD, 512], F32))
        XN = Buf(kb.sb("XN", [128, KD, 512], BF16))
        HT = kb.sbring("HT", 2, [128, JR, 512], BF16)
        wA = kb.sbring("wA", 3, [128, KD, 256], BF16)
        wB = kb.sbring("wB", 2, [128, JR, 512], BF16)
        ws = WStream(kb, 2, (JR // 2) * 512)
        sqr = kb.sbring("SQ", 3, [128, 512], BF16)
        sqf = kb.sbring("SQF", 2, [128, 512], F32)
        R = Buf(kb.sb("R", [128, 512], F32))
        gains = kb.sb("gains_sb", [128, 3 * KD], F32)
        ones = kb.sb("ones", [128, 128], BF16)
        psA = kb.psring("psA", 3)
        psD = kb.psring("psD", 4)
        psS = Buf(kb.psum("psS", [128, 512]))
        if tail:
            kgain = kb.sb("kgain_sb", [128, 2], F32)
            w0, w1 = wB.bufs[0].t, wB.bufs[1].t

            def f32view(t, j0):
                return t[:, j0:j0 + 2, :].bitcast(F32).rearrange("p a b -> p (a b)")
            Rq = Ring([Buf(f32view(w0, 0)), Buf(f32view(w0, 2))])
            GB = Buf(f32view(w0, 4))
            OB = Ring([Buf(w1[:, j, :]) for j in range(3)])
            tail_alias = [(wB.bufs[0], Rq.bufs + [GB]), (wB.bufs[1], OB.bufs)]

        c_ones = p.op("dve", X("memset", ones[:], 1.0))
        kb.eps = kb.sb("eps_sb", [128, 1], F32)
        kb.eps_op = p.op("dve", X("memset", kb.eps[:], EPS))
        d_g = p.dma("sp", gains[:], g_d[:, :], ring="misc", nslots=4)
        if tail:
            kng = kb.sb("kng_sb", [128, 3], F32)
            qng = kb.sb("qng_sb", [128, 3], F32)
            d_k1 = p.dma("sp", kng[:], kng_d[:, :], ring="misc", nslots=4)
            d_k2 = p.dma("sp", qng[:], qng_d[:, :], ring="misc", nslots=4)
            o_ = p.op("dve", X("tensor_tensor", out=kgain[:], in0=kng[:, 1:3], in1=qng[:, 1:3], op=ALU.mult), deps=[d_k1, d_k2])
            d_kg = p.op("dve", X("tensor_scalar", out=kgain[:], in0=kgain[:], scalar1=HD ** -0.5, scalar2=None, op0=ALU.mult), deps=[o_])

        w_o_v = w_o.rearrange("(k p) n -> p k n", p=128)
        w_up_v = w_up.rearrange("(k p) n -> p k n", p=128)
        w_dn_v = w_dn.rearrange("(j p) n -> p j n", p=128)
        xT_v = xT_d.rearrange("(k p) t -> p k t", p=128)
        aT_v = aT_d.rearrange("(k p) t -> p k t", p=128)
        xo_v = xo_d.rearrange("(k p) t -> p k t", p=128)

        def load_wA(src_v, c0, ncols=256):
            s = wA.next()
            o = p.dma("pool", s.t[:, :, 0:ncols], src_v[:, :, c0:c0 + ncols], deps=s.wdeps(), ring="wA", nslots=3)
            s.wrote(o)
            return s

        out_dmas = []
        for ps_ in range(NP):
            tsl = slice(ps_ * 512, (ps_ + 1) * 512)
            ld = []
            for q in range(4):
                ks = slice(q * KD // 4, (q + 1) * KD // 4)
                ld.append(p.dma("sp", XT.t[:, ks, :], xT_v[:, ks, tsl], deps=XT.wdeps(), ring="xin", nslots=4))
            XT.wrote(ld[0])
            for o_ in ld[1:]:
                XT.also_wrote(o_)
            xt_ready = list(ld)
            la = []
            for q in range(2):
                ks = slice(q * KD // 2, (q + 1) * KD // 2)
                la.append(p.dma("sp", XN.t[:, ks, :], aT_v[:, ks, tsl], deps=XN.wdeps(), ring="ain", nslots=2))
            XN.wrote(la[0])
            XN.also_wrote(la[1])
            xn_ready = list(la)
            adds = []
            for mg in range(KD // 2):
                s = load_wA(w_o_v, mg * 256)
                for mi in range(2):
                    m = 2 * mg + mi
                    bank = psA.next()
                    mm = mm_group(kb, bank, [(s.t[:, k, mi * 128:(mi + 1) * 128], XN.t[:, k, :], (s.rdeps() + xn_ready + [c_ones]) if k == 0 else []) for k in range(KD)])
                    s.read(mm)
                    XN.read(mm)
                    o = p.op("dve", X("tensor_tensor", out=XT.t[:, m, :], in0=XT.t[:, m, :], in1=bank.t[:], op=ALU.add), deps=[mm] + xt_ready)
                    bank.read(o)
                    adds.append(o)
            XT.wrote(adds[-1])

            def norm_into_XN(goff, xt_dep):
                r_op = rms_stats(kb, XT, KD, D, ones, sqr, psS, R, xt_dep)
                last = None
                for k in range(KD):
                    eng = "dve"
                    last_ = p.op(eng, X("scalar_tensor_tensor", out=XN.t[:, k, :], in0=XT.t[:, k, :], scalar=gains[:, goff + k:goff + k + 1], in1=R.t[:], op0=ALU.mult, op1=ALU.mult),
                                 deps=[r_op, d_g] + XN.wdeps() + xt_dep)
                    last = last_
                R.read(last)
                return [last, last]

            xn_ops = norm_into_XN(0, [adds[-1]])
            XN.wrote(xn_ops[0])
            xn_ready = list(xn_ops)

            hts = [None] * NR
            dn_adds = [adds[-1]]

            def up_round(r):
                ht = HT.next()
                evs = []
                for jg in range(JR // 2):
                    s = load_wA(w_up_v, (r * JR + 2 * jg) * 128)
                    for ji in range(2):
                        jl = 2 * jg + ji
                        bank = psA.next()
                        mm = mm_group(kb, bank, [(s.t[:, k, ji * 128:(ji + 1) * 128], XN.t[:, k, :], (s.rdeps() + xn_ready) if k == 0 else []) for k in range(KD)])
                        s.read(mm)
                        XN.read(mm)
                        sf = sqf.next()
                        o0 = p.op("act", X("activation", out=sf.t[:], in_=bank.t[:], func=AF.Square), deps=[mm] + sf.wdeps())
                        sf.wrote(o0)
                        o = p.op("dve", X("scalar_tensor_tensor", out=ht.t[:, jl, :], in0=bank.t[:], scalar=0.0, in1=sf.t[:], op0=ALU.is_gt, op1=ALU.mult),
                                 deps=[mm, o0] + ht.wdeps())
                        sf.read(o)
                        bank.read(o0)
                        bank.read(o)
                        evs.append(o)
                ht.wrote(evs[-1])
                hts[r] = ht

            def down_round(r):
                ht = hts[r]
                for mg in range(KD // 4):
                    s = wB.next()
                    wd_ = s.wdeps()
                    jh = JR // 2
                    ws.load(s.t[:, 0:jh, :], w_dn_v[:, r * JR:r * JR + jh, mg * 512:(mg + 1) * 512], jh, 512, wd_, s, True)
                    ws.load(s.t[:, jh:JR, :], w_dn_v[:, r * JR + jh:(r + 1) * JR, mg * 512:(mg + 1) * 512], jh, 512, wd_, s, False)
                    banks = [psD.next() for _ in range(4)]
                    last = None
                    for jl in range(JR):
                        for mi in range(4):
                            d = []
                            if jl == 0:
                                d = banks[mi].wdeps()
                                if mi == 0:
                                    d = d + s.rdeps() + ht.rdeps()
                            last = p.op("pe", X("matmul", banks[mi].t[:], lhsT=s.t[:, jl, mi * 128:(mi + 1) * 128], rhs=ht.t[:, jl, :], start=(jl == 0), stop=(jl == JR - 1)), deps=d)
                    s.read(last)
                    ht.read(last)
                    for mi in range(4):
                        banks[mi].wrote(last)
                        m = 4 * mg + mi
                        eng = "dve" if mi % 2 == 0 else "pool"
                        eng = "dve"
                        o = p.op(eng, X("tensor_tensor", out=XT.t[:, m, :], in0=XT.t[:, m, :], in1=banks[mi].t[:], op=ALU.add), deps=[last] + dn_adds[-1:])
                        banks[mi].read(o)
                        dn_adds.append(o)

            for r in range(NR + 1):
                if r < NR:
                    up_round(r)
                if r >= 1:
                    down_round(r - 1)
            XT.wrote(dn_adds[-1])
            xt_ready = [dn_adds[-1]]
            XN.read(dn_adds[-1])
            for q in range(4):
                ks = slice(q * KD // 4, (q + 1) * KD // 4)
                o = p.dma("sp", xo_v[:, ks, tsl], XT.t[:, ks, :], deps=xt_ready, ring="xout", nslots=4)
                XT.read(o)
                out_dmas.append(o)

            if tail:
                for wslot, als in tail_alias:
                    for al in als:
                        for op_ in wslot.wdeps():
                            al.read(op_)
                KVD = NKV * HD
                w_qg_v = w_qg.rearrange("(k p) n -> p k n", p=128)
                w_kv_v = w_kv.rearrange("(k p) n -> p k n", p=128)
                xn_ops = norm_into_XN(KD, xt_ready)
                XN.wrote(xn_ops[0])
                xn_ready = list(xn_ops)
                pend = None

                def finish_q(item):
                    bank, sq, hd, o_sq = item
                    mm2 = mm_group(kb, psS, [(ones[:], sq.t[:], [o_sq])])
                    sq.read(mm2)
                    rq = Rq.next()
                    o2 = rstd_from(kb, psS, rq, 1.0 / HD, [mm2])
                    ob = OB.next()
                    o3 = p.op("dve", X("tensor_tensor", out=ob.t[:], in0=bank.t[:], in1=rq.t[:], op=ALU.mult), deps=[o2] + ob.wdeps())
                    rq.wrote(o3)
                    bank.read(o3)
                    ob.wrote(o3)
                    o4 = p.dma("sp", qT_o[hd * 128:(hd + 1) * 128, tsl], ob.t[:], deps=[o3], ring="oq", nslots=3)
                    ob.read(o4)
                    out_dmas.append(o4)

                for hg in range(NQH // 2):
                    s = load_wA(w_qg_v, hg * 256)
                    for hi in range(2):
                        hd = 2 * hg + hi
                        bank = psA.next()
                        mm = mm_group(kb, bank, [(s.t[:, k, hi * 128:(hi + 1) * 128], XN.t[:, k, :], (s.rdeps() + xn_ready) if k == 0 else []) for k in range(KD)])
                        s.read(mm)
                        XN.read(mm)
                        sq = sqr.next()
                        o_sq = p.op("act", X("activation", out=sq.t[:], in_=bank.t[:], func=AF.Square), deps=[mm] + sq.wdeps())
                        sq.wrote(o_sq)
                        bank.read(o_sq)
                        if pend is not None:
                            finish_q(pend)
                        pend = (bank, sq, hd, o_sq)
                s = load_wA(w_qg_v, D, NG)
                bank = psA.next()
                mm = p_last = None
                n = KD
                d0 = bank.wdeps() + s.rdeps() + xn_ready
                for k in range(KD):
                    mm = p.op("pe", X("matmul", bank.t[0:NG, :], lhsT=s.t[:, k, 0:NG], rhs=XN.t[:, k, :], start=(k == 0), stop=(k == n - 1)), deps=d0 if k == 0 else [])
                bank.wrote(mm)
                s.read(mm)
                XN.read(mm)
                finish_q(pend)
                o = p.op("act", X("activation", out=GB.t[0:NG, :], in_=bank.t[0:NG, :], func=AF.Sigmoid), deps=[mm] + GB.wdeps())
                bank.read(o)
                GB.wrote(o)
                o2 = p.dma("sp", gT_o[:, tsl], GB.t[0:NG, :], deps=[o], ring="og", nslots=1)
                GB.read(o2)
                out_dmas.append(o2)
                last = None
                for k in range(KD):
                    eng = "dve"
                    o = p.op(eng, X("scalar_tensor_tensor", out=XN.t[:, k, :], in0=XT.t[:, k, :], scalar=gains[:, 2 * KD + k:2 * KD + k + 1], in1=R.t[:], op0=ALU.mult, op1=ALU.mult),
                             deps=XN.wdeps() + xt_ready + R.rdeps())
                    last = o
                XN.wrote(last)
                R.read(last)
                xn_ready = [last]
                pend = None

                def finish_k(item):
                    bank, sq, o_sq, dst, gi, row0 = item
                    mm2 = mm_group(kb, psS, [(ones[:], sq.t[:], [o_sq])])
                    sq.read(mm2)
                    rq = Rq.next()
                    o2 = rstd_from(kb, psS, rq, 1.0 / HD, [mm2])
                    ob = OB.next()
                    o3 = p.op("dve", X("scalar_tensor_tensor", out=ob.t[:], in0=bank.t[:], scalar=kgain[:, gi:gi + 1], in1=rq.t[:], op0=ALU.mult, op1=ALU.mult), deps=[o2, d_kg] + ob.wdeps())
                    rq.wrote(o3)
                    bank.read(o3)
                    ob.wrote(o3)
                    o4 = p.dma("sp", dst[row0:row0 + 128, tsl], ob.t[:], deps=[o3], ring="oq", nslots=3)
                    ob.read(o4)
                    out_dmas.append(o4)

                for (j, dst, gi) in ((0, kcT_o, None), (1, vcT_o, None), (2, ksT_o, 0), (4, kwT_o, 1)):
                    for gg in range(NKV // 2):
                        s = load_wA(w_kv_v, j * KVD + gg * 256)
                        for gi2 in range(2):
                            g = 2 * gg + gi2
                            bank = psA.next()
                            mm = mm_group(kb, bank, [(s.t[:, k, gi2 * 128:(gi2 + 1) * 128], XN.t[:, k, :], (s.rdeps() + xn_ready) if k == 0 else []) for k in range(KD)])
                            s.read(mm)
                            XN.read(mm)
                            if gi is None:
                                ob = OB.next()
                                o3 = p.op("act", X("copy", out=ob.t[:], in_=bank.t[:]), deps=[mm] + ob.wdeps())
                                bank.read(o3)
                                ob.wrote(o3)
                                o4 = p.dma("sp", dst[g * 128:(g + 1) * 128, tsl], ob.t[:], deps=[o3], ring="oq", nslots=3)
                                ob.read(o4)
                                out_dmas.append(o4)
                            else:
                                sq = sqr.next()
                                o_sq = p.op("act", X("activation", out=sq.t[:], in_=bank.t[:], func=AF.Square), deps=[mm] + sq.wdeps())
                                sq.wrote(o_sq)
                                bank.read(o_sq)
                                if pend is not None:
                                    finish_k(pend)
                                pend = (bank, sq, o_sq, dst, gi, g * 128)
                if pend is not None:
                    finish_k(pend)
                for (j, dst) in ((3, vs_o), (5, vw_o)):
                    for half in range(KVD // 256):
                        s = load_wA(w_kv_v, j * KVD + half * 256)
                        for tt in range(4):
                            bank = psA.next()
                            mm = mm_group(kb, bank, [(XN.t[:, k, tt * 128:(tt + 1) * 128], s.t[:, k, :], (s.rdeps() + xn_ready) if k == 0 else []) for k in range(KD)], out_ap=bank.t[:, 0:256])
                            s.read(mm)
                            XN.read(mm)
                            ob = OB.next()
                            o3 = p.op("act", X("copy", out=ob.t[:, 0:256], in_=bank.t[:, 0:256]), deps=[mm] + ob.wdeps())
                            bank.read(o3)
                            ob.wrote(o3)
                            o4 = p.dma("sp", dst[ps_ * 512 + tt * 128:ps_ * 512 + (tt + 1) * 128, half * 256:(half + 1) * 256], ob.t[:, 0:256], deps=[o3], ring="oq", nslots=3)
                            ob.read(o4)
                            out_dmas.append(o4)
            if tail:
                for wslot, als in tail_alias:
                    for al in als:
                        for op_ in al.wdeps():
                            wslot.read(op_)
        p.wait("sp", out_dmas)
        p.emit(ctx)
    return nc


def build_A(S, D, DK=256, DV=512, GATE_CAP=15.0):
    nc = bass.Bass("TRN2", target_bir_lowering=False)
    KD = D // 128
    NB = S // 512
    NH = 2
    DKC = DK // 128
    DVC = DV // 128
    QC = NH * DKC
    OC = NH * DVC
    with ExitStack() as ctx:
        kb = KB(nc, ctx)
        p = kb.p
        xT_d = kb.din("xT", [D, S], F32)
        ga_d = kb.din("ga", [128, KD], F32)
        w_q = kb.din("w_q", [D, NH * DK], F32)
        w_k = kb.din("w_k", [D, NH * DK], F32)
        w_v = kb.din("w_v", [D, NH * DV], F32)
        w_og = kb.din("w_og", [D, NH * DV], F32)
        w_g = kb.din("w_g", [D, 4], F32)
        bg_d = kb.din("bg", [2, 2], F32)
        hg_d = kb.din("hgain", [128, OC], F32)
        cc_d = kb.din("c_causal", [128, 64], F32)
        cs_d = kb.din("c_seg", [2, 1024], F32)
        csel_d = kb.din("c_sel", [2, 256], F32)
        ci2_d = kb.din("c_i2", [2, 2], F32)
        out_d = kb.dout("hgT", [NH * DV, S], BF16)

        XTr = kb.sbring("XTr", 2, [128, 4, 512], F32)
        XN = Buf(kb.sb("XN", [128, KD, 512], BF16))
        wA = kb.sbring("wA", 3, [128, KD, 256], BF16)
        sqr = kb.sbring("SQ", 3, [128, 512], BF16)
        R = Buf(kb.sb("R", [128, 512], F32))
        ga = kb.sb("ga_sb", [128, KD], F32)
        hgain = kb.sb("hgain_sb", [128, OC], F32)
        ones = kb.sb("ones", [128, 128], BF16)
        causal = kb.sb("causal", [128, 64], F32)
        cseg = kb.sb("cseg", [2, 512], F32)
        csel = kb.sb("csel", [2, 256], F32)
        ci2 = kb.sb("ci2", [2, 2], F32)
        bg = kb.sb("bg_sb", [2, 2], F32)
        bg15 = kb.sb("bg15", [2, 2], F32)
        one1 = kb.sb("one1", [2, 1], F32)
        QT = Buf(kb.sb("QT", [128, QC, 512], BF16))
        KT = Buf(kb.sb("KT", [128, QC, 512], BF16))
        KG = Buf(kb.sb("KG", [128, 4, NH, DK], BF16))
        VT = Buf(kb.sb("VT", [128, 4, NH, DV], BF16))
        OG = Buf(kb.sb("OG", [128, OC, 512], BF16))
        HG = Buf(kb.sb("HG", [128, OC, 512], BF16))
        OT = kb.sbring("OT", 2, [128, 512], F32)
        C32 = [Buf(kb.sb("C32_%d" % i, [128, 512], F32)) for i in range(QC)]
        Cd = [Buf(kb.sb("Cd_%d" % i, [128, 512], F32)) for i in range(QC)]
        Cb = [Buf(kb.sb("Cb_%d" % i, [128, 512], BF16)) for i in range(QC)]
        n32 = [Buf(kb.sb("n32_%d" % i, [128, 2], F32)) for i in range(QC)]
        nrep = [Buf(kb.sb("nrep_%d" % i, [128, 128], BF16)) for i in range(QC)]
        EMWb = Buf(kb.sb("EMWb", [128, NH, 512], F32))
        G128 = Buf(kb.sb("G128", [128, 4, NH], F32))
        SC = Buf(kb.sb("SCk", [128, 4, NH], F32))
        DECb = Buf(kb.sb("DECb", [128, 8, NH], F32))
        rcol = Buf(kb.sb("rcol", [128, 8], F32))
        SW = kb.sbring("SW", 3, [128, 64], BF16)
        DN = kb.sbring("DN", 3, [128, 64], F32)
        HTt = kb.sbring("HTt", 3, [128, DVC, 64], F32)
        HSQ = kb.sbring("HSQ", 3, [128, DVC, 64], BF16)
        RS = kb.sbring("RS", 3, [128, 64], F32)
        GV = [Buf(kb.sb("GV%d" % i, [2, 512], F32)) for i in range(6)]
        gs = Buf(kb.sb("gsmall", [2, 64], F32))
        mprev = Buf(kb.sb("mprev", [2, 1], F32))

        psA = kb.psring("psA", 3)
        psS = Buf(kb.psum("psS", [128, 512]))
        psc = kb.psring("psc", 2)
        psC = kb.psring("psC", 2)

        c_ones = p.op("dve", X("memset", ones[:], 1.0))
        kb.eps = kb.sb("eps_sb", [128, 1], F32)
        kb.eps_op = p.op("dve", X("memset", kb.eps[:], EPS))
        c_one1 = p.op("dve", X("memset", one1[:], 1.0))
        c_mp = p.op("dve", X("memset", mprev.t[:], 0.0))
        mprev.wrote(c_mp)
        cinit = []
        for i in range(QC):
            o = p.op("pool", X("memset", C32[i].t[:], 0.0))
            C32[i].wrote(o)
            o = p.op("pool", X("memset", Cb[i].t[:], 0.0))
            Cb[i].wrote(o)
            o = p.op("pool", X("memset", n32[i].t[:], 0.0))
            n32[i].wrote(o)
            o = p.op("pool", X("memset", nrep[i].t[:], 0.0))
            nrep[i].wrote(o)
        dc = [p.dma("sp", ga[:], ga_d[:, :], ring="misc", nslots=8),
              p.dma("sp", hgain[:], hg_d[:, :], ring="misc", nslots=8),
              p.dma("sp", causal[:], cc_d[:, :], ring="misc", nslots=8),
              p.dma("sp", cseg[:], cs_d[:, 0:512], ring="misc", nslots=8),
              p.dma("sp", csel[:], csel_d[:, :], ring="misc", nslots=8),
              p.dma("sp", ci2[:], ci2_d[:, :], ring="misc", nslots=8),
              p.dma("sp", bg[:], bg_d[:, :], ring="misc", nslots=8)]
        c_bg15 = p.op("dve", X("tensor_scalar", out=bg15[:], in0=bg[:], scalar1=1.0 / GATE_CAP, scalar2=None, op0=ALU.mult), deps=[dc[6]])

        xT_v = xT_d.rearrange("(k p) t -> p k t", p=128)
        wv = {n: a.rearrange("(k p) n -> p k n", p=128) for n, a in (("q", w_q), ("k", w_k), ("v", w_v), ("o", w_og), ("g", w_g))}
        out_v = out_d.rearrange("(c p) t -> p c t", p=128)

        def load_wA(name, c0, ncols=256):
            s = wA.next()
            o = p.dma("pool", s.t[:, :, 0:ncols], wv[name][:, :, c0:c0 + ncols], deps=s.wdeps(), ring="wA", nslots=3)
            s.wrote(o)
            return s

        def proj_fm(s, c0, ncols=128, M=None):
            bank = psA.next()
            d0 = bank.wdeps() + s.rdeps() + XN.rdeps() + [c_ones]
            mm = None
            for k in range(KD):
                mm = p.op("pe", X("matmul", bank.t[0:ncols, :], lhsT=s.t[:, k, c0:c0 + ncols], rhs=XN.t[:, k, :], start=(k == 0), stop=(k == KD - 1)), deps=d0 if k == 0 else [])
            bank.wrote(mm)
            s.read(mm)
            XN.read(mm)
            return bank, mm

        def proj_tm(s, tt):
            bank = psA.next()
            d0 = bank.wdeps() + s.rdeps() + XN.rdeps()
            mm = None
            for k in range(KD):
                mm = p.op("pe", X("matmul", bank.t[:, 0:256], lhsT=XN.t[:, k, tt * 128:(tt + 1) * 128], rhs=s.t[:, k, :], start=(k == 0), stop=(k == KD - 1)), deps=d0 if k == 0 else [])
            bank.wrote(mm)
            s.read(mm)
            XN.read(mm)
            return bank, mm

        out_dmas = []
        pend = [None]

        for bi in range(NB):
            tsl = slice(bi * 512, (bi + 1) * 512)
            sq_ops = []
            first = True
            mm = None
            for grp in range(KD // 4):
                xs = XTr.next()
                ld = p.dma("sp", xs.t[:], xT_v[:, 4 * grp:4 * grp + 4, tsl], deps=xs.wdeps(), ring="xin", nslots=2)
                xs.wrote(ld)
                for kk in range(4):
                    k = 4 * grp + kk
                    sq = sqr.next()
                    o = p.op("act", X("activation", out=sq.t[:], in_=xs.t[:, kk, :], func=AF.Square), deps=sq.wdeps() + [ld])
                    sq.wrote(o)
                    xs.read(o)
                    d = [o, c_ones]
                    if k == 0:
                        d += psS.wdeps()
                    mm = p.op("pe", X("matmul", psS.t[:], lhsT=ones[:], rhs=sq.t[:], start=(k == 0), stop=(k == KD - 1)), deps=d)
                    sq.read(mm)
                    o2 = p.op("dve", X("tensor_scalar", out=XN.t[:, k, :], in0=xs.t[:, kk, :], scalar1=ga[:, k:k + 1], scalar2=None, op0=ALU.mult), deps=[ld, dc[0]] + XN.wdeps())
                    xs.read(o2)
                    if first:
                        XN.wrote(o2)
                        first = False
                    else:
                        XN.also_wrote(o2)
            psS.wrote(mm)
            r_op = rstd_from(kb, psS, R, 1.0 / D, [mm])
            R.wrote(r_op)
            bank = psA.next()
            d0 = bank.wdeps() + [r_op, c_one1]
            for tt in range(4):
                mm = p.op("pe", X("matmul", bank.t[:, tt:tt + 1], lhsT=R.t[0:1, tt * 128:(tt + 1) * 128], rhs=one1[0:1, 0:1], start=True, stop=True), deps=d0 if tt == 0 else [])
            bank.wrote(mm)
            R.read(mm)
            o = p.op("dve", X("tensor_copy", out=rcol.t[:, 0:4], in_=bank.t[:, 0:4]), deps=[mm] + rcol.wdeps())
            bank.read(o)
            o = p.op("dve", X("tensor_scalar", out=rcol.t[:, 4:8], in0=rcol.t[:, 0:4], scalar1=DK ** -0.5, scalar2=None, op0=ALU.mult), deps=[o])
            rcol.wrote(o)

            s = load_wA("g", 0, 4)
            bank_i, mm_i = proj_fm(s, 0, 2)
            bank_f, mm_f = proj_fm(s, 2, 2)
            V = GV
            o = p.op("dve", X("tensor_tensor", out=V[0].t[:], in0=bank_i.t[0:2, :], in1=R.t[0:2, :], op=ALU.mult), deps=[mm_i, r_op] + V[0].wdeps())
            bank_i.read(o)
            o = p.op("act", X("activation", out=V[0].t[:], in_=V[0].t[:], func=AF.Tanh, scale=1.0 / GATE_CAP, bias=bg15[:, 0:1]), deps=[o, c_bg15])
            o_li = p.op("dve", X("tensor_scalar", out=V[0].t[:], in0=V[0].t[:], scalar1=GATE_CAP, scalar2=None, op0=ALU.mult), deps=[o])
            V[0].wrote(o_li)
            o = p.op("dve", X("tensor_tensor", out=V[1].t[:], in0=bank_f.t[0:2, :], in1=R.t[0:2, :], op=ALU.mult), deps=[mm_f, r_op] + V[1].wdeps())
            bank_f.read(o)
            R.read(o)
            o = p.op("act", X("activation", out=V[1].t[:], in_=V[1].t[:], func=AF.Tanh, scale=1.0 / GATE_CAP, bias=bg15[:, 1:2]), deps=[o, c_bg15])
            o = p.op("act", X("activation", out=V[1].t[:], in_=V[1].t[:], func=AF.Exp, scale=-GATE_CAP), deps=[o])
            o = p.op("act", X("activation", out=V[1].t[:], in_=V[1].t[:], func=AF.Ln, bias=one1[:, 0:1]), deps=[o, c_one1])
            o_lf = p.op("dve", X("tensor_scalar", out=V[1].t[:], in0=V[1].t[:], scalar1=-1.0, scalar2=None, op0=ALU.mult), deps=[o])
            V[1].wrote(o_lf)
            o_b = p.op("dve", X("tensor_tensor_scan", out=V[2].t[:], data0=cseg[:, 0:512], data1=V[1].t[:], initial=0.0, op0=ALU.mult, op1=ALU.add), deps=[o_lf, dc[3]] + V[2].wdeps())
            V[2].wrote(o_b)
            o_a = p.op("dve", X("tensor_tensor", out=V[3].t[:], in0=V[0].t[:], in1=V[2].t[:], op=ALU.subtract), deps=[o_li, o_b] + V[3].wdeps())
            V[3].wrote(o_a)
            b3 = V[2].t[:].rearrange("p (c s) -> p c s", s=64)
            a3 = V[3].t[:].rearrange("p (c s) -> p c s", s=64)
            o = p.op("dve", X("tensor_copy", out=gs.t[:, 0:8], in_=b3[:, :, 63]), deps=[o_b] + gs.wdeps())
            o = p.op("dve", X("tensor_reduce", out=gs.t[:, 8:16], in_=a3, axis=AX.X, op=ALU.max), deps=[o_a])
            o = p.op("dve", X("tensor_tensor", out=gs.t[:, 8:16], in0=gs.t[:, 8:16], in1=gs.t[:, 0:8], op=ALU.add), deps=[o])
            o = p.op("dve", X("tensor_tensor_scan", out=gs.t[:, 16:24], data0=gs.t[:, 0:8], data1=gs.t[:, 8:16], initial=mprev.t[:, 0:1], op0=ALU.add, op1=ALU.max), deps=[o] + mprev.rdeps())
            o = p.op("dve", X("tensor_copy", out=gs.t[:, 24:25], in_=mprev.t[:, 0:1]), deps=[o])
            o = p.op("dve", X("tensor_copy", out=gs.t[:, 25:32], in_=gs.t[:, 16:23]), deps=[o])
            o_mp = p.op("dve", X("tensor_copy", out=mprev.t[:, 0:1], in_=gs.t[:, 23:24]), deps=[o])
            mprev.wrote(o_mp)
            o = p.op("dve", X("tensor_tensor", out=gs.t[:, 32:40], in0=gs.t[:, 0:8], in1=gs.t[:, 24:32], op=ALU.add), deps=[o_mp])
            o = p.op("dve", X("tensor_tensor", out=gs.t[:, 32:40], in0=gs.t[:, 32:40], in1=gs.t[:, 16:24], op=ALU.subtract), deps=[o])
            o_dec = p.op("act", X("activation", out=gs.t[:, 32:40], in_=gs.t[:, 32:40], func=AF.Exp), deps=[o])
            gs.wrote(o_dec)
            mcur_b = gs.t[:, 24:32].unsqueeze(2).to_broadcast([2, 8, 64])
            g3 = V[4].t[:].rearrange("p (c s) -> p c s", s=64)
            o = p.op("dve", X("tensor_tensor", out=g3, in0=a3, in1=mcur_b, op=ALU.subtract), deps=[o_a, o_mp] + V[4].wdeps())
            o_g = p.op("act", X("activation", out=V[4].t[:], in_=V[4].t[:], func=AF.Exp), deps=[o])
            V[4].wrote(o_g)
            e3 = V[5].t[:].rearrange("p (c s) -> p c s", s=64)
            o = p.op("dve", X("tensor_tensor", out=e3, in0=b3, in1=mcur_b, op=ALU.add), deps=[o_b, o_mp] + V[5].wdeps())
            o_e = p.op("act", X("activation", out=V[5].t[:], in_=V[5].t[:], func=AF.Exp, scale=-1.0), deps=[o])
            V[5].wrote(o_e)
            bank = psA.next()
            d0 = bank.wdeps() + [o_g, dc[5]]
            for tt in range(4):
                mm = p.op("pe", X("matmul", bank.t[:, 2 * tt:2 * tt + 2], lhsT=V[4].t[0:2, tt * 128:(tt + 1) * 128], rhs=ci2[0:2, 0:2], start=True, stop=True), deps=d0 if tt == 0 else [])
            V[4].read(mm)
            for h in range(NH):
                mm = p.op("pe", X("matmul", bank.t[:, 16 + 8 * h:24 + 8 * h], lhsT=csel[0:2, h * 128:(h + 1) * 128], rhs=gs.t[0:2, 32:40], start=True, stop=True), deps=[o_dec, dc[4]])
            gs.read(mm)
            bank.wrote(mm)
            o = p.op("act", X("copy", out=G128.t[:].rearrange("p t h -> p (t h)"), in_=bank.t[:, 0:8]), deps=[mm] + G128.wdeps())
            G128.wrote(o)
            o2 = p.op("act", X("copy", out=DECb.t[:].rearrange("p c h -> p h c"), in_=bank.t[:, 16:32].rearrange("p (h c) -> p h c", h=NH)), deps=[mm] + DECb.wdeps())
            DECb.wrote(o2)
            bank.read(o2)
            o = p.op("dve", X("tensor_tensor", out=SC.t[:], in0=G128.t[:], in1=rcol.t[:, 4:8].unsqueeze(2).to_broadcast([128, 4, NH]), op=ALU.mult), deps=[o] + rcol.rdeps() + SC.wdeps())
            SC.wrote(o)
            G128.read(o)
            first = True
            for h in range(NH):
                bank = psA.next()
                mm = p.op("pe", X("matmul", bank.t[:], lhsT=csel[0:2, h * 128:(h + 1) * 128], rhs=V[5].t[0:2, :], start=True, stop=True), deps=bank.wdeps() + [o_e, dc[4]])
                bank.wrote(mm)
                V[5].read(mm)
                o = p.op("act", X("copy", out=EMWb.t[:, h, :], in_=bank.t[:]), deps=[mm] + EMWb.wdeps())
                bank.read(o)
                if first:
                    EMWb.wrote(o)
                    first = False
                else:
                    EMWb.also_wrote(o)

            fq = fk = fkg = fv = fo = True
            for h in range(NH):
                s = load_wA("q", h * 256)
                for dci in range(DKC):
                    bank, mm = proj_fm(s, dci * 128)
                    o = p.op("dve", X("tensor_tensor", out=QT.t[:, h * DKC + dci, :], in0=bank.t[:], in1=R.t[:], op=ALU.mult), deps=[mm, r_op] + QT.wdeps())
                    bank.read(o)
                    R.read(o)
                    QT.wrote(o) if fq else QT.also_wrote(o)
                    fq = False
            for h in range(NH):
                s = load_wA("k", h * 256)
                for dci in range(DKC):
                    bank, mm = proj_fm(s, dci * 128)
                    o = p.op("dve", X("scalar_tensor_tensor", out=KT.t[:, h * DKC + dci, :], in0=bank.t[:], scalar=DK ** -0.5, in1=R.t[:], op0=ALU.mult, op1=ALU.mult), deps=[mm, r_op] + KT.wdeps())
                    bank.read(o)
                    R.read(o)
                    KT.wrote(o) if fk else KT.also_wrote(o)
                    fk = False
                for tt in range(4):
                    bank, mm = proj_tm(s, tt)
                    o = p.op("act", X("activation", out=KG.t[:, tt, h, :], in_=bank.t[:, 0:256], func=AF.Copy, scale=SC.t[:, tt, h:h + 1]), deps=[mm] + SC.rdeps() + KG.wdeps())
                    bank.read(o)
                    SC.read(o)
                    KG.wrote(o) if fkg else KG.also_wrote(o)
                    fkg = False
            for h in range(NH):
                for half in range(DV // 256):
                    s = load_wA("v", h * DV + half * 256)
                    for tt in range(4):
                        bank, mm = proj_tm(s, tt)
                        o = p.op("act", X("activation", out=VT.t[:, tt, h, half * 256:(half + 1) * 256], in_=bank.t[:, 0:256], func=AF.Copy, scale=rcol.t[:, tt:tt + 1]), deps=[mm] + rcol.rdeps() + VT.wdeps())
                        bank.read(o)
                        rcol.read(o)
                        VT.wrote(o) if fv else VT.also_wrote(o)
                        fv = False
            for h in range(NH):
                for half in range(DV // 256):
                    s = load_wA("o", h * DV + half * 256)
                    for ci in range(2):
                        ec = half * 2 + ci
                        bank, mm = proj_fm(s, ci * 128)
                        ot = OT.next()
                        o1 = p.op("dve", X("tensor_tensor", out=ot.t[:], in0=bank.t[:], in1=R.t[:], op=ALU.mult), deps=[mm, r_op] + ot.wdeps())
                        bank.read(o1)
                        R.read(o1)
                        o2 = p.op("act", X("activation", out=ot.t[:], in_=ot.t[:], func=AF.Sigmoid), deps=[o1])
                        o3 = p.op("pool", X("tensor_scalar", out=OG.t[:, h * DVC + ec, :], in0=ot.t[:], scalar1=hgain[:, h * DVC + ec:h * DVC + ec + 1], scalar2=None, op0=ALU.mult), deps=[o2, dc[1]] + OG.wdeps())
                        ot.wrote(o3)
                        OG.wrote(o3) if fo else OG.also_wrote(o3)
                        fo = False

            fhg = [True]

            def finish(item):
                (pc, ht, h, c, o_ht) = item
                hs = HSQ.next()
                o_sq = p.op("act", X("activation", out=hs.t[:], in_=ht.t[:], func=AF.Square), deps=[o_ht] + hs.wdeps())
                hs.wrote(o_sq)
                mm = None
                for ec in range(DVC):
                    mm = p.op("pe", X("matmul", pc.t[:, 384:448], lhsT=ones[:], rhs=hs.t[:, ec, :], start=(ec == 0), stop=(ec == DVC - 1)), deps=([o_sq] + pc.wdeps()) if ec == 0 else [])
                hs.read(mm)
                pc.also_wrote(mm)
                rs = RS.next()
                o1 = p.op("act", X("activation", out=rs.t[:], in_=pc.t[:, 384:448], func=AF.Sqrt, scale=1.0 / DV, bias=kb.eps[:, 0:1]), deps=[mm, kb.eps_op] + rs.wdeps())
                pc.read(o1)
                o2 = p.op("dve", X("reciprocal", out=rs.t[:], in_=rs.t[:]), deps=[o1])
                o3 = p.op("dve", X("tensor_tensor", out=ht.t[:], in0=ht.t[:], in1=rs.t[:].unsqueeze(1).to_broadcast([128, DVC, 64]), op=ALU.mult), deps=[o2, o_sq])
                rs.wrote(o3)
                o4 = p.op("dve", X("tensor_tensor", out=HG.t[:, h * DVC:(h + 1) * DVC, c * 64:(c + 1) * 64], in0=ht.t[:], in1=OG.t[:, h * DVC:(h + 1) * DVC, c * 64:(c + 1) * 64], op=ALU.mult), deps=[o3] + OG.rdeps() + HG.wdeps())
                ht.wrote(o4)
                OG.read(o4)
                HG.wrote(o4) if fhg[0] else HG.also_wrote(o4)
                fhg[0] = False

            for c in range(8):
                tt = c // 2
                pb = 64 * (c % 2)
                csl = slice(c * 64, (c + 1) * 64)
                tl = slice(tt * 128, (tt + 1) * 128)
                for h in range(NH):
                    pc = psc.next()
                    qs = [h * DKC + i for i in range(DKC)]
                    d0 = pc.wdeps() + QT.rdeps() + KT.rdeps()
                    mm = None
                    for i, qi in enumerate(qs):
                        mm = p.op("pe", X("matmul", pc.t[:, 0:64], lhsT=KT.t[:, qi, tl], rhs=QT.t[:, qi, csl], start=(i == 0), stop=(i == DKC - 1)), deps=d0 if i == 0 else [])
                    pc.wrote(mm)
                    QT.read(mm)
                    KT.read(mm)
                    sw = SW.next()
                    o_sw = p.op("dve", X("scalar_tensor_tensor", out=sw.t[pb:pb + 64, :], in0=pc.t[pb:pb + 64, 0:64], scalar=G128.t[pb:pb + 64, tt, h:h + 1], in1=causal[pb:pb + 64, :], op0=ALU.mult, op1=ALU.mult),
                                deps=[mm, dc[2]] + G128.rdeps() + sw.wdeps())
                    sw.wrote(o_sw)
                    pc.read(o_sw)
                    d0 = [o_sw] + VT.rdeps() + Cb[qs[0]].rdeps() + Cb[qs[1]].rdeps() + nrep[qs[0]].rdeps() + nrep[qs[1]].rdeps()
                    for ec in range(DVC):
                        for i, qi in enumerate(qs):
                            mm = p.op("pe", X("matmul", pc.t[:, 64 + 64 * ec:128 + 64 * ec], lhsT=Cb[qi].t[:, ec * 128:(ec + 1) * 128], rhs=QT.t[:, qi, csl], start=(i == 0), stop=False), deps=d0 if (ec == 0 and i == 0) else [])
                        mm = p.op("pe", X("matmul", pc.t[:, 64 + 64 * ec:128 + 64 * ec], lhsT=VT.t[pb:pb + 64, tt, h, ec * 128:(ec + 1) * 128], rhs=sw.t[pb:pb + 64, :], start=False, stop=True))
                    for i, qi in enumerate(qs):
                        mm = p.op("pe", X("matmul", pc.t[:, 320:384], lhsT=nrep[qi].t[:], rhs=QT.t[:, qi, csl], start=(i == 0), stop=False))
                    mm_nd = p.op("pe", X("matmul", pc.t[:, 320:384], lhsT=ones[pb:pb + 64, :], rhs=sw.t[pb:pb + 64, :], start=False, stop=True))
                    pc.also_wrote(mm_nd)
                    sw.read(mm_nd)
                    for qi in qs:
                        Cb[qi].read(mm_nd)
                        nrep[qi].read(mm_nd)
                    VT.read(mm_nd)
                    QT.read(mm_nd)
                    mmcs = []
                    bcs = []
                    for i, qi in enumerate(qs):
                        bc = psC.next()
                        d0 = bc.wdeps() + KG.rdeps() + VT.rdeps()
                        mmc = p.op("pe", X("matmul", bc.t[:], lhsT=KG.t[pb:pb + 64, tt, h, i * 128:(i + 1) * 128], rhs=VT.t[pb:pb + 64, tt, h, :], start=True, stop=True), deps=d0)
                        bc.wrote(mmc)
                        mmcs.append(mmc)
                        bcs.append(bc)
                    mmn = None
                    for i, qi in enumerate(qs):
                        mmn = p.op("pe", X("matmul", pc.t[:, 448 + i:449 + i], lhsT=KG.t[pb:pb + 64, tt, h, i * 128:(i + 1) * 128], rhs=ones[pb:pb + 64, 0:1], start=True, stop=True))
                    pc.also_wrote(mmn)
                    KG.read(mmn)
                    VT.read(mmn)
                    for i, qi in enumerate(qs):
                        bc = bcs[i]
                        mmc = mmcs[i]
                        dcol = DECb.t[:, c, h:h + 1]
                        o_cd = p.op("act", X("activation", out=Cd[qi].t[:], in_=C32[qi].t[:], func=AF.Copy, scale=dcol), deps=Cd[qi].wdeps() + C32[qi].rdeps() + DECb.rdeps())
                        Cd[qi].wrote(o_cd)
                        o_c = p.op("dve", X("scalar_tensor_tensor", out=C32[qi].t[:], in0=bc.t[:], scalar=dcol, in1=Cd[qi].t[:], op0=ALU.mult, op1=ALU.add), deps=[mmc, o_cd] + DECb.rdeps() + C32[qi].wdeps())
                        bc.read(o_c)
                        Cd[qi].read(o_c)
                        C32[qi].wrote(o_c)
                        o_cb = p.op("pool", X("tensor_copy", out=Cb[qi].t[:], in_=C32[qi].t[:]), deps=[o_c] + Cb[qi].wdeps())
                        Cb[qi].wrote(o_cb)
                        C32[qi].read(o_cb)
                        o_n1 = p.op("dve", X("tensor_tensor", out=n32[qi].t[:, 0:1], in0=n32[qi].t[:, 0:1], in1=pc.t[:, 448 + i:449 + i], op=ALU.add), deps=[mmn] + n32[qi].wdeps())
                        o_n2 = p.op("dve", X("tensor_tensor", out=n32[qi].t[:, 0:1], in0=n32[qi].t[:, 0:1], in1=dcol, op=ALU.mult), deps=[o_n1])
                        n32[qi].wrote(o_n2)
                        pc.read(o_n1)
                        o_nr = p.op("pool", X("tensor_copy", out=nrep[qi].t[:], in_=n32[qi].t[:, 0:1].to_broadcast([128, 128])), deps=[o_n2] + nrep[qi].wdeps())
                        nrep[qi].wrote(o_nr)
                        n32[qi].read(o_nr)
                    DECb.read(o_c)
                    dn = DN.next()
                    o0 = p.op("act", X("activation", out=dn.t[:], in_=pc.t[:, 320:384], func=AF.Abs), deps=[mm_nd, mmn] + dn.wdeps())
                    pc.read(o0)
                    o1 = p.op("dve", X("tensor_tensor", out=dn.t[:], in0=dn.t[:], in1=EMWb.t[:, h, csl], op=ALU.max), deps=[o0] + EMWb.rdeps())
                    EMWb.read(o1)
                    o2 = p.op("dve", X("reciprocal", out=dn.t[:], in_=dn.t[:]), deps=[o1])
                    ht = HTt.next()
                    o_ht = p.op("dve", X("tensor_tensor", out=ht.t[:], in0=pc.t[:, 64:320].rearrange("p (e t) -> p e t", e=DVC), in1=dn.t[:].unsqueeze(1).to_broadcast([128, DVC, 64]), op=ALU.mult), deps=[o2, mm_nd, mmn] + ht.wdeps())
                    dn.wrote(o_ht)
                    ht.wrote(o_ht)
                    pc.read(o_ht)
                    if pend[0] is not None:
                        finish(pend[0])
                    pend[0] = (pc, ht, h, c, o_ht)
            finish(pend[0])
            pend[0] = None
            o = p.dma("sp", out_v[:, :, tsl], HG.t[:], deps=HG.rdeps(), ring="hgout", nslots=1)
            HG.read(o)
            out_dmas.append(o)
        p.wait("sp", out_dmas)
        p.emit(ctx)
    return nc


def build_C(S, HD=128, HPG=8, CMP_BLOCK=32, CMP_STRIDE=16, CMP_HID=512, SEL_BLOCK=64, N_SEL=16, WINDOW=512):
    nc = bass.Bass("TRN2", target_bir_lowering=False)
    NQB = S // 64
    NCMP = (S - CMP_BLOCK) // CMP_STRIDE + 1
    NCC = (NCMP + 127) // 128
    NSB = S // SEL_BLOCK
    NKC = S // 128
    HC = CMP_HID // 128
    scale = HD ** -0.5
    with ExitStack() as ctx:
        kb = KB(nc, ctx)
        p = kb.p
        qT_d = kb.din("qT", [HPG * HD, S], BF16)
        gT_d = kb.din("gT", [3 * HPG, S], F32)
        kcT_d = kb.din("kcT", [HD, S], BF16)
        vcT_d = kb.din("vcT", [HD, S], BF16)
        ksT_d = kb.din("ksT", [HD, S], BF16)
        kwT_d = kb.din("kwT", [HD, S], BF16)
        vs_d = kb.din("vs", [S, HD], BF16)
        vw_d = kb.din("vw", [S, HD], BF16)
        w1_d = kb.din("cmp_w1", [2, CMP_BLOCK * HD, CMP_HID], F32)
        w2_d = kb.din("cmp_w2", [2, CMP_HID, HD], F32)
        posT_d = kb.din("posT", [HD, 2 * CMP_BLOCK], F32)
        kng_d = kb.din("kng", [HD, 3], F32)
        qng_d = kb.din("qng", [HD, 3], F32)
        ov_d = kb.din("c_ov", [128, NCC * NSB], F32)
        ex_d = kb.din("c_ex", [64, NKC * 128], F32)
        selg_d = kb.din("c_selg", [3 * HPG, 3 * HPG * 128], F32)
        i64_d = kb.din("c_i64", [64, 64], F32)
        o_d = kb.dout("oT", [HPG * HD, S], BF16)

        Q = Buf(kb.sb("Q", [128, HPG, S], BF16))
        GB_ = kb.sbring("GBc", 2, [128, 3 * HPG, 64], F32)
        KS = Buf(kb.sb("KS", [128, S], BF16))
        KW = Buf(kb.sb("KW", [128, S], BF16))
        VS = Buf(kb.sb("VS", [128, NKC, HD], BF16))
        VW = Buf(kb.sb("VW", [128, NKC, HD], BF16))
        KC = Buf(kb.sb("KC", [128, NCC * 128], BF16))
        VC = Buf(kb.sb("VC", [128, NCC, HD], BF16))
        RAW = kb.sbring("RAW", 1, [128, S], BF16)
        if HPG * S >= 2 * CMP_BLOCK * CMP_HID:
            hq = HPG // 2
            W1 = Buf(Q.t[:, hq:HPG, :].rearrange("p h s -> p (h s)")[:, 0:CMP_BLOCK * CMP_HID].rearrange("p (i n) -> p i n", i=CMP_BLOCK))
            w1_alias = True
        else:
            W1 = Buf(kb.sb("W1", [128, CMP_BLOCK, CMP_HID], BF16))
            w1_alias = False
        W2 = Buf(kb.sb("W2", [128, 2, HC, HD], BF16))
        posT = kb.sb("posT_sb", [128, 2 * CMP_BLOCK], F32)
        posTb = kb.sb("posT_bf", [128, 2 * CMP_BLOCK], BF16)
        kng = kb.sb("kng_sb", [128, 3], F32)
        qng = kb.sb("qng_sb", [128, 3], F32)
        kcg = kb.sb("kcg", [128, 1], F32)
        ov = kb.sb("ov", [128, NCC * NSB], BF16)
        ex = kb.sb("ex", [64, NKC * 128], BF16)
        i64 = kb.sb("i64", [64, 64], BF16)
        ones = kb.sb("ones", [128, 128], BF16)
        tiny = kb.sb("tiny", [128, 1], F32)
        HID = kb.sbring("HID", 2, [128, HC, 256], BF16)
        Z = kb.sbring("Z", 2, [128, 256], F32)
        Z2 = kb.sbring("Z2", 2, [128, 256], F32)
        pbias = Buf(kb.sb("pbias", [128, 2 * HC], F32))
        PT = kb.sbring("PT", 4, [128, HPG, 64], BF16)
        PN = kb.sbring("PN", 2, [128, HPG, 64], BF16)
        RD = kb.sbring("RD", 2, [128, 512], F32)
        CF = kb.sbring("CF", 2, [128, 512], F32)
        OACC = Buf(kb.sb("OACC", [128, 512], F32))
        OB = kb.sbring("OBo", 2, [128, HPG, 64], BF16)
        PM = kb.sbring("PMx", 2, [64, 64], F32)
        PM2 = kb.sbring("PM2", 2, [64, 64], F32)
        M8 = kb.sbring("M8", 2, [64, 16], F32)
        SEL = kb.sbring("SEL", 2, [64, 64], BF16)
        SELT = kb.sbring("SELT", 2, [64, 64], BF16)
        sqb = kb.sbring("SQc", 2, [128, 256], BF16)
        RK = Buf(kb.sb("RK", [128, 256], F32))

        psS = kb.psring("psS", 2)
        psN = kb.psring("psN", 2)
        psD = kb.psring("psD", 2)
        psM = Buf(kb.psum("psM", [128, 512]))
        psX = Buf(kb.psum("psX", [128, 512]))
        psB = psX

        c_ones = p.op("dve", X("memset", ones[:], 1.0))
        kb.eps = kb.sb("eps_sb", [128, 1], F32)
        kb.eps_op = p.op("dve", X("memset", kb.eps[:], EPS))
        c_tiny = p.op("dve", X("memset", tiny[:], 1e-30))
        dq = []
        for h in range(HPG // 2 if w1_alias else HPG):
            dq.append(p.dma("sp", Q.t[:, h, :], qT_d[h * 128:(h + 1) * 128, :], ring="ld", nslots=24))
        Q.wrote(dq[0])
        for o in dq[1:]:
            Q.also_wrote(o)
        o = p.dma("sp", KS.t[:], ksT_d[:, :], ring="ld", nslots=24); KS.wrote(o)
        o = p.dma("sp", KW.t[:], kwT_d[:, :], ring="ld", nslots=24); KW.wrote(o)
        o = p.dma("sp", VS.t[:], vs_d.rearrange("(c p) d -> p c d", p=128), ring="ld", nslots=24); VS.wrote(o)
        o = p.dma("sp", VW.t[:], vw_d.rearrange("(c p) d -> p c d", p=128), ring="ld", nslots=24); VW.wrote(o)
        d_pos = p.dma("sp", posT[:], posT_d[:, :], ring="ld", nslots=24)
        d_kng = p.dma("sp", kng[:], kng_d[:, :], ring="ld", nslots=24)
        d_qng = p.dma("sp", qng[:], qng_d[:, :], ring="ld", nslots=24)
        c_ov = p.dma("pool", ov[:], ov_d[:, :], ring="ldc", nslots=4)
        c_i64 = p.dma("pool", i64[:], i64_d[:, :], ring="ldc", nslots=4)
        c_ex = p.dma("pool", ex[:], ex_d[:, :], ring="ldc", nslots=4)
        c_pos = p.op("dve", X("tensor_copy", out=posTb[:], in_=posT[:]), deps=[d_pos])
        o = p.op("dve", X("tensor_tensor", out=kcg[:], in0=kng[:, 0:1], in1=qng[:, 0:1], op=ALU.mult), deps=[d_kng, d_qng])
        c_kcg = p.op("dve", X("tensor_scalar", out=kcg[:], in0=kcg[:], scalar1=scale, scalar2=None, op0=ALU.mult), deps=[o])
        z0 = p.op("pool", X("memset", KC.t[:], 0.0)); KC.wrote(z0)
        z1 = p.op("pool", X("memset", VC.t[:], 0.0)); VC.wrote(z1)
        w2l = p.dma("pool", W2.t[:], w2_d.rearrange("j (c p) d -> p j c d", p=128), ring="w2", nslots=1)
        W2.wrote(w2l)

        for j in range(2):
            raw = RAW.next()
            dr = p.dma("sp", raw.t[:], (kcT_d if j == 0 else vcT_d)[:, :], deps=raw.wdeps(), ring="ldr", nslots=2)
            raw.wrote(dr)
            dw = p.dma("pool", W1.t[:], w1_d[j].rearrange("(i p) n -> p i n", p=128), deps=W1.wdeps(), ring="w1", nslots=1)
            W1.wrote(dw)
            bank = psB
            d0 = bank.wdeps() + [dw, c_pos]
            mm = None
            for hc in range(HC):
                for i in range(CMP_BLOCK):
                    mm = p.op("pe", X("matmul", bank.t[:, hc:hc + 1], lhsT=W1.t[:, i, hc * 128:(hc + 1) * 128], rhs=posTb[:, j * CMP_BLOCK + i:j * CMP_BLOCK + i + 1], start=(i == 0), stop=(i == CMP_BLOCK - 1)), deps=d0 if (hc == 0 and i == 0) else [])
            bank.wrote(mm)
            o_pb = p.op("dve", X("tensor_copy", out=pbias.t[:, j * HC:(j + 1) * HC], in_=bank.t[:, 0:HC]), deps=[mm] + pbias.wdeps())
            bank.read(o_pb)
            pbias.wrote(o_pb)
            hid = HID.next()
            first = True
            for hc in range(HC):
                for c0 in range(0, NCMP, 256):
                    ncol = min(256, NCMP - c0)
                    bank = psS.next()
                    d0 = bank.wdeps() + [dw, dr]
                    for i in range(CMP_BLOCK):
                        t0 = CMP_STRIDE * c0 + i
                        mm = p.op("pe", X("matmul", bank.t[:, 0:ncol], lhsT=W1.t[:, i, hc * 128:(hc + 1) * 128], rhs=raw.t[:, t0:t0 + CMP_STRIDE * (ncol - 1) + 1:CMP_STRIDE], start=(i == 0), stop=(i == CMP_BLOCK - 1)), deps=d0 if i == 0 else [])
                    bank.wrote(mm)
                    W1.read(mm)
                    raw.read(mm)
                    z = Z.next()
                    z2 = Z2.next()
                    o1 = p.op("act", X("activation", out=z.t[:, 0:ncol], in_=bank.t[:, 0:ncol], func=AF.Identity, bias=pbias.t[:, j * HC + hc:j * HC + hc + 1]), deps=[mm, o_pb] + z.wdeps())
                    bank.read(o1)
                    o2 = p.op("dve", X("tensor_tensor", out=z2.t[:, 0:ncol], in0=z.t[:, 0:ncol], in1=z.t[:, 0:ncol], op=ALU.mult), deps=[o1] + z2.wdeps())
                    o3 = p.op("dve", X("tensor_scalar", out=z2.t[:, 0:ncol], in0=z2.t[:, 0:ncol], scalar1=0.044715, scalar2=1.0, op0=ALU.mult, op1=ALU.add), deps=[o2])
                    o4 = p.op("dve", X("tensor_tensor", out=z2.t[:, 0:ncol], in0=z2.t[:, 0:ncol], in1=z.t[:, 0:ncol], op=ALU.mult), deps=[o3])
                    o5 = p.op("act", X("activation", out=z2.t[:, 0:ncol], in_=z2.t[:, 0:ncol], func=AF.Sigmoid, scale=1.5957691216057308), deps=[o4])
                    o6 = p.op("dve", X("tensor_tensor", out=hid.t[:, hc, c0:c0 + ncol] if NCMP <= 256 else hid.t[:, hc, 0:ncol], in0=z.t[:, 0:ncol], in1=z2.t[:, 0:ncol], op=ALU.mult), deps=[o5] + hid.wdeps())
                    z.wrote(o6)
                    z2.wrote(o6)
                    hid.wrote(o6) if first else hid.also_wrote(o6)
                    first = False
            assert NCMP <= 256
            if j == 0:
                bank = psS.next()
                d0 = bank.wdeps() + hid.rdeps() + [w2l]
                for hc in range(HC):
                    mm = p.op("pe", X("matmul", bank.t[:, 0:NCMP], lhsT=W2.t[:, 0, hc, :], rhs=hid.t[:, hc, 0:NCMP], start=(hc == 0), stop=(hc == HC - 1)), deps=d0 if hc == 0 else [])
                bank.wrote(mm)
                hid.read(mm)
                sq = sqb.next()
                o_sq = p.op("act", X("activation", out=sq.t[:, 0:NCMP], in_=bank.t[:, 0:NCMP], func=AF.Square), deps=[mm] + sq.wdeps())
                sq.wrote(o_sq)
                bank.read(o_sq)
                mm2 = p.op("pe", X("matmul", psB.t[:, 0:NCMP], lhsT=ones[:], rhs=sq.t[:, 0:NCMP], start=True, stop=True), deps=[o_sq, c_ones] + psB.wdeps())
                psB.wrote(mm2)
                sq.read(mm2)
                o1 = p.op("act", X("activation", out=RK.t[:, 0:NCMP], in_=psB.t[:, 0:NCMP], func=AF.Sqrt, scale=1.0 / HD, bias=kb.eps[:, 0:1]), deps=[mm2, kb.eps_op] + RK.wdeps())
                psB.read(o1)
                o2 = p.op("dve", X("reciprocal", out=RK.t[:, 0:NCMP], in_=RK.t[:, 0:NCMP]), deps=[o1])
                o3 = p.op("dve", X("scalar_tensor_tensor", out=KC.t[:, 0:NCMP], in0=bank.t[:, 0:NCMP], scalar=kcg[:, 0:1], in1=RK.t[:, 0:NCMP], op0=ALU.mult, op1=ALU.mult), deps=[o2, c_kcg, mm] + KC.wdeps())
                bank.read(o3)
                RK.wrote(o3)
                KC.wrote(o3)
            else:
                for cc in range(NCC):
                    ncol = min(128, NCMP - cc * 128)
                    bank = psS.next()
                    d0 = bank.wdeps() + hid.rdeps() + [w2l]
                    for hc in range(HC):
                        mm = p.op("pe", X("matmul", bank.t[0:ncol, 0:HD], lhsT=hid.t[:, hc, cc * 128:cc * 128 + ncol], rhs=W2.t[:, 1, hc, :], start=(hc == 0), stop=(hc == HC - 1)), deps=d0 if hc == 0 else [])
                    bank.wrote(mm)
                    hid.read(mm)
                    o = p.op("act", X("copy", out=VC.t[0:ncol, cc, :], in_=bank.t[0:ncol, 0:HD]), deps=[mm] + VC.wdeps())
                    bank.read(o)
                    VC.wrote(o) if cc == 0 else VC.also_wrote(o)

        if w1_alias:
            for h in range(HPG // 2, HPG):
                o = p.dma("sp", Q.t[:, h, :], qT_d[h * 128:(h + 1) * 128, :], deps=W1.wdeps(), ring="ld", nslots=24)
                Q.also_wrote(o)
        out_dmas = []
        o_v = o_d.rearrange("(h p) t -> p h t", p=128)
        for qb in range(NQB):
            qsl = slice(qb * 64, (qb + 1) * 64)
            t0 = qb * 64
            rhsq = Q.t[:, :, qsl]
            need_topk = (qb + 1) > N_SEL
            kc_last = qb // 2
            st = {"selT": None, "pts": [], "rd": None}
            gb = GB_.next()
            dg = p.dma("sp", gb.t[:], gT_d[:, qsl].partition_broadcast(128), deps=gb.wdeps(), ring="gbc", nslots=2)
            gb.wrote(dg)
            st["gb"] = gb

            def stage1(job):
                (br, kT_ap, v_ap, kdeps, first, last, mask_fn) = job
                bank = psS.next()
                mm = p.op("pe", X("matmul", bank.t[:], lhsT=kT_ap, rhs=rhsq, start=True, stop=True), deps=bank.wdeps() + Q.rdeps() + kdeps)
                bank.wrote(mm)
                pt = PT.next()
                o = p.op("act", X("activation", out=pt.t[:].rearrange("p h q -> p (h q)"), in_=bank.t[:], func=AF.Exp), deps=[mm] + pt.wdeps())
                bank.read(o)
                pt.wrote(o)
                o = mask_fn(pt, o)
                return (pt, o)

            def stage2(job, s1, acc):
                (br, kT_ap, v_ap, kdeps, first, last, mask_fn) = job
                pt, o = s1
                bN, bD = acc
                d0 = [o]
                if first:
                    d0 = d0 + bN.wdeps() + bD.wdeps()
                ptf = pt.t[:].rearrange("p h q -> p (h q)")
                m1 = p.op("pe", X("matmul", bN.t[:], lhsT=v_ap, rhs=ptf, start=first, stop=last), deps=d0 + kdeps)
                m2 = p.op("pe", X("matmul", bD.t[:], lhsT=ones[:], rhs=ptf, start=first, stop=last), deps=[c_ones])
                pt.read(m2)
                if last:
                    bN.wrote(m1)
                    bD.wrote(m2)
                return m2

            def finish_branch(r, m_last, first_branch, acc):
                bN, bD = acc
                rd = RD.next()
                o1 = p.op("dve", X("tensor_scalar", out=rd.t[:], in0=bD.t[:], scalar1=tiny[:, 0:1], scalar2=None, op0=ALU.max), deps=[m_last, c_tiny] + rd.wdeps())
                bD.read(o1)
                o2 = p.op("dve", X("reciprocal", out=rd.t[:], in_=rd.t[:]), deps=[o1])
                rd.wrote(o2)
                cf = CF.next()
                gb = st["gb"]
                o3 = p.op("dve", X("tensor_tensor", out=cf.t[:], in0=gb.t[:, r * HPG:(r + 1) * HPG, :].rearrange("p h q -> p (h q)"), in1=rd.t[:], op=ALU.mult), deps=[o2] + gb.rdeps() + cf.wdeps())
                gb.read(o3)
                rd.read(o3)
                if first_branch:
                    o4 = p.op("dve", X("tensor_tensor", out=OACC.t[:], in0=bN.t[:], in1=cf.t[:], op=ALU.mult), deps=[o3, m_last] + OACC.wdeps())
                else:
                    o4a = p.op("dve", X("tensor_tensor", out=cf.t[:], in0=bN.t[:], in1=cf.t[:], op=ALU.mult), deps=[o3, m_last])
                    o4 = p.op("dve", X("tensor_tensor", out=OACC.t[:], in0=OACC.t[:], in1=cf.t[:], op=ALU.add), deps=[o4a] + OACC.wdeps())
                cf.wrote(o4)
                bN.read(o4)
                OACC.wrote(o4)
                return rd, o2

            def topk(rd, o_rd, pts):
                pns = []
                for cc, pt in enumerate(pts):
                    pn = PN.next()
                    o = p.op("dve", X("tensor_tensor", out=pn.t[:].rearrange("p h q -> p (h q)"), in0=pt.t[:].rearrange("p h q -> p (h q)"), in1=rd.t[:], op=ALU.mult), deps=[o_rd] + pt.rdeps() + pn.wdeps())
                    pn.wrote(o)
                    pt.read(o)
                    rd.read(o)
                    pns.append((pn, o))
                d0 = psX.wdeps() + [c_ov]
                n_mm = len(pns) * HPG
                i = 0
                mm = None
                for cc, (pn, o) in enumerate(pns):
                    for h in range(HPG):
                        mm = p.op("pe", X("matmul", psX.t[0:64, 0:NSB], lhsT=pn.t[:, h, :], rhs=ov[:, cc * NSB:(cc + 1) * NSB], start=(i == 0), stop=(i == n_mm - 1)), deps=(d0 if i == 0 else []) + ([o] if h == 0 else []))
                        i += 1
                    pn.read(mm)
                psX.wrote(mm)
                pm = PM.next()
                o1 = p.op("dve", X("tensor_copy", out=pm.t[:, 0:NSB], in_=psX.t[0:64, 0:NSB]), deps=[mm] + pm.wdeps())
                psX.read(o1)
                o2 = p.op("dve", X("memset", pm.t[:, 0:1], 1e30), deps=[o1])
                o3 = p.op("dve", X("memset", pm.t[:, qb:qb + 1], 1e30), deps=[o2])
                if qb + 1 < NSB:
                    o3 = p.op("dve", X("memset", pm.t[:, qb + 1:NSB], -1.0), deps=[o3])
                m8 = M8.next()
                o4 = p.op("dve", X("max", out=m8.t[:, 0:8], in_=pm.t[:, 0:NSB]), deps=[o3] + m8.wdeps())
                pm2 = PM2.next()
                o5 = p.op("dve", X("match_replace", out=pm2.t[:, 0:NSB], in_to_replace=m8.t[:, 0:8], in_values=pm.t[:, 0:NSB], imm_value=-2.0), deps=[o4] + pm2.wdeps())
                o6 = p.op("dve", X("max", out=m8.t[:, 8:16], in_=pm2.t[:, 0:NSB]), deps=[o5])
                pm2.wrote(o6)
                sel = SEL.next()
                o7 = p.op("dve", X("tensor_scalar", out=sel.t[:, 0:NSB], in0=pm.t[:, 0:NSB], scalar1=m8.t[:, 15:16], scalar2=None, op0=ALU.is_ge), deps=[o6] + sel.wdeps())
                m8.wrote(o7)
                pm.wrote(o7)
                sel.wrote(o7)
                mmT = p.op("pe", X("matmul", psX.t[0:NSB, 64:128], lhsT=sel.t[:, 0:NSB], rhs=i64[:], start=True, stop=True), deps=[o7, c_i64] + psX.wdeps())
                psX.wrote(mmT)
                sel.read(mmT)
                selT = SELT.next()
                o8 = p.op("act", X("copy", out=selT.t[0:NSB, :], in_=psX.t[0:NSB, 64:128]), deps=[mmT] + selT.wdeps())
                psX.read(o8)
                selT.wrote(o8)
                st["selT"] = selT

            jobs = []
            ncv = min(NCMP, (t0 + 63 - (CMP_BLOCK - 1)) // CMP_STRIDE + 1)
            ncc_q = max(1, (ncv + 127) // 128)
            for cc in range(ncc_q):
                def mask_cmp(pt, o, cc=cc):
                    o2 = p.op("pool", X("affine_select", out=pt.t[:], in_=pt.t[:], pattern=[[0, HPG], [1, 64]], compare_op=ALU.is_ge, fill=0.0,
                                        base=t0 - (CMP_BLOCK - 1) - CMP_STRIDE * 128 * cc, channel_multiplier=-CMP_STRIDE), deps=[o])
                    pt.wrote(o2)
                    st["pts"].append(pt)
                    return o2
                jobs.append((0, KC.t[:, cc * 128:(cc + 1) * 128], VC.t[:, cc, :], KC.rdeps() + VC.rdeps(), cc == 0, cc == ncc_q - 1, mask_cmp))
            kc_first = max(0, (t0 - (WINDOW - 1)) // 128)
            for kc in range(kc_first, kc_last + 1):
                def mask_win(pt, o, kc=kc):
                    if 128 * kc < t0 + 63 - (WINDOW - 1):
                        o = p.op("pool", X("affine_select", out=pt.t[:], in_=pt.t[:], pattern=[[0, HPG], [-1, 64]], compare_op=ALU.is_ge, fill=0.0,
                                           base=128 * kc - t0 + WINDOW - 1, channel_multiplier=1), deps=[o])
                        pt.wrote(o)
                    if 128 * kc + 127 > t0:
                        o = p.op("pool", X("affine_select", out=pt.t[:], in_=pt.t[:], pattern=[[0, HPG], [1, 64]], compare_op=ALU.is_ge, fill=0.0,
                                           base=t0 - 128 * kc, channel_multiplier=-1), deps=[o])
                        pt.wrote(o)
                    return o
                jobs.append((2, KW.t[:, kc * 128:(kc + 1) * 128], VW.t[:, kc, :], KW.rdeps() + VW.rdeps(), kc == kc_first, kc == kc_last, mask_win))
            for kc in range(kc_last + 1):
                def mask_sel(pt, o, kc=kc):
                    if need_topk:
                        selT = st["selT"]
                        bm = psM
                        mmx = p.op("pe", X("matmul", bm.t[:, 0:64], lhsT=ex[0:NSB, kc * 128:(kc + 1) * 128], rhs=selT.t[0:NSB, :], start=True, stop=True), deps=bm.wdeps() + selT.rdeps() + [c_ex])
                        bm.wrote(mmx)
                        selT.read(mmx)
                        o = p.op("dve", X("tensor_tensor", out=pt.t[:], in0=pt.t[:], in1=bm.t[:, 0:64].unsqueeze(1).to_broadcast([128, HPG, 64]), op=ALU.mult), deps=[o, mmx])
                        bm.read(o)
                        pt.wrote(o)
                    if kc == kc_last:
                        o = p.op("pool", X("affine_select", out=pt.t[:], in_=pt.t[:], pattern=[[0, HPG], [1, 64]], compare_op=ALU.is_ge, fill=0.0,
                                           base=t0 - 128 * kc, channel_multiplier=-1), deps=[o])
                        pt.wrote(o)
                    return o
                jobs.append((1, KS.t[:, kc * 128:(kc + 1) * 128], VS.t[:, kc, :], KS.rdeps() + VS.rdeps(), kc == 0, kc == kc_last, mask_sel))

            nj = len(jobs)
            accs = {}
            s1s = [None] * nj
            def can_issue(i):
                return not (jobs[i][0] == 1 and need_topk and st["selT"] is None)
            if can_issue(0):
                s1s[0] = stage1(jobs[0])
            first_branch = True
            for i in range(nj):
                if i + 1 < nj and s1s[i + 1] is None and can_issue(i + 1):
                    s1s[i + 1] = stage1(jobs[i + 1])
                if s1s[i] is None:
                    s1s[i] = stage1(jobs[i])
                br = jobs[i][0]
                if jobs[i][4]:
                    accs[br] = (psN.next(), psD.next())
                m2 = stage2(jobs[i], s1s[i], accs[br])
                if jobs[i][5]:
                    rd, o_rd = finish_branch(br, m2, first_branch, accs[br])
                    first_branch = False
                    if br == 0 and need_topk:
                        topk(rd, o_rd, st["pts"])

            ob = OB.next()
            o = p.op("act", X("copy", out=ob.t[:].rearrange("p h q -> p (h q)"), in_=OACC.t[:]), deps=OACC.rdeps() + ob.wdeps())
            OACC.read(o)
            ob.wrote(o)
            od = p.dma("sp", o_v[:, :, qsl], ob.t[:], deps=[o], ring="oout", nslots=2)
            ob.read(od)
            out_dmas.append(od)
        p.wait("sp", out_dmas)
        p.emit(ctx)
    return nc


D_MODEL = 4096
SEQ = 4096
BATCH = 2
D_FF = 4 * D_MODEL
M_HEADS = 8
M_DV = 512
M_DK = 256
N_KV = 4
HPG = 8
HEAD_DIM = 128
KV_DIM = 512
NCORES = 8


def _lay(g):
    return np.ascontiguousarray(np.asarray(g, np.float32).reshape(-1, 128).T)


def _consts_A():
    cc = np.tril(np.ones((64, 64), np.float32)).T
    cc = np.ascontiguousarray(np.concatenate([cc, cc], 0))
    seg = np.ones((2, 512), np.float32)
    seg[:, ::64] = 0
    neg = np.zeros((2, 512), np.float32)
    neg[:, ::64] = -1e30
    sel = np.zeros((2, 256), np.float32)
    sel[0, :128] = 1
    sel[1, 128:] = 1
    return {"c_causal": cc, "c_seg": np.concatenate([seg, neg], 1), "c_sel": sel, "c_i2": np.eye(2, dtype=np.float32)}


def _consts_C(S):
    NCMP = (S - 32) // 16 + 1
    NCC = (NCMP + 127) // 128
    NSB = S // 64
    NKC = S // 128
    c0 = np.arange(NCC * 128)[:, None] * 16
    s0 = np.arange(NSB)[None, :] * 64
    ovm = np.maximum(np.minimum(c0 + 32, s0 + 64) - np.maximum(c0, s0), 0) / 32.0
    ovm[NCMP:] = 0
    ov = np.ascontiguousarray(ovm.reshape(NCC, 128, NSB).transpose(1, 0, 2).reshape(128, NCC * NSB).astype(np.float32))
    ex = np.zeros((64, NKC * 128), np.float32)
    pp = np.arange(NKC * 128)
    jj = 2 * (pp // 128) + (pp % 128) // 64
    ok = jj < 64
    ex[jj[ok], pp[ok]] = 1
    selg = np.zeros((24, 24 * 128), np.float32)
    for r in range(24):
        selg[r, r * 128:(r + 1) * 128] = 1
    return {"c_ov": ov, "c_ex": ex, "c_selg": selg, "c_i64": np.eye(64, dtype=np.float32)}


_NC_CACHE = {}


def _get(name, fn):
    if name not in _NC_CACHE:
        _NC_CACHE[name] = fn()
    return _NC_CACHE[name]


def _run(nc, in_maps):
    res = run_bass_kernel_spmd(nc, in_maps, core_ids=list(range(len(in_maps))))
    return res.results


def kernel(x, attn_norm_g, mlp_norm_g, m_w_in, m_b_gate, m_head_g, m_w_out, kv_norm_g, w_kv,
           k_norm_g, cmp_pos, cmp_w1, cmp_w2, n_w_qg, q_norm_g, n_w_out, mlp_w_up, mlp_w_down):
    f32 = np.float32
    x = np.asarray(x, f32)
    B, S, D = x.shape
    T = S // 4
    xT = [np.ascontiguousarray(x[b].T) for b in range(B)]
    m_w_in = np.asarray(m_w_in, f32)[0]
    QK = M_HEADS * M_DK
    ncA = _get("A", lambda: build_A(S, D, M_DK, M_DV))
    cA = _consts_A()
    bgate = np.asarray(m_b_gate, f32)[0]
    hg_all = np.asarray(m_head_g, f32)[0]
    ga0 = _lay(np.asarray(attn_norm_g, f32)[0])
    mapsA = []
    for c in range(NCORES):
        b, hp = c // 4, c % 4
        h0 = 2 * hp
        cq = slice(h0 * M_DK, (h0 + 2) * M_DK)
        cv = slice(h0 * M_DV, (h0 + 2) * M_DV)
        gcols = [2 * QK + 2 * D + h0, 2 * QK + 2 * D + h0 + 1, 2 * QK + 2 * D + M_HEADS + h0, 2 * QK + 2 * D + M_HEADS + h0 + 1]
        m = {"xT": xT[b], "ga": ga0,
             "w_q": np.ascontiguousarray(m_w_in[:, 0:QK][:, cq]),
             "w_k": np.ascontiguousarray(m_w_in[:, QK:2 * QK][:, cq]),
             "w_v": np.ascontiguousarray(m_w_in[:, 2 * QK:2 * QK + D][:, cv]),
             "w_og": np.ascontiguousarray(m_w_in[:, 2 * QK + D:2 * QK + 2 * D][:, cv]),
             "w_g": np.ascontiguousarray(m_w_in[:, gcols]),
             "bg": np.ascontiguousarray(np.stack([bgate[[h0, h0 + 1]], bgate[[M_HEADS + h0, M_HEADS + h0 + 1]]], axis=1)),
             "hgain": _lay(hg_all[h0 * M_DV:(h0 + 2) * M_DV])}
        m.update(cA)
        mapsA.append(m)
    rA = _run(ncA, mapsA)
    aT = [np.concatenate([np.asarray(rA[b * 4 + hp]["hgT"]) for hp in range(4)], axis=0) for b in range(B)]
    del rA, mapsA

    ncB = _get("B", lambda: build_BD(T, D, D_FF, True))
    gainsB = np.ascontiguousarray(np.concatenate([_lay(np.asarray(mlp_norm_g, f32)[0]), _lay(np.asarray(attn_norm_g, f32)[1]), _lay(np.asarray(kv_norm_g, f32))], axis=1))
    kngT = np.ascontiguousarray(np.asarray(k_norm_g, f32).T)
    qngT = np.ascontiguousarray(np.asarray(q_norm_g, f32)[0].T)
    w_o0 = np.asarray(m_w_out, f32)[0]
    w_up0 = np.asarray(mlp_w_up, f32)[0]
    w_dn0 = np.asarray(mlp_w_down, f32)[0]
    w_qg = np.asarray(n_w_qg, f32)[0]
    w_kv_ = np.asarray(w_kv, f32)
    mapsB = []
    for c in range(NCORES):
        b, r = c // 4, c % 4
        ts = slice(r * T, (r + 1) * T)
        mapsB.append({"xT": np.ascontiguousarray(xT[b][:, ts]), "aT": np.ascontiguousarray(aT[b][:, ts]),
                      "w_o": w_o0, "w_up": w_up0, "w_dn": w_dn0, "gains": gainsB,
                      "w_qg": w_qg, "w_kv": w_kv_, "kng": kngT, "qng": qngT})
    rB = _run(ncB, mapsB)
    del mapsB

    def catT(name, b):
        return np.concatenate([np.asarray(rB[b * 4 + r][name]) for r in range(4)], axis=1)

    def catR(name, b):
        return np.concatenate([np.asarray(rB[b * 4 + r][name]) for r in range(4)], axis=0)

    x2T = [catT("xoT", b) for b in range(B)]
    ncC = _get("C", lambda: build_C(S))
    cC = _consts_C(S)
    posT = np.ascontiguousarray(np.asarray(cmp_pos, f32).transpose(2, 0, 1).reshape(HEAD_DIM, -1))
    w1 = np.asarray(cmp_w1, f32)
    w2 = np.asarray(cmp_w2, f32)
    mapsC = []
    for b in range(B):
        qT = catT("qT", b)
        gT = catT("gT", b)
        kcT, vcT, ksT, kwT = catT("kcT", b), catT("vcT", b), catT("ksT", b), catT("kwT", b)
        vs, vw = catR("vs", b), catR("vw", b)
        for g in range(N_KV):
            gr = slice(g * 128, (g + 1) * 128)
            grow = np.concatenate([np.arange(r_ * 32 + g * 8, r_ * 32 + g * 8 + 8) for r_ in range(3)])
            m = {"qT": np.ascontiguousarray(qT[g * 1024:(g + 1) * 1024]), "gT": np.ascontiguousarray(gT[grow]),
                 "kcT": np.ascontiguousarray(kcT[gr]), "vcT": np.ascontiguousarray(vcT[gr]),
                 "ksT": np.ascontiguousarray(ksT[gr]), "kwT": np.ascontiguousarray(kwT[gr]),
                 "vs": np.ascontiguousarray(vs[:, gr]), "vw": np.ascontiguousarray(vw[:, gr]),
                 "cmp_w1": w1, "cmp_w2": w2, "posT": posT, "kng": kngT, "qng": qngT}
            m.update(cC)
            mapsC.append(m)
    del rB
    rC = _run(ncC, mapsC)
    aT2 = [np.concatenate([np.asarray(rC[b * 4 + g]["oT"]) for g in range(4)], axis=0) for b in range(B)]
    del rC, mapsC

    ncD = _get("D", lambda: build_BD(T, D, D_FF, False))
    gainsD = np.ascontiguousarray(np.concatenate([_lay(np.asarray(mlp_norm_g, f32)[1])] * 3, axis=1))
    w_o1 = np.asarray(n_w_out, f32)[0]
    w_up1 = np.asarray(mlp_w_up, f32)[1]
    w_dn1 = np.asarray(mlp_w_down, f32)[1]
    mapsD = []
    for c in range(NCORES):
        b, r = c // 4, c % 4
        ts = slice(r * T, (r + 1) * T)
        mapsD.append({"xT": np.ascontiguousarray(x2T[b][:, ts]), "aT": np.ascontiguousarray(aT2[b][:, ts]),
                      "w_o": w_o1, "w_up": w_up1, "w_dn": w_dn1, "gains": gainsD})
    rD = _run(ncD, mapsD)
    out = np.empty((B, S, D), f32)
    for c in range(NCORES):
        b, r = c // 4, c % 4
        out[b, r * T:(r + 1) * T, :] = np.asarray(rD[c]["xoT"]).T
    return out
```

```python
import numpy as np
from contextlib import ExitStack
import ml_dtypes
import concourse.bass as bass
import concourse.mybir as mybir
from concourse.bass_utils import run_bass_kernel_spmd

F32 = mybir.dt.float32
BF16 = mybir.dt.bfloat16
ALU = mybir.AluOpType
AF = mybir.ActivationFunctionType
AX = mybir.AxisListType
NPBF = ml_dtypes.bfloat16

EPS = 1e-6
ENGS = ("pe", "act", "dve", "pool", "sp")


class Op:
    __slots__ = ("eng", "fn", "deps", "is_dma", "sem", "val", "signal", "ring", "slot")

    def __init__(self, eng, fn, deps, is_dma=False):
        self.eng = eng
        self.fn = fn
        self.deps = [d for d in deps if d is not None]
        self.is_dma = is_dma
        self.sem = None
        self.val = None
        self.signal = False
        self.ring = None
        self.slot = None


def X(name, *a, **kw):
    return (name, a, kw)


class Prog:
    def __init__(self, nc):
        self.nc = nc
        self.ops = {e: [] for e in ENGS}
        self.rings = {}

    def op(self, eng, fn, deps=()):
        o = Op(eng, fn, deps)
        self.ops[eng].append(o)
        return o

    def dma(self, q, out, in_, deps=(), ring="dflt", nslots=4, **kw):
        def fn(e, out=out, in_=in_, kw=kw):
            return e.dma_start(out=out, in_=in_, **kw)
        o = Op(q, fn, deps, is_dma=True)
        self.ops[q].append(o)
        r = self.rings.setdefault(ring, {"n": nslots, "count": 0, "uses": {}, "last": {}})
        s = r["count"] % r["n"]
        r["count"] += 1
        r["uses"][s] = r["uses"].get(s, 0) + 1
        if s in r["last"]:
            o.deps.append(r["last"][s])
        r["last"][s] = o
        o.ring = ring
        o.slot = s
        o.val = 16 * r["uses"][s]
        return o

    def wait(self, eng, deps):
        o = Op(eng, None, deps)
        self.ops[eng].append(o)
        return o

    def emit(self, ctx):
        nc = self.nc
        esem = {e: ctx.enter_context(nc.semaphore("s_" + e)) for e in ("pe", "act", "dve", "pool")}
        rsem = {}
        for name, r in self.rings.items():
            rsem[name] = [ctx.enter_context(nc.semaphore("r_%s_%d" % (name, i))) for i in range(len(r["uses"]))]
        for e in ENGS:
            for o in self.ops[e]:
                for d in o.deps:
                    if (not d.is_dma) and (d.eng != e or e != "pe"):
                        d.signal = True
        for e in ("pe", "act", "dve", "pool"):
            c = 0
            for o in self.ops[e]:
                if o.is_dma:
                    continue
                if o.signal:
                    c += 1
                    o.sem = esem[e]
                    o.val = c
        for e in ENGS:
            for o in self.ops[e]:
                if o.is_dma:
                    o.sem = rsem[o.ring][o.slot]
        blk = ctx.enter_context(nc.Block())

        def run(e, engobj):
            waited = {}
            for o in self.ops[e]:
                for d in o.deps:
                    if (not d.is_dma) and d.eng == e and e == "pe":
                        continue
                    key = id(d.sem)
                    if waited.get(key, 0) >= d.val:
                        continue
                    engobj.wait_ge(d.sem, d.val)
                    waited[key] = d.val
                if o.fn is None:
                    continue
                if isinstance(o.fn, tuple):
                    ins = getattr(engobj, o.fn[0])(*o.fn[1], **o.fn[2])
                else:
                    ins = o.fn(engobj)
                if o.is_dma:
                    ins.then_inc(o.sem, 16)
                elif o.signal:
                    ins.then_inc(o.sem, 1)

        @blk.tensor
        def _(eng):
            run("pe", eng)

        @blk.scalar
        def _(eng):
            run("act", eng)

        @blk.vector
        def _(eng):
            run("dve", eng)

        @blk.gpsimd
        def _(eng):
            run("pool", eng)

        @blk.sync
        def _(eng):
            run("sp", eng)


class Buf:
    def __init__(self, t):
        self.t = t
        self.ws = {}
        self.rs = {}
        self.dw = []
        self.dr = []

    @staticmethod
    def _add(d, lst, op):
        if op.is_dma:
            lst.append(op)
        else:
            d[op.eng] = op

    def wdeps(self):
        return list(self.ws.values()) + self.dw + list(self.rs.values()) + self.dr

    def rdeps(self):
        return list(self.ws.values()) + self.dw

    def wrote(self, op):
        self.ws = {}
        self.rs = {}
        self.dw = []
        self.dr = []
        self._add(self.ws, self.dw, op)

    def also_wrote(self, op):
        self._add(self.ws, self.dw, op)

    def read(self, op):
        self._add(self.rs, self.dr, op)


class Ring:
    def __init__(self, bufs):
        self.bufs = bufs
        self.i = 0

    def next(self):
        b = self.bufs[self.i % len(self.bufs)]
        self.i += 1
        return b


class WStream:
    def __init__(self, kb, nstg=2, elems=4096, name="stg"):
        self.kb = kb
        self.n = nstg
        self.elems = elems
        self.name = name
        self.ring = kb.sbring(name, nstg, [128, elems], F32)
        self.i = 0

    def load(self, dst_ap, src_ap, a, b, wd, dstbuf, first):
        p = self.kb.p
        stg = self.ring.next()
        view = stg.t[:, 0:a * b].rearrange("p (a b) -> p a b", a=a)
        d = p.dma("sp", view, src_ap, deps=stg.wdeps(), ring=self.name, nslots=self.n)
        stg.wrote(d)
        c = p.op("act", X("copy", out=dst_ap, in_=view), deps=[d] + list(wd))
        self.i += 1
        stg.read(c)
        if first:
            dstbuf.wrote(c)
        else:
            dstbuf.also_wrote(c)
        return c


class KB:
    def __init__(self, nc, ctx):
        self.nc = nc
        self.ctx = ctx
        self.p = Prog(nc)
        self.nps = 0

    def sb(self, name, shape, dt):
        return self.ctx.enter_context(self.nc.sbuf_tensor(name, list(shape), dt))

    def psum(self, name, shape, dt=F32):
        return self.ctx.enter_context(self.nc.psum_tensor(name, list(shape), dt))

    def sbring(self, name, n, shape, dt):
        return Ring([Buf(self.sb("%s%d" % (name, i), shape, dt)) for i in range(n)])

    def psring(self, name, n, shape=(128, 512), dt=F32):
        return Ring([Buf(self.psum("%s%d" % (name, i), shape, dt)) for i in range(n)])

    def din(self, name, shape, dt):
        return self.nc.dram_tensor(name, list(shape), dt, kind="ExternalInput").ap()

    def dout(self, name, shape, dt):
        return self.nc.dram_tensor(name, list(shape), dt, kind="ExternalOutput").ap()


def mm_group(kb, bank, pairs, extra_deps=(), out_ap=None):
    p = kb.p
    n = len(pairs)
    last = None
    out = bank.t[:] if out_ap is None else out_ap
    for i, (l, r, deps) in enumerate(pairs):
        d = list(deps)
        if i == 0:
            d += bank.wdeps() + list(extra_deps)
        last = p.op("pe", X("matmul", out, lhsT=l, rhs=r, start=(i == 0), stop=(i == n - 1)), deps=d)
    bank.wrote(last)
    return last


def rms_stats(kb, XT, KD, D, ones, sqring, ssbank, R, xt_ready):
    p = kb.p
    last = None
    for k in range(KD):
        sq = sqring.next()
        o = p.op("act", X("activation", out=sq.t[:], in_=XT.t[:, k, :], func=AF.Square), deps=sq.wdeps() + xt_ready)
        sq.wrote(o)
        d = [o]
        if k == 0:
            d += ssbank.wdeps()
        last = p.op("pe", X("matmul", ssbank.t[:], lhsT=ones[:], rhs=sq.t[:], start=(k == 0), stop=(k == KD - 1)), deps=d)
        sq.read(last)
    ssbank.wrote(last)
    o2 = rstd_from(kb, ssbank, R, 1.0 / D, [last])
    R.wrote(o2)
    return o2


def rstd_from(kb, bank, dst, inv_n, deps, eps_ap=None):
    p = kb.p
    o1 = p.op("act", X("activation", out=dst.t[:], in_=bank.t[:], func=AF.Sqrt, scale=inv_n, bias=kb.eps[:, 0:1]), deps=list(deps) + dst.wdeps() + [kb.eps_op])
    bank.read(o1)
    o2 = p.op("dve", X("reciprocal", out=dst.t[:], in_=dst.t[:]), deps=[o1])
    return o2


def build_BD(T, D, DFF, tail, HD=128, NKV=4, NQH=32, NG=96):
    nc = bass.Bass("TRN2", target_bir_lowering=False)
    KD = D // 128
    KF = DFF // 128
    NP = T // 512
    JR = 8
    NR = KF // JR
    with ExitStack() as ctx:
        kb = KB(nc, ctx)
        p = kb.p
        xT_d = kb.din("xT", [D, T], F32)
        aT_d = kb.din("aT", [D, T], BF16)
        w_o = kb.din("w_o", [D, D], F32)
        w_up = kb.din("w_up", [D, DFF], F32)
        w_dn = kb.din("w_dn", [DFF, D], F32)
        g_d = kb.din("gains", [128, 3 * KD], F32)
        xo_d = kb.dout("xoT", [D, T], F32)
        if tail:
            KVD = NKV * HD
            w_qg = kb.din("w_qg", [D, D + NG], F32)
            w_kv = kb.din("w_kv", [D, 6 * KVD], F32)
            kng_d = kb.din("kng", [128, 3], F32)
            qng_d = kb.din("qng", [128, 3], F32)
            qT_o = kb.dout("qT", [D, T], BF16)
            gT_o = kb.dout("gT", [NG, T], F32)
            kcT_o = kb.dout("kcT", [KVD, T], BF16)
            vcT_o = kb.dout("vcT", [KVD, T], BF16)
            ksT_o = kb.dout("ksT", [KVD, T], BF16)
            kwT_o = kb.dout("kwT", [KVD, T], BF16)
            vs_o = kb.dout("vs", [T, KVD], BF16)
            vw_o = kb.dout("vw", [T, KVD], BF16)

        XT = Buf(kb.sb("XT", [128, KD, 512], F32))
        XN = Buf(kb.sb("XN", [128, KD, 512], BF16))
        HT = kb.sbring("HT", 2, [128, JR, 512], BF16)
        wA = kb.sbring("wA", 3, [128, KD, 256], BF16)
        wB = kb.sbring("wB", 2, [128, JR, 512], BF16)
        ws = WStream(kb, 2, (JR // 2) * 512)
        sqr = kb.sbring("SQ", 3, [128, 512], BF16)
        sqf = kb.sbring("SQF", 2, [128, 512], F32)
        R = Buf(kb.sb("R", [128, 512], F32))
        gains = kb.sb("gains_sb", [128, 3 * KD], F32)
        ones = kb.sb("ones", [128, 128], BF16)
        psA = kb.psring("psA", 3)
        psD = kb.psring("psD", 4)
        psS = Buf(kb.psum("psS", [128, 512]))
        if tail:
            kgain = kb.sb("kgain_sb", [128, 2], F32)
            w0, w1 = wB.bufs[0].t, wB.bufs[1].t

            def f32view(t, j0):
                return t[:, j0:j0 + 2, :].bitcast(F32).rearrange("p a b -> p (a b)")
            Rq = Ring([Buf(f32view(w0, 0)), Buf(f32view(w0, 2))])
            GB = Buf(f32view(w0, 4))
            OB = Ring([Buf(w1[:, j, :]) for j in range(3)])
            tail_alias = [(wB.bufs[0], Rq.bufs + [GB]), (wB.bufs[1], OB.bufs)]

        c_ones = p.op("dve", X("memset", ones[:], 1.0))
        kb.eps = kb.sb("eps_sb", [128, 1], F32)
        kb.eps_op = p.op("dve", X("memset", kb.eps[:], EPS))
        d_g = p.dma("sp", gains[:], g_d[:, :], ring="misc", nslots=4)
        if tail:
            kng = kb.sb("kng_sb", [128, 3], F32)
            qng = kb.sb("qng_sb", [128, 3], F32)
            d_k1 = p.dma("sp", kng[:], kng_d[:, :], ring="misc", nslots=4)
            d_k2 = p.dma("sp", qng[:], qng_d[:, :], ring="misc", nslots=4)
            o_ = p.op("dve", X("tensor_tensor", out=kgain[:], in0=kng[:, 1:3], in1=qng[:, 1:3], op=ALU.mult), deps=[d_k1, d_k2])
            d_kg = p.op("dve", X("tensor_scalar", out=kgain[:], in0=kgain[:], scalar1=HD ** -0.5, scalar2=None, op0=ALU.mult), deps=[o_])

        w_o_v = w_o.rearrange("(k p) n -> p k n", p=128)
        w_up_v = w_up.rearrange("(k p) n -> p k n", p=128)
        w_dn_v = w_dn.rearrange("(j p) n -> p j n", p=128)
        xT_v = xT_d.rearrange("(k p) t -> p k t", p=128)
        aT_v = aT_d.rearrange("(k p) t -> p k t", p=128)
        xo_v = xo_d.rearrange("(k p) t -> p k t", p=128)

        def load_wA(src_v, c0, ncols=256):
            s = wA.next()
            o = p.dma("pool", s.t[:, :, 0:ncols], src_v[:, :, c0:c0 + ncols], deps=s.wdeps(), ring="wA", nslots=3)
            s.wrote(o)
            return s

        out_dmas = []
        for ps_ in range(NP):
            tsl = slice(ps_ * 512, (ps_ + 1) * 512)
            ld = []
            for q in range(4):
                ks = slice(q * KD // 4, (q + 1) * KD // 4)
                ld.append(p.dma("sp", XT.t[:, ks, :], xT_v[:, ks, tsl], deps=XT.wdeps(), ring="xin", nslots=4))
            XT.wrote(ld[0])
            for o_ in ld[1:]:
                XT.also_wrote(o_)
            xt_ready = list(ld)
            la = []
            for q in range(2):
                ks = slice(q * KD // 2, (q + 1) * KD // 2)
                la.append(p.dma("sp", XN.t[:, ks, :], aT_v[:, ks, tsl], deps=XN.wdeps(), ring="ain", nslots=2))
            XN.wrote(la[0])
            XN.also_wrote(la[1])
            xn_ready = list(la)
            adds = []
            for mg in range(KD // 2):
                s = load_wA(w_o_v, mg * 256)
                for mi in range(2):
                    m = 2 * mg + mi
                    bank = psA.next()
                    mm = mm_group(kb, bank, [(s.t[:, k, mi * 128:(mi + 1) * 128], XN.t[:, k, :], (s.rdeps() + xn_ready + [c_ones]) if k == 0 else []) for k in range(KD)])
                    s.read(mm)
                    XN.read(mm)
                    o = p.op("dve", X("tensor_tensor", out=XT.t[:, m, :], in0=XT.t[:, m, :], in1=bank.t[:], op=ALU.add), deps=[mm] + xt_ready)
                    bank.read(o)
                    adds.append(o)
            XT.wrote(adds[-1])

            def norm_into_XN(goff, xt_dep):
                r_op = rms_stats(kb, XT, KD, D, ones, sqr, psS, R, xt_dep)
                last = None
                for k in range(KD):
                    eng = "dve"
                    last_ = p.op(eng, X("scalar_tensor_tensor", out=XN.t[:, k, :], in0=XT.t[:, k, :], scalar=gains[:, goff + k:goff + k + 1], in1=R.t[:], op0=ALU.mult, op1=ALU.mult),
                                 deps=[r_op, d_g] + XN.wdeps() + xt_dep)
                    last = last_
                R.read(last)
                return [last, last]

            xn_ops = norm_into_XN(0, [adds[-1]])
            XN.wrote(xn_ops[0])
            xn_ready = list(xn_ops)

            hts = [None] * NR
            dn_adds = [adds[-1]]

            def up_round(r):
                ht = HT.next()
                evs = []
                for jg in range(JR // 2):
                    s = load_wA(w_up_v, (r * JR + 2 * jg) * 128)
                    for ji in range(2):
                        jl = 2 * jg + ji
                        bank = psA.next()
                        mm = mm_group(kb, bank, [(s.t[:, k, ji * 128:(ji + 1) * 128], XN.t[:, k, :], (s.rdeps() + xn_ready) if k == 0 else []) for k in range(KD)])
                        s.read(mm)
                        XN.read(mm)
                        sf = sqf.next()
                        o0 = p.op("act", X("activation", out=sf.t[:], in_=bank.t[:], func=AF.Square), deps=[mm] + sf.wdeps())
                        sf.wrote(o0)
                        o = p.op("dve", X("scalar_tensor_tensor", out=ht.t[:, jl, :], in0=bank.t[:], scalar=0.0, in1=sf.t[:], op0=ALU.is_gt, op1=ALU.mult),
                                 deps=[mm, o0] + ht.wdeps())
                        sf.read(o)
                        bank.read(o0)
                        bank.read(o)
                        evs.append(o)
                ht.wrote(evs[-1])
                hts[r] = ht

            def down_round(r):
                ht = hts[r]
                for mg in range(KD // 4):
                    s = wB.next()
                    wd_ = s.wdeps()
                    jh = JR // 2
                    ws.load(s.t[:, 0:jh, :], w_dn_v[:, r * JR:r * JR + jh, mg * 512:(mg + 1) * 512], jh, 512, wd_, s, True)
                    ws.load(s.t[:, jh:JR, :], w_dn_v[:, r * JR + jh:(r + 1) * JR, mg * 512:(mg + 1) * 512], jh, 512, wd_, s, False)
                    banks = [psD.next() for _ in range(4)]
                    last = None
                    for jl in range(JR):
                        for mi in range(4):
                            d = []
                            if jl == 0:
                                d = banks[mi].wdeps()
                                if mi == 0:
                                    d = d + s.rdeps() + ht.rdeps()
                            last = p.op("pe", X("matmul", banks[mi].t[:], lhsT=s.t[:, jl, mi * 128:(mi + 1) * 128], rhs=ht.t[:, jl, :], start=(jl == 0), stop=(jl == JR - 1)), deps=d)
                    s.read(last)
                    ht.read(last)
                    for mi in range(4):
                        banks[mi].wrote(last)
                        m = 4 * mg + mi
                        eng = "dve" if mi % 2 == 0 else "pool"
                        eng = "dve"
                        o = p.op(eng, X("tensor_tensor", out=XT.t[:, m, :], in0=XT.t[:, m, :], in1=banks[mi].t[:], op=ALU.add), deps=[last] + dn_adds[-1:])
                        banks[mi].read(o)
                        dn_adds.append(o)

            for r in range(NR + 1):
                if r < NR:
                    up_round(r)
                if r >= 1:
                    down_round(r - 1)
            XT.wrote(dn_adds[-1])
            xt_ready = [dn_adds[-1]]
            XN.read(dn_adds[-1])
            for q in range(4):
                ks = slice(q * KD // 4, (q + 1) * KD // 4)
                o = p.dma("sp", xo_v[:, ks, tsl], XT.t[:, ks, :], deps=xt_ready, ring="xout", nslots=4)
                XT.read(o)
                out_dmas.append(o)

            if tail:
                for wslot, als in tail_alias:
                    for al in als:
                        for op_ in wslot.wdeps():
                            al.read(op_)
                KVD = NKV * HD
                w_qg_v = w_qg.rearrange("(k p) n -> p k n", p=128)
                w_kv_v = w_kv.rearrange("(k p) n -> p k n", p=128)
                xn_ops = norm_into_XN(KD, xt_ready)
                XN.wrote(xn_ops[0])
                xn_ready = list(xn_ops)
                pend = None

                def finish_q(item):
                    bank, sq, hd, o_sq = item
                    mm2 = mm_group(kb, psS, [(ones[:], sq.t[:], [o_sq])])
                    sq.read(mm2)
                    rq = Rq.next()
                    o2 = rstd_from(kb, psS, rq, 1.0 / HD, [mm2])
                    ob = OB.next()
                    o3 = p.op("dve", X("tensor_tensor", out=ob.t[:], in0=bank.t[:], in1=rq.t[:], op=ALU.mult), deps=[o2] + ob.wdeps())
                    rq.wrote(o3)
                    bank.read(o3)
                    ob.wrote(o3)
                    o4 = p.dma("sp", qT_o[hd * 128:(hd + 1) * 128, tsl], ob.t[:], deps=[o3], ring="oq", nslots=3)
                    ob.read(o4)
                    out_dmas.append(o4)

                for hg in range(NQH // 2):
                    s = load_wA(w_qg_v, hg * 256)
                    for hi in range(2):
                        hd = 2 * hg + hi
                        bank = psA.next()
                        mm = mm_group(kb, bank, [(s.t[:, k, hi * 128:(hi + 1) * 128], XN.t[:, k, :], (s.rdeps() + xn_ready) if k == 0 else []) for k in range(KD)])
                        s.read(mm)
                        XN.read(mm)
                        sq = sqr.next()
                        o_sq = p.op("act", X("activation", out=sq.t[:], in_=bank.t[:], func=AF.Square), deps=[mm] + sq.wdeps())
                        sq.wrote(o_sq)
                        bank.read(o_sq)
                        if pend is not None:
                            finish_q(pend)
                        pend = (bank, sq, hd, o_sq)
                s = load_wA(w_qg_v, D, NG)
                bank = psA.next()
                mm = p_last = None
                n = KD
                d0 = bank.wdeps() + s.rdeps() + xn_ready
                for k in range(KD):
                    mm = p.op("pe", X("matmul", bank.t[0:NG, :], lhsT=s.t[:, k, 0:NG], rhs=XN.t[:, k, :], start=(k == 0), stop=(k == n - 1)), deps=d0 if k == 0 else [])
                bank.wrote(mm)
                s.read(mm)
                XN.read(mm)
                finish_q(pend)
                o = p.op("act", X("activation", out=GB.t[0:NG, :], in_=bank.t[0:NG, :], func=AF.Sigmoid), deps=[mm] + GB.wdeps())
                bank.read(o)
                GB.wrote(o)
                o2 = p.dma("sp", gT_o[:, tsl], GB.t[0:NG, :], deps=[o], ring="og", nslots=1)
                GB.read(o2)
                out_dmas.append(o2)
                last = None
                for k in range(KD):
                    eng = "dve"
                    o = p.op(eng, X("scalar_tensor_tensor", out=XN.t[:, k, :], in0=XT.t[:, k, :], scalar=gains[:, 2 * KD + k:2 * KD + k + 1], in1=R.t[:], op0=ALU.mult, op1=ALU.mult),
                             deps=XN.wdeps() + xt_ready + R.rdeps())
                    last = o
                XN.wrote(last)
                R.read(last)
                xn_ready = [last]
                pend = None

                def finish_k(item):
                    bank, sq, o_sq, dst, gi, row0 = item
                    mm2 = mm_group(kb, psS, [(ones[:], sq.t[:], [o_sq])])
                    sq.read(mm2)
                    rq = Rq.next()
                    o2 = rstd_from(kb, psS, rq, 1.0 / HD, [mm2])
                    ob = OB.next()
                    o3 = p.op("dve", X("scalar_tensor_tensor", out=ob.t[:], in0=bank.t[:], scalar=kgain[:, gi:gi + 1], in1=rq.t[:], op0=ALU.mult, op1=ALU.mult), deps=[o2, d_kg] + ob.wdeps())
                    rq.wrote(o3)
                    bank.read(o3)
                    ob.wrote(o3)
                    o4 = p.dma("sp", dst[row0:row0 + 128, tsl], ob.t[:], deps=[o3], ring="oq", nslots=3)
                    ob.read(o4)
                    out_dmas.append(o4)

                for (j, dst, gi) in ((0, kcT_o, None), (1, vcT_o, None), (2, ksT_o, 0), (4, kwT_o, 1)):
                    for gg in range(NKV // 2):
                        s = load_wA(w_kv_v, j * KVD + gg * 256)
                        for gi2 in range(2):
                            g = 2 * gg + gi2
                            bank = psA.next()
                            mm = mm_group(kb, bank, [(s.t[:, k, gi2 * 128:(gi2 + 1) * 128], XN.t[:, k, :], (s.rdeps() + xn_ready) if k == 0 else []) for k in range(KD)])
                            s.read(mm)
                            XN.read(mm)
                            if gi is None:
                                ob = OB.next()
                                o3 = p.op("act", X("copy", out=ob.t[:], in_=bank.t[:]), deps=[mm] + ob.wdeps())
                                bank.read(o3)
                                ob.wrote(o3)
                                o4 = p.dma("sp", dst[g * 128:(g + 1) * 128, tsl], ob.t[:], deps=[o3], ring="oq", nslots=3)
                                ob.read(o4)
                                out_dmas.append(o4)
                            else:
                                sq = sqr.next()
                                o_sq = p.op("act", X("activation", out=sq.t[:], in_=bank.t[:], func=AF.Square), deps=[mm] + sq.wdeps())
                                sq.wrote(o_sq)
                                bank.read(o_sq)
                                if pend is not None:
                                    finish_k(pend)
                                pend = (bank, sq, o_sq, dst, gi, g * 128)
                if pend is not None:
                    finish_k(pend)
                for (j, dst) in ((3, vs_o), (5, vw_o)):
                    for half in range(KVD // 256):
                        s = load_wA(w_kv_v, j * KVD + half * 256)
                        for tt in range(4):
                            bank = psA.next()
                            mm = mm_group(kb, bank, [(XN.t[:, k, tt * 128:(tt + 1) * 128], s.t[:, k, :], (s.rdeps() + xn_ready) if k == 0 else []) for k in range(KD)], out_ap=bank.t[:, 0:256])
                            s.read(mm)
                            XN.read(mm)
                            ob = OB.next()
                            o3 = p.op("act", X("copy", out=ob.t[:, 0:256], in_=bank.t[:, 0:256]), deps=[mm] + ob.wdeps())
                            bank.read(o3)
                            ob.wrote(o3)
                            o4 = p.dma("sp", dst[ps_ * 512 + tt * 128:ps_ * 512 + (tt + 1) * 128, half * 256:(half + 1) * 256], ob.t[:, 0:256], deps=[o3], ring="oq", nslots=3)
                            ob.read(o4)
                            out_dmas.append(o4)
            if tail:
                for wslot, als in tail_alias:
                    for al in als:
                        for op_ in al.wdeps():
                            wslot.read(op_)
        p.wait("sp", out_dmas)
        p.emit(ctx)
    return nc


def build_A(S, D, DK=256, DV=512, GATE_CAP=15.0):
    nc = bass.Bass("TRN2", target_bir_lowering=False)
    KD = D // 128
    NB = S // 512
    NH = 2
    DKC = DK // 128
    DVC = DV // 128
    QC = NH * DKC
    OC = NH * DVC
    with ExitStack() as ctx:
        kb = KB(nc, ctx)
        p = kb.p
        xT_d = kb.din("xT", [D, S], F32)
        ga_d = kb.din("ga", [128, KD], F32)
        w_q = kb.din("w_q", [D, NH * DK], F32)
        w_k = kb.din("w_k", [D, NH * DK], F32)
        w_v = kb.din("w_v", [D, NH * DV], F32)
        w_og = kb.din("w_og", [D, NH * DV], F32)
        w_g = kb.din("w_g", [D, 4], F32)
        bg_d = kb.din("bg", [2, 2], F32)
        hg_d = kb.din("hgain", [128, OC], F32)
        cc_d = kb.din("c_causal", [128, 64], F32)
        cs_d = kb.din("c_seg", [2, 1024], F32)
        csel_d = kb.din("c_sel", [2, 256], F32)
        ci2_d = kb.din("c_i2", [2, 2], F32)
        out_d = kb.dout("hgT", [NH * DV, S], BF16)

        XTr = kb.sbring("XTr", 2, [128, 4, 512], F32)
        XN = Buf(kb.sb("XN", [128, KD, 512], BF16))
        wA = kb.sbring("wA", 3, [128, KD, 256], BF16)
        sqr = kb.sbring("SQ", 3, [128, 512], BF16)
        R = Buf(kb.sb("R", [128, 512], F32))
        ga = kb.sb("ga_sb", [128, KD], F32)
        hgain = kb.sb("hgain_sb", [128, OC], F32)
        ones = kb.sb("ones", [128, 128], BF16)
        causal = kb.sb("causal", [128, 64], F32)
        cseg = kb.sb("cseg", [2, 512], F32)
        csel = kb.sb("csel", [2, 256], F32)
        ci2 = kb.sb("ci2", [2, 2], F32)
        bg = kb.sb("bg_sb", [2, 2], F32)
        bg15 = kb.sb("bg15", [2, 2], F32)
        one1 = kb.sb("one1", [2, 1], F32)
        QT = Buf(kb.sb("QT", [128, QC, 512], BF16))
        KT = Buf(kb.sb("KT", [128, QC, 512], BF16))
        KG = Buf(kb.sb("KG", [128, 4, NH, DK], BF16))
        VT = Buf(kb.sb("VT", [128, 4, NH, DV], BF16))
        OG = Buf(kb.sb("OG", [128, OC, 512], BF16))
        HG = Buf(kb.sb("HG", [128, OC, 512], BF16))
        OT = kb.sbring("OT", 2, [128, 512], F32)
        C32 = [Buf(kb.sb("C32_%d" % i, [128, 512], F32)) for i in range(QC)]
        Cd = [Buf(kb.sb("Cd_%d" % i, [128, 512], F32)) for i in range(QC)]
        Cb = [Buf(kb.sb("Cb_%d" % i, [128, 512], BF16)) for i in range(QC)]
        n32 = [Buf(kb.sb("n32_%d" % i, [128, 2], F32)) for i in range(QC)]
        nrep = [Buf(kb.sb("nrep_%d" % i, [128, 128], BF16)) for i in range(QC)]
        EMWb = Buf(kb.sb("EMWb", [128, NH, 512], F32))
        G128 = Buf(kb.sb("G128", [128, 4, NH], F32))
        SC = Buf(kb.sb("SCk", [128, 4, NH], F32))
        DECb = Buf(kb.sb("DECb", [128, 8, NH], F32))
        rcol = Buf(kb.sb("rcol", [128, 8], F32))
        SW = kb.sbring("SW", 3, [128, 64], BF16)
        DN = kb.sbring("DN", 3, [128, 64], F32)
        HTt = kb.sbring("HTt", 3, [128, DVC, 64], F32)
        HSQ = kb.sbring("HSQ", 3, [128, DVC, 64], BF16)
        RS = kb.sbring("RS", 3, [128, 64], F32)
        GV = [Buf(kb.sb("GV%d" % i, [2, 512], F32)) for i in range(6)]
        gs = Buf(kb.sb("gsmall", [2, 64], F32))
        mprev = Buf(kb.sb("mprev", [2, 1], F32))

        psA = kb.psring("psA", 3)
        psS = Buf(kb.psum("psS", [128, 512]))
        psc = kb.psring("psc", 2)
        psC = kb.psring("psC", 2)

        c_ones = p.op("dve", X("memset", ones[:], 1.0))
        kb.eps = kb.sb("eps_sb", [128, 1], F32)
        kb.eps_op = p.op("dve", X("memset", kb.eps[:], EPS))
        c_one1 = p.op("dve", X("memset", one1[:], 1.0))
        c_mp = p.op("dve", X("memset", mprev.t[:], 0.0))
        mprev.wrote(c_mp)
        cinit = []
        for i in range(QC):
            o = p.op("pool", X("memset", C32[i].t[:], 0.0))
            C32[i].wrote(o)
            o = p.op("pool", X("memset", Cb[i].t[:], 0.0))
            Cb[i].wrote(o)
            o = p.op("pool", X("memset", n32[i].t[:], 0.0))
            n32[i].wrote(o)
            o = p.op("pool", X("memset", nrep[i].t[:], 0.0))
            nrep[i].wrote(o)
        dc = [p.dma("sp", ga[:], ga_d[:, :], ring="misc", nslots=8),
              p.dma("sp", hgain[:], hg_d[:, :], ring="misc", nslots=8),
              p.dma("sp", causal[:], cc_d[:, :], ring="misc", nslots=8),
              p.dma("sp", cseg[:], cs_d[:, 0:512], ring="misc", nslots=8),
              p.dma("sp", csel[:], csel_d[:, :], ring="misc", nslots=8),
              p.dma("sp", ci2[:], ci2_d[:, :], ring="misc", nslots=8),
              p.dma("sp", bg[:], bg_d[:, :], ring="misc", nslots=8)]
        c_bg15 = p.op("dve", X("tensor_scalar", out=bg15[:], in0=bg[:], scalar1=1.0 / GATE_CAP, scalar2=None, op0=ALU.mult), deps=[dc[6]])

        xT_v = xT_d.rearrange("(k p) t -> p k t", p=128)
        wv = {n: a.rearrange("(k p) n -> p k n", p=128) for n, a in (("q", w_q), ("k", w_k), ("v", w_v), ("o", w_og), ("g", w_g))}
        out_v = out_d.rearrange("(c p) t -> p c t", p=128)

        def load_wA(name, c0, ncols=256):
            s = wA.next()
            o = p.dma("pool", s.t[:, :, 0:ncols], wv[name][:, :, c0:c0 + ncols], deps=s.wdeps(), ring="wA", nslots=3)
            s.wrote(o)
            return s

        def proj_fm(s, c0, ncols=128, M=None):
            bank = psA.next()
            d0 = bank.wdeps() + s.rdeps() + XN.rdeps() + [c_ones]
            mm = None
            for k in range(KD):
                mm = p.op("pe", X("matmul", bank.t[0:ncols, :], lhsT=s.t[:, k, c0:c0 + ncols], rhs=XN.t[:, k, :], start=(k == 0), stop=(k == KD - 1)), deps=d0 if k == 0 else [])
            bank.wrote(mm)
            s.read(mm)
            XN.read(mm)
            return bank, mm

        def proj_tm(s, tt):
            bank = psA.next()
            d0 = bank.wdeps() + s.rdeps() + XN.rdeps()
            mm = None
            for k in range(KD):
                mm = p.op("pe", X("matmul", bank.t[:, 0:256], lhsT=XN.t[:, k, tt * 128:(tt + 1) * 128], rhs=s.t[:, k, :], start=(k == 0), stop=(k == KD - 1)), deps=d0 if k == 0 else [])
            bank.wrote(mm)
            s.read(mm)
            XN.read(mm)
            return bank, mm

        out_dmas = []
        pend = [None]

        for bi in range(NB):
            tsl = slice(bi * 512, (bi + 1) * 512)
            sq_ops = []
            first = True
            mm = None
            for grp in range(KD // 4):
                xs = XTr.next()
                ld = p.dma("sp", xs.t[:], xT_v[:, 4 * grp:4 * grp + 4, tsl], deps=xs.wdeps(), ring="xin", nslots=2)
                xs.wrote(ld)
                for kk in range(4):
                    k = 4 * grp + kk
                    sq = sqr.next()
                    o = p.op("act", X("activation", out=sq.t[:], in_=xs.t[:, kk, :], func=AF.Square), deps=sq.wdeps() + [ld])
                    sq.wrote(o)
                    xs.read(o)
                    d = [o, c_ones]
                    if k == 0:
                        d += psS.wdeps()
                    mm = p.op("pe", X("matmul", psS.t[:], lhsT=ones[:], rhs=sq.t[:], start=(k == 0), stop=(k == KD - 1)), deps=d)
                    sq.read(mm)
                    o2 = p.op("dve", X("tensor_scalar", out=XN.t[:, k, :], in0=xs.t[:, kk, :], scalar1=ga[:, k:k + 1], scalar2=None, op0=ALU.mult), deps=[ld, dc[0]] + XN.wdeps())
                    xs.read(o2)
                    if first:
                        XN.wrote(o2)
                        first = False
                    else:
                        XN.also_wrote(o2)
            psS.wrote(mm)
            r_op = rstd_from(kb, psS, R, 1.0 / D, [mm])
            R.wrote(r_op)
            bank = psA.next()
            d0 = bank.wdeps() + [r_op, c_one1]
            for tt in range(4):
                mm = p.op("pe", X("matmul", bank.t[:, tt:tt + 1], lhsT=R.t[0:1, tt * 128:(tt + 1) * 128], rhs=one1[0:1, 0:1], start=True, stop=True), deps=d0 if tt == 0 else [])
            bank.wrote(mm)
            R.read(mm)
            o = p.op("dve", X("tensor_copy", out=rcol.t[:, 0:4], in_=bank.t[:, 0:4]), deps=[mm] + rcol.wdeps())
            bank.read(o)
            o = p.op("dve", X("tensor_scalar", out=rcol.t[:, 4:8], in0=rcol.t[:, 0:4], scalar1=DK ** -0.5, scalar2=None, op0=ALU.mult), deps=[o])
            rcol.wrote(o)

            s = load_wA("g", 0, 4)
            bank_i, mm_i = proj_fm(s, 0, 2)
            bank_f, mm_f = proj_fm(s, 2, 2)
            V = GV
            o = p.op("dve", X("tensor_tensor", out=V[0].t[:], in0=bank_i.t[0:2, :], in1=R.t[0:2, :], op=ALU.mult), deps=[mm_i, r_op] + V[0].wdeps())
            bank_i.read(o)
            o = p.op("act", X("activation", out=V[0].t[:], in_=V[0].t[:], func=AF.Tanh, scale=1.0 / GATE_CAP, bias=bg15[:, 0:1]), deps=[o, c_bg15])
            o_li = p.op("dve", X("tensor_scalar", out=V[0].t[:], in0=V[0].t[:], scalar1=GATE_CAP, scalar2=None, op0=ALU.mult), deps=[o])
            V[0].wrote(o_li)
            o = p.op("dve", X("tensor_tensor", out=V[1].t[:], in0=bank_f.t[0:2, :], in1=R.t[0:2, :], op=ALU.mult), deps=[mm_f, r_op] + V[1].wdeps())
            bank_f.read(o)
            R.read(o)
            o = p.op("act", X("activation", out=V[1].t[:], in_=V[1].t[:], func=AF.Tanh, scale=1.0 / GATE_CAP, bias=bg15[:, 1:2]), deps=[o, c_bg15])
            o = p.op("act", X("activation", out=V[1].t[:], in_=V[1].t[:], func=AF.Exp, scale=-GATE_CAP), deps=[o])
            o = p.op("act", X("activation", out=V[1].t[:], in_=V[1].t[:], func=AF.Ln, bias=one1[:, 0:1]), deps=[o, c_one1])
            o_lf = p.op("dve", X("tensor_scalar", out=V[1].t[:], in0=V[1].t[:], scalar1=-1.0, scalar2=None, op0=ALU.mult), deps=[o])
            V[1].wrote(o_lf)
            o_b = p.op("dve", X("tensor_tensor_scan", out=V[2].t[:], data0=cseg[:, 0:512], data1=V[1].t[:], initial=0.0, op0=ALU.mult, op1=ALU.add), deps=[o_lf, dc[3]] + V[2].wdeps())
            V[2].wrote(o_b)
            o_a = p.op("dve", X("tensor_tensor", out=V[3].t[:], in0=V[0].t[:], in1=V[2].t[:], op=ALU.subtract), deps=[o_li, o_b] + V[3].wdeps())
            V[3].wrote(o_a)
            b3 = V[2].t[:].rearrange("p (c s) -> p c s", s=64)
            a3 = V[3].t[:].rearrange("p (c s) -> p c s", s=64)
            o = p.op("dve", X("tensor_copy", out=gs.t[:, 0:8], in_=b3[:, :, 63]), deps=[o_b] + gs.wdeps())
            o = p.op("dve", X("tensor_reduce", out=gs.t[:, 8:16], in_=a3, axis=AX.X, op=ALU.max), deps=[o_a])
            o = p.op("dve", X("tensor_tensor", out=gs.t[:, 8:16], in0=gs.t[:, 8:16], in1=gs.t[:, 0:8], op=ALU.add), deps=[o])
            o = p.op("dve", X("tensor_tensor_scan", out=gs.t[:, 16:24], data0=gs.t[:, 0:8], data1=gs.t[:, 8:16], initial=mprev.t[:, 0:1], op0=ALU.add, op1=ALU.max), deps=[o] + mprev.rdeps())
            o = p.op("dve", X("tensor_copy", out=gs.t[:, 24:25], in_=mprev.t[:, 0:1]), deps=[o])
            o = p.op("dve", X("tensor_copy", out=gs.t[:, 25:32], in_=gs.t[:, 16:23]), deps=[o])
            o_mp = p.op("dve", X("tensor_copy", out=mprev.t[:, 0:1], in_=gs.t[:, 23:24]), deps=[o])
            mprev.wrote(o_mp)
            o = p.op("dve", X("tensor_tensor", out=gs.t[:, 32:40], in0=gs.t[:, 0:8], in1=gs.t[:, 24:32], op=ALU.add), deps=[o_mp])
            o = p.op("dve", X("tensor_tensor", out=gs.t[:, 32:40], in0=gs.t[:, 32:40], in1=gs.t[:, 16:24], op=ALU.subtract), deps=[o])
            o_dec = p.op("act", X("activation", out=gs.t[:, 32:40], in_=gs.t[:, 32:40], func=AF.Exp), deps=[o])
            gs.wrote(o_dec)
            mcur_b = gs.t[:, 24:32].unsqueeze(2).to_broadcast([2, 8, 64])
            g3 = V[4].t[:].rearrange("p (c s) -> p c s", s=64)
            o = p.op("dve", X("tensor_tensor", out=g3, in0=a3, in1=mcur_b, op=ALU.subtract), deps=[o_a, o_mp] + V[4].wdeps())
            o_g = p.op("act", X("activation", out=V[4].t[:], in_=V[4].t[:], func=AF.Exp), deps=[o])
            V[4].wrote(o_g)
            e3 = V[5].t[:].rearrange("p (c s) -> p c s", s=64)
            o = p.op("dve", X("tensor_tensor", out=e3, in0=b3, in1=mcur_b, op=ALU.add), deps=[o_b, o_mp] + V[5].wdeps())
            o_e = p.op("act", X("activation", out=V[5].t[:], in_=V[5].t[:], func=AF.Exp, scale=-1.0), deps=[o])
            V[5].wrote(o_e)
            bank = psA.next()
            d0 = bank.wdeps() + [o_g, dc[5]]
            for tt in range(4):
                mm = p.op("pe", X("matmul", bank.t[:, 2 * tt:2 * tt + 2], lhsT=V[4].t[0:2, tt * 128:(tt + 1) * 128], rhs=ci2[0:2, 0:2], start=True, stop=True), deps=d0 if tt == 0 else [])
            V[4].read(mm)
            for h in range(NH):
                mm = p.op("pe", X("matmul", bank.t[:, 16 + 8 * h:24 + 8 * h], lhsT=csel[0:2, h * 128:(h + 1) * 128], rhs=gs.t[0:2, 32:40], start=True, stop=True), deps=[o_dec, dc[4]])
            gs.read(mm)
            bank.wrote(mm)
            o = p.op("act", X("copy", out=G128.t[:].rearrange("p t h -> p (t h)"), in_=bank.t[:, 0:8]), deps=[mm] + G128.wdeps())
            G128.wrote(o)
            o2 = p.op("act", X("copy", out=DECb.t[:].rearrange("p c h -> p h c"), in_=bank.t[:, 16:32].rearrange("p (h c) -> p h c", h=NH)), deps=[mm] + DECb.wdeps())
            DECb.wrote(o2)
            bank.read(o2)
            o = p.op("dve", X("tensor_tensor", out=SC.t[:], in0=G128.t[:], in1=rcol.t[:, 4:8].unsqueeze(2).to_broadcast([128, 4, NH]), op=ALU.mult), deps=[o] + rcol.rdeps() + SC.wdeps())
            SC.wrote(o)
            G128.read(o)
            first = True
            for h in range(NH):
                bank = psA.next()
                mm = p.op("pe", X("matmul", bank.t[:], lhsT=csel[0:2, h * 128:(h + 1) * 128], rhs=V[5].t[0:2, :], start=True, stop=True), deps=bank.wdeps() + [o_e, dc[4]])
                bank.wrote(mm)
                V[5].read(mm)
                o = p.op("act", X("copy", out=EMWb.t[:, h, :], in_=bank.t[:]), deps=[mm] + EMWb.wdeps())
                bank.read(o)
                if first:
                    EMWb.wrote(o)
                    first = False
                else:
                    EMWb.also_wrote(o)

            fq = fk = fkg = fv = fo = True
            for h in range(NH):
                s = load_wA("q", h * 256)
                for dci in range(DKC):
                    bank, mm = proj_fm(s, dci * 128)
                    o = p.op("dve", X("tensor_tensor", out=QT.t[:, h * DKC + dci, :], in0=bank.t[:], in1=R.t[:], op=ALU.mult), deps=[mm, r_op] + QT.wdeps())
                    bank.read(o)
                    R.read(o)
                    QT.wrote(o) if fq else QT.also_wrote(o)
                    fq = False
            for h in range(NH):
                s = load_wA("k", h * 256)
                for dci in range(DKC):
                    bank, mm = proj_fm(s, dci * 128)
                    o = p.op("dve", X("scalar_tensor_tensor", out=KT.t[:, h * DKC + dci, :], in0=bank.t[:], scalar=DK ** -0.5, in1=R.t[:], op0=ALU.mult, op1=ALU.mult), deps=[mm, r_op] + KT.wdeps())
                    bank.read(o)
                    R.read(o)
                    KT.wrote(o) if fk else KT.also_wrote(o)
                    fk = False
                for tt in range(4):
                    bank, mm = proj_tm(s, tt)
                    o = p.op("act", X("activation", out=KG.t[:, tt, h, :], in_=bank.t[:, 0:256], func=AF.Copy, scale=SC.t[:, tt, h:h + 1]), deps=[mm] + SC.rdeps() + KG.wdeps())
                    bank.read(o)
                    SC.read(o)
                    KG.wrote(o) if fkg else KG.also_wrote(o)
                    fkg = False
            for h in range(NH):
                for half in range(DV // 256):
                    s = load_wA("v", h * DV + half * 256)
                    for tt in range(4):
                        bank, mm = proj_tm(s, tt)
                        o = p.op("act", X("activation", out=VT.t[:, tt, h, half * 256:(half + 1) * 256], in_=bank.t[:, 0:256], func=AF.Copy, scale=rcol.t[:, tt:tt + 1]), deps=[mm] + rcol.rdeps() + VT.wdeps())
                        bank.read(o)
                        rcol.read(o)
                        VT.wrote(o) if fv else VT.also_wrote(o)
                        fv = False
            for h in range(NH):
                for half in range(DV // 256):
                    s = load_wA("o", h * DV + half * 256)
                    for ci in range(2):
                        ec = half * 2 + ci
                        bank, mm = proj_fm(s, ci * 128)
                        ot = OT.next()
                        o1 = p.op("dve", X("tensor_tensor", out=ot.t[:], in0=bank.t[:], in1=R.t[:], op=ALU.mult), deps=[mm, r_op] + ot.wdeps())
                        bank.read(o1)
                        R.read(o1)
                        o2 = p.op("act", X("activation", out=ot.t[:], in_=ot.t[:], func=AF.Sigmoid), deps=[o1])
                        o3 = p.op("pool", X("tensor_scalar", out=OG.t[:, h * DVC + ec, :], in0=ot.t[:], scalar1=hgain[:, h * DVC + ec:h * DVC + ec + 1], scalar2=None, op0=ALU.mult), deps=[o2, dc[1]] + OG.wdeps())
                        ot.wrote(o3)
                        OG.wrote(o3) if fo else OG.also_wrote(o3)
                        fo = False

            fhg = [True]

            def finish(item):
                (pc, ht, h, c, o_ht) = item
                hs = HSQ.next()
                o_sq = p.op("act", X("activation", out=hs.t[:], in_=ht.t[:], func=AF.Square), deps=[o_ht] + hs.wdeps())
                hs.wrote(o_sq)
                mm = None
                for ec in range(DVC):
                    mm = p.op("pe", X("matmul", pc.t[:, 384:448], lhsT=ones[:], rhs=hs.t[:, ec, :], start=(ec == 0), stop=(ec == DVC - 1)), deps=([o_sq] + pc.wdeps()) if ec == 0 else [])
                hs.read(mm)
                pc.also_wrote(mm)
                rs = RS.next()
                o1 = p.op("act", X("activation", out=rs.t[:], in_=pc.t[:, 384:448], func=AF.Sqrt, scale=1.0 / DV, bias=kb.eps[:, 0:1]), deps=[mm, kb.eps_op] + rs.wdeps())
                pc.read(o1)
                o2 = p.op("dve", X("reciprocal", out=rs.t[:], in_=rs.t[:]), deps=[o1])
                o3 = p.op("dve", X("tensor_tensor", out=ht.t[:], in0=ht.t[:], in1=rs.t[:].unsqueeze(1).to_broadcast([128, DVC, 64]), op=ALU.mult), deps=[o2, o_sq])
                rs.wrote(o3)
                o4 = p.op("dve", X("tensor_tensor", out=HG.t[:, h * DVC:(h + 1) * DVC, c * 64:(c + 1) * 64], in0=ht.t[:], in1=OG.t[:, h * DVC:(h + 1) * DVC, c * 64:(c + 1) * 64], op=ALU.mult), deps=[o3] + OG.rdeps() + HG.wdeps())
                ht.wrote(o4)
                OG.read(o4)
                HG.wrote(o4) if fhg[0] else HG.also_wrote(o4)
                fhg[0] = False

            for c in range(8):
                tt = c // 2
                pb = 64 * (c % 2)
                csl = slice(c * 64, (c + 1) * 64)
                tl = slice(tt * 128, (tt + 1) * 128)
                for h in range(NH):
                    pc = psc.next()
                    qs = [h * DKC + i for i in range(DKC)]
                    d0 = pc.wdeps() + QT.rdeps() + KT.rdeps()
                    mm = None
                    for i, qi in enumerate(qs):
                        mm = p.op("pe", X("matmul", pc.t[:, 0:64], lhsT=KT.t[:, qi, tl], rhs=QT.t[:, qi, csl], start=(i == 0), stop=(i == DKC - 1)), deps=d0 if i == 0 else [])
                    pc.wrote(mm)
                    QT.read(mm)
                    KT.read(mm)
                    sw = SW.next()
                    o_sw = p.op("dve", X("scalar_tensor_tensor", out=sw.t[pb:pb + 64, :], in0=pc.t[pb:pb + 64, 0:64], scalar=G128.t[pb:pb + 64, tt, h:h + 1], in1=causal[pb:pb + 64, :], op0=ALU.mult, op1=ALU.mult),
                                deps=[mm, dc[2]] + G128.rdeps() + sw.wdeps())
                    sw.wrote(o_sw)
                    pc.read(o_sw)
                    d0 = [o_sw] + VT.rdeps() + Cb[qs[0]].rdeps() + Cb[qs[1]].rdeps() + nrep[qs[0]].rdeps() + nrep[qs[1]].rdeps()
                    for ec in range(DVC):
                        for i, qi in enumerate(qs):
                            mm = p.op("pe", X("matmul", pc.t[:, 64 + 64 * ec:128 + 64 * ec], lhsT=Cb[qi].t[:, ec * 128:(ec + 1) * 128], rhs=QT.t[:, qi, csl], start=(i == 0), stop=False), deps=d0 if (ec == 0 and i == 0) else [])
                        mm = p.op("pe", X("matmul", pc.t[:, 64 + 64 * ec:128 + 64 * ec], lhsT=VT.t[pb:pb + 64, tt, h, ec * 128:(ec + 1) * 128], rhs=sw.t[pb:pb + 64, :], start=False, stop=True))
                    for i, qi in enumerate(qs):
                        mm = p.op("pe", X("matmul", pc.t[:, 320:384], lhsT=nrep[qi].t[:], rhs=QT.t[:, qi, csl], start=(i == 0), stop=False))
                    mm_nd = p.op("pe", X("matmul", pc.t[:, 320:384], lhsT=ones[pb:pb + 64, :], rhs=sw.t[pb:pb + 64, :], start=False, stop=True))
                    pc.also_wrote(mm_nd)
                    sw.read(mm_nd)
                    for qi in qs:
                        Cb[qi].read(mm_nd)
                        nrep[qi].read(mm_nd)
                    VT.read(mm_nd)
                    QT.read(mm_nd)
                    mmcs = []
                    bcs = []
                    for i, qi in enumerate(qs):
                        bc = psC.next()
                        d0 = bc.wdeps() + KG.rdeps() + VT.rdeps()
                        mmc = p.op("pe", X("matmul", bc.t[:], lhsT=KG.t[pb:pb + 64, tt, h, i * 128:(i + 1) * 128], rhs=VT.t[pb:pb + 64, tt, h, :], start=True, stop=True), deps=d0)
                        bc.wrote(mmc)
                        mmcs.append(mmc)
                        bcs.append(bc)
                    mmn = None
                    for i, qi in enumerate(qs):
                        mmn = p.op("pe", X("matmul", pc.t[:, 448 + i:449 + i], lhsT=KG.t[pb:pb + 64, tt, h, i * 128:(i + 1) * 128], rhs=ones[pb:pb + 64, 0:1], start=True, stop=True))
                    pc.also_wrote(mmn)
                    KG.read(mmn)
                    VT.read(mmn)
                    for i, qi in enumerate(qs):
                        bc = bcs[i]
                        mmc = mmcs[i]
                        dcol = DECb.t[:, c, h:h + 1]
                        o_cd = p.op("act", X("activation", out=Cd[qi].t[:], in_=C32[qi].t[:], func=AF.Copy, scale=dcol), deps=Cd[qi].wdeps() + C32[qi].rdeps() + DECb.rdeps())
                        Cd[qi].wrote(o_cd)
                        o_c = p.op("dve", X("scalar_tensor_tensor", out=C32[qi].t[:], in0=bc.t[:], scalar=dcol, in1=Cd[qi].t[:], op0=ALU.mult, op1=ALU.add), deps=[mmc, o_cd] + DECb.rdeps() + C32[qi].wdeps())
                        bc.read(o_c)
                        Cd[qi].read(o_c)
                        C32[qi].wrote(o_c)
                        o_cb = p.op("pool", X("tensor_copy", out=Cb[qi].t[:], in_=C32[qi].t[:]), deps=[o_c] + Cb[qi].wdeps())
                        Cb[qi].wrote(o_cb)
                        C32[qi].read(o_cb)
                        o_n1 = p.op("dve", X("tensor_tensor", out=n32[qi].t[:, 0:1], in0=n32[qi].t[:, 0:1], in1=pc.t[:, 448 + i:449 + i], op=ALU.add), deps=[mmn] + n32[qi].wdeps())
                        o_n2 = p.op("dve", X("tensor_tensor", out=n32[qi].t[:, 0:1], in0=n32[qi].t[:, 0:1], in1=dcol, op=ALU.mult), deps=[o_n1])
                        n32[qi].wrote(o_n2)
                        pc.read(o_n1)
                        o_nr = p.op("pool", X("tensor_copy", out=nrep[qi].t[:], in_=n32[qi].t[:, 0:1].to_broadcast([128, 128])), deps=[o_n2] + nrep[qi].wdeps())
                        nrep[qi].wrote(o_nr)
                        n32[qi].read(o_nr)
                    DECb.read(o_c)
                    dn = DN.next()
                    o0 = p.op("act", X("activation", out=dn.t[:], in_=pc.t[:, 320:384], func=AF.Abs), deps=[mm_nd, mmn] + dn.wdeps())
                    pc.read(o0)
                    o1 = p.op("dve", X("tensor_tensor", out=dn.t[:], in0=dn.t[:], in1=EMWb.t[:, h, csl], op=ALU.max), deps=[o0] + EMWb.rdeps())
                    EMWb.read(o1)
                    o2 = p.op("dve", X("reciprocal", out=dn.t[:], in_=dn.t[:]), deps=[o1])
                    ht = HTt.next()
                    o_ht = p.op("dve", X("tensor_tensor", out=ht.t[:], in0=pc.t[:, 64:320].rearrange("p (e t) -> p e t", e=DVC), in1=dn.t[:].unsqueeze(1).to_broadcast([128, DVC, 64]), op=ALU.mult), deps=[o2, mm_nd, mmn] + ht.wdeps())
                    dn.wrote(o_ht)
                    ht.wrote(o_ht)
                    pc.read(o_ht)
                    if pend[0] is not None:
                        finish(pend[0])
                    pend[0] = (pc, ht, h, c, o_ht)
            finish(pend[0])
            pend[0] = None
            o = p.dma("sp", out_v[:, :, tsl], HG.t[:], deps=HG.rdeps(), ring="hgout", nslots=1)
            HG.read(o)
            out_dmas.append(o)
        p.wait("sp", out_dmas)
        p.emit(ctx)
    return nc


def build_C(S, HD=128, HPG=8, CMP_BLOCK=32, CMP_STRIDE=16, CMP_HID=512, SEL_BLOCK=64, N_SEL=16, WINDOW=512):
    nc = bass.Bass("TRN2", target_bir_lowering=False)
    NQB = S // 64
    NCMP = (S - CMP_BLOCK) // CMP_STRIDE + 1
    NCC = (NCMP + 127) // 128
    NSB = S // SEL_BLOCK
    NKC = S // 128
    HC = CMP_HID // 128
    scale = HD ** -0.5
    with ExitStack() as ctx:
        kb = KB(nc, ctx)
        p = kb.p
        qT_d = kb.din("qT", [HPG * HD, S], BF16)
        gT_d = kb.din("gT", [3 * HPG, S], F32)
        kcT_d = kb.din("kcT", [HD, S], BF16)
        vcT_d = kb.din("vcT", [HD, S], BF16)
        ksT_d = kb.din("ksT", [HD, S], BF16)
        kwT_d = kb.din("kwT", [HD, S], BF16)
        vs_d = kb.din("vs", [S, HD], BF16)
        vw_d = kb.din("vw", [S, HD], BF16)
        w1_d = kb.din("cmp_w1", [2, CMP_BLOCK * HD, CMP_HID], F32)
        w2_d = kb.din("cmp_w2", [2, CMP_HID, HD], F32)
        posT_d = kb.din("posT", [HD, 2 * CMP_BLOCK], F32)
        kng_d = kb.din("kng", [HD, 3], F32)
        qng_d = kb.din("qng", [HD, 3], F32)
        ov_d = kb.din("c_ov", [128, NCC * NSB], F32)
        ex_d = kb.din("c_ex", [64, NKC * 128], F32)
        selg_d = kb.din("c_selg", [3 * HPG, 3 * HPG * 128], F32)
        i64_d = kb.din("c_i64", [64, 64], F32)
        o_d = kb.dout("oT", [HPG * HD, S], BF16)

        Q = Buf(kb.sb("Q", [128, HPG, S], BF16))
        GT = Buf(kb.sb("GT", [3 * HPG, S], F32))
        KS = Buf(kb.sb("KS", [128, S], BF16))
        KW = Buf(kb.sb("KW", [128, S], BF16))
        VS = Buf(kb.sb("VS", [128, NKC, HD], BF16))
        VW = Buf(kb.sb("VW", [128, NKC, HD], BF16))
        KC = Buf(kb.sb("KC", [128, NCC * 128], BF16))
        VC = Buf(kb.sb("VC", [128, NCC, HD], BF16))
        RAW = kb.sbring("RAW", 1, [128, S], BF16)
        if HPG * S >= 2 * CMP_BLOCK * CMP_HID:
            hq = HPG // 2
            W1 = Buf(Q.t[:, hq:HPG, :].rearrange("p h s -> p (h s)")[:, 0:CMP_BLOCK * CMP_HID].rearrange("p (i n) -> p i n", i=CMP_BLOCK))
            w1_alias = True
        else:
            W1 = Buf(kb.sb("W1", [128, CMP_BLOCK, CMP_HID], BF16))
            w1_alias = False
        W2 = Buf(kb.sb("W2", [128, 2, HC, HD], BF16))
        posT = kb.sb("posT_sb", [128, 2 * CMP_BLOCK], F32)
        posTb = kb.sb("posT_bf", [128, 2 * CMP_BLOCK], BF16)
        kng = kb.sb("kng_sb", [128, 3], F32)
        qng = kb.sb("qng_sb", [128, 3], F32)
        kcg = kb.sb("kcg", [128, 1], F32)
        ov = kb.sb("ov", [128, NCC * NSB], BF16)
        ex = kb.sb("ex", [64, NKC * 128], BF16)
        selg = kb.sb("selg", [3 * HPG, 3 * HPG * 128], F32)
        i64 = kb.sb("i64", [64, 64], BF16)
        ones = kb.sb("ones", [128, 128], BF16)
        tiny = kb.sb("tiny", [128, 1], F32)
        HID = kb.sbring("HID", 2, [128, HC, 256], BF16)
        Z = kb.sbring("Z", 2, [128, 256], F32)
        Z2 = kb.sbring("Z2", 2, [128, 256], F32)
        pbias = Buf(kb.sb("pbias", [128, 2 * HC], F32))
        PT = kb.sbring("PT", 4, [128, HPG, 64], BF16)
        PN = kb.sbring("PN", 2, [128, HPG, 64], BF16)
        RD = kb.sbring("RD", 2, [128, 512], F32)
        CF = kb.sbring("CF", 2, [128, 512], F32)
        OACC = Buf(kb.sb("OACC", [128, 512], F32))
        OB = kb.sbring("OBo", 2, [128, HPG, 64], BF16)
        PM = kb.sbring("PMx", 2, [64, 64], F32)
        PM2 = kb.sbring("PM2", 2, [64, 64], F32)
        M8 = kb.sbring("M8", 2, [64, 16], F32)
        SEL = kb.sbring("SEL", 2, [64, 64], BF16)
        SELT = kb.sbring("SELT", 2, [64, 64], BF16)
        sqb = kb.sbring("SQc", 2, [128, 256], BF16)
        RK = Buf(kb.sb("RK", [128, 256], F32))

        psS = kb.psring("psS", 2)
        psN = kb.psring("psN", 2)
        psD = kb.psring("psD", 2)
        psM = Buf(kb.psum("psM", [128, 512]))
        psX = Buf(kb.psum("psX", [128, 512]))
        psB = psX

        c_ones = p.op("dve", X("memset", ones[:], 1.0))
        kb.eps = kb.sb("eps_sb", [128, 1], F32)
        kb.eps_op = p.op("dve", X("memset", kb.eps[:], EPS))
        c_tiny = p.op("dve", X("memset", tiny[:], 1e-30))
        dq = []
        for h in range(HPG // 2 if w1_alias else HPG):
            dq.append(p.dma("sp", Q.t[:, h, :], qT_d[h * 128:(h + 1) * 128, :], ring="ld", nslots=24))
        Q.wrote(dq[0])
        for o in dq[1:]:
            Q.also_wrote(o)
        o = p.dma("sp", GT.t[:], gT_d[:, :], ring="ld", nslots=24); GT.wrote(o)
        o = p.dma("sp", KS.t[:], ksT_d[:, :], ring="ld", nslots=24); KS.wrote(o)
        o = p.dma("sp", KW.t[:], kwT_d[:, :], ring="ld", nslots=24); KW.wrote(o)
        o = p.dma("sp", VS.t[:], vs_d.rearrange("(c p) d -> p c d", p=128), ring="ld", nslots=24); VS.wrote(o)
        o = p.dma("sp", VW.t[:], vw_d.rearrange("(c p) d -> p c d", p=128), ring="ld", nslots=24); VW.wrote(o)
        d_pos = p.dma("sp", posT[:], posT_d[:, :], ring="ld", nslots=24)
        d_kng = p.dma("sp", kng[:], kng_d[:, :], ring="ld", nslots=24)
        d_qng = p.dma("sp", qng[:], qng_d[:, :], ring="ld", nslots=24)
        d_selg = p.dma("sp", selg[:], selg_d[:, :], ring="ld", nslots=24)
        c_ov = p.dma("pool", ov[:], ov_d[:, :], ring="ldc", nslots=4)
        c_i64 = p.dma("pool", i64[:], i64_d[:, :], ring="ldc", nslots=4)
        c_ex = p.dma("pool", ex[:], ex_d[:, :], ring="ldc", nslots=4)
        c_pos = p.op("dve", X("tensor_copy", out=posTb[:], in_=posT[:]), deps=[d_pos])
        o = p.op("dve", X("tensor_tensor", out=kcg[:], in0=kng[:, 0:1], in1=qng[:, 0:1], op=ALU.mult), deps=[d_kng, d_qng])
        c_kcg = p.op("dve", X("tensor_scalar", out=kcg[:], in0=kcg[:], scalar1=scale, scalar2=None, op0=ALU.mult), deps=[o])
        z0 = p.op("pool", X("memset", KC.t[:], 0.0)); KC.wrote(z0)
        z1 = p.op("pool", X("memset", VC.t[:], 0.0)); VC.wrote(z1)
        w2l = p.dma("pool", W2.t[:], w2_d.rearrange("j (c p) d -> p j c d", p=128), ring="w2", nslots=1)
        W2.wrote(w2l)

        for j in range(2):
            raw = RAW.next()
            dr = p.dma("sp", raw.t[:], (kcT_d if j == 0 else vcT_d)[:, :], deps=raw.wdeps(), ring="ldr", nslots=2)
            raw.wrote(dr)
            dw = p.dma("pool", W1.t[:], w1_d[j].rearrange("(i p) n -> p i n", p=128), deps=W1.wdeps(), ring="w1", nslots=1)
            W1.wrote(dw)
            bank = psB
            d0 = bank.wdeps() + [dw, c_pos]
            mm = None
            for hc in range(HC):
                for i in range(CMP_BLOCK):
                    mm = p.op("pe", X("matmul", bank.t[:, hc:hc + 1], lhsT=W1.t[:, i, hc * 128:(hc + 1) * 128], rhs=posTb[:, j * CMP_BLOCK + i:j * CMP_BLOCK + i + 1], start=(i == 0), stop=(i == CMP_BLOCK - 1)), deps=d0 if (hc == 0 and i == 0) else [])
            bank.wrote(mm)
            o_pb = p.op("dve", X("tensor_copy", out=pbias.t[:, j * HC:(j + 1) * HC], in_=bank.t[:, 0:HC]), deps=[mm] + pbias.wdeps())
            bank.read(o_pb)
            pbias.wrote(o_pb)
            hid = HID.next()
            first = True
            for hc in range(HC):
                for c0 in range(0, NCMP, 256):
                    ncol = min(256, NCMP - c0)
                    bank = psS.next()
                    d0 = bank.wdeps() + [dw, dr]
                    for i in range(CMP_BLOCK):
                        t0 = CMP_STRIDE * c0 + i
                        mm = p.op("pe", X("matmul", bank.t[:, 0:ncol], lhsT=W1.t[:, i, hc * 128:(hc + 1) * 128], rhs=raw.t[:, t0:t0 + CMP_STRIDE * (ncol - 1) + 1:CMP_STRIDE], start=(i == 0), stop=(i == CMP_BLOCK - 1)), deps=d0 if i == 0 else [])
                    bank.wrote(mm)
                    W1.read(mm)
                    raw.read(mm)
                    z = Z.next()
                    z2 = Z2.next()
                    o1 = p.op("act", X("activation", out=z.t[:, 0:ncol], in_=bank.t[:, 0:ncol], func=AF.Identity, bias=pbias.t[:, j * HC + hc:j * HC + hc + 1]), deps=[mm, o_pb] + z.wdeps())
                    bank.read(o1)
                    o2 = p.op("dve", X("tensor_tensor", out=z2.t[:, 0:ncol], in0=z.t[:, 0:ncol], in1=z.t[:, 0:ncol], op=ALU.mult), deps=[o1] + z2.wdeps())
                    o3 = p.op("dve", X("tensor_scalar", out=z2.t[:, 0:ncol], in0=z2.t[:, 0:ncol], scalar1=0.044715, scalar2=1.0, op0=ALU.mult, op1=ALU.add), deps=[o2])
                    o4 = p.op("dve", X("tensor_tensor", out=z2.t[:, 0:ncol], in0=z2.t[:, 0:ncol], in1=z.t[:, 0:ncol], op=ALU.mult), deps=[o3])
                    o5 = p.op("act", X("activation", out=z2.t[:, 0:ncol], in_=z2.t[:, 0:ncol], func=AF.Sigmoid, scale=1.5957691216057308), deps=[o4])
                    o6 = p.op("dve", X("tensor_tensor", out=hid.t[:, hc, c0:c0 + ncol] if NCMP <= 256 else hid.t[:, hc, 0:ncol], in0=z.t[:, 0:ncol], in1=z2.t[:, 0:ncol], op=ALU.mult), deps=[o5] + hid.wdeps())
                    z.wrote(o6)
                    z2.wrote(o6)
                    hid.wrote(o6) if first else hid.also_wrote(o6)
                    first = False
            assert NCMP <= 256
            if j == 0:
                bank = psS.next()
                d0 = bank.wdeps() + hid.rdeps() + [w2l]
                for hc in range(HC):
                    mm = p.op("pe", X("matmul", bank.t[:, 0:NCMP], lhsT=W2.t[:, 0, hc, :], rhs=hid.t[:, hc, 0:NCMP], start=(hc == 0), stop=(hc == HC - 1)), deps=d0 if hc == 0 else [])
                bank.wrote(mm)
                hid.read(mm)
                sq = sqb.next()
                o_sq = p.op("act", X("activation", out=sq.t[:, 0:NCMP], in_=bank.t[:, 0:NCMP], func=AF.Square), deps=[mm] + sq.wdeps())
                sq.wrote(o_sq)
                bank.read(o_sq)
                mm2 = p.op("pe", X("matmul", psB.t[:, 0:NCMP], lhsT=ones[:], rhs=sq.t[:, 0:NCMP], start=True, stop=True), deps=[o_sq, c_ones] + psB.wdeps())
                psB.wrote(mm2)
                sq.read(mm2)
                o1 = p.op("act", X("activation", out=RK.t[:, 0:NCMP], in_=psB.t[:, 0:NCMP], func=AF.Sqrt, scale=1.0 / HD, bias=kb.eps[:, 0:1]), deps=[mm2, kb.eps_op] + RK.wdeps())
                psB.read(o1)
                o2 = p.op("dve", X("reciprocal", out=RK.t[:, 0:NCMP], in_=RK.t[:, 0:NCMP]), deps=[o1])
                o3 = p.op("dve", X("scalar_tensor_tensor", out=KC.t[:, 0:NCMP], in0=bank.t[:, 0:NCMP], scalar=kcg[:, 0:1], in1=RK.t[:, 0:NCMP], op0=ALU.mult, op1=ALU.mult), deps=[o2, c_kcg, mm] + KC.wdeps())
                bank.read(o3)
                RK.wrote(o3)
                KC.wrote(o3)
            else:
                for cc in range(NCC):
                    ncol = min(128, NCMP - cc * 128)
                    bank = psS.next()
                    d0 = bank.wdeps() + hid.rdeps() + [w2l]
                    for hc in range(HC):
                        mm = p.op("pe", X("matmul", bank.t[0:ncol, 0:HD], lhsT=hid.t[:, hc, cc * 128:cc * 128 + ncol], rhs=W2.t[:, 1, hc, :], start=(hc == 0), stop=(hc == HC - 1)), deps=d0 if hc == 0 else [])
                    bank.wrote(mm)
                    hid.read(mm)
                    o = p.op("act", X("copy", out=VC.t[0:ncol, cc, :], in_=bank.t[0:ncol, 0:HD]), deps=[mm] + VC.wdeps())
                    bank.read(o)
                    VC.wrote(o) if cc == 0 else VC.also_wrote(o)

        if w1_alias:
            for h in range(HPG // 2, HPG):
                o = p.dma("sp", Q.t[:, h, :], qT_d[h * 128:(h + 1) * 128, :], deps=W1.wdeps(), ring="ld", nslots=24)
                Q.also_wrote(o)
        out_dmas = []
        o_v = o_d.rearrange("(h p) t -> p h t", p=128)
        for qb in range(NQB):
            qsl = slice(qb * 64, (qb + 1) * 64)
            t0 = qb * 64
            rhsq = Q.t[:, :, qsl]
            need_topk = (qb + 1) > N_SEL
            kc_last = qb // 2
            st = {"selT": None, "pts": [], "rd": None}

            def stage1(job):
                (br, kT_ap, v_ap, kdeps, first, last, mask_fn) = job
                bank = psS.next()
                mm = p.op("pe", X("matmul", bank.t[:], lhsT=kT_ap, rhs=rhsq, start=True, stop=True), deps=bank.wdeps() + Q.rdeps() + kdeps)
                bank.wrote(mm)
                pt = PT.next()
                o = p.op("act", X("activation", out=pt.t[:].rearrange("p h q -> p (h q)"), in_=bank.t[:], func=AF.Exp), deps=[mm] + pt.wdeps())
                bank.read(o)
                pt.wrote(o)
                o = mask_fn(pt, o)
                return (pt, o)

            def stage2(job, s1, acc):
                (br, kT_ap, v_ap, kdeps, first, last, mask_fn) = job
                pt, o = s1
                bN, bD = acc
                d0 = [o]
                if first:
                    d0 = d0 + bN.wdeps() + bD.wdeps()
                ptf = pt.t[:].rearrange("p h q -> p (h q)")
                m1 = p.op("pe", X("matmul", bN.t[:], lhsT=v_ap, rhs=ptf, start=first, stop=last), deps=d0 + kdeps)
                m2 = p.op("pe", X("matmul", bD.t[:], lhsT=ones[:], rhs=ptf, start=first, stop=last), deps=[c_ones])
                pt.read(m2)
                if last:
                    bN.wrote(m1)
                    bD.wrote(m2)
                return m2

            def finish_branch(r, m_last, first_branch, acc):
                bN, bD = acc
                d0 = psX.wdeps() + GT.rdeps() + [d_selg]
                mm = None
                for h in range(HPG):
                    row = r * HPG + h
                    mm = p.op("pe", X("matmul", psX.t[:, h * 64:(h + 1) * 64], lhsT=selg[:, row * 128:(row + 1) * 128], rhs=GT.t[:, qsl], start=True, stop=True), deps=d0 if h == 0 else [])
                psX.wrote(mm)
                rd = RD.next()
                o1 = p.op("dve", X("tensor_scalar", out=rd.t[:], in0=bD.t[:], scalar1=tiny[:, 0:1], scalar2=None, op0=ALU.max), deps=[m_last, c_tiny] + rd.wdeps())
                bD.read(o1)
                o2 = p.op("dve", X("reciprocal", out=rd.t[:], in_=rd.t[:]), deps=[o1])
                rd.wrote(o2)
                cf = CF.next()
                o3 = p.op("dve", X("tensor_tensor", out=cf.t[:], in0=psX.t[:], in1=rd.t[:], op=ALU.mult), deps=[o2, mm] + cf.wdeps())
                psX.read(o3)
                rd.read(o3)
                if first_branch:
                    o4 = p.op("dve", X("tensor_tensor", out=OACC.t[:], in0=bN.t[:], in1=cf.t[:], op=ALU.mult), deps=[o3, m_last] + OACC.wdeps())
                else:
                    o4a = p.op("dve", X("tensor_tensor", out=cf.t[:], in0=bN.t[:], in1=cf.t[:], op=ALU.mult), deps=[o3, m_last])
                    o4 = p.op("dve", X("tensor_tensor", out=OACC.t[:], in0=OACC.t[:], in1=cf.t[:], op=ALU.add), deps=[o4a] + OACC.wdeps())
                cf.wrote(o4)
                bN.read(o4)
                OACC.wrote(o4)
                return rd, o2

            def topk(rd, o_rd, pts):
                pns = []
                for cc, pt in enumerate(pts):
                    pn = PN.next()
                    o = p.op("dve", X("tensor_tensor", out=pn.t[:].rearrange("p h q -> p (h q)"), in0=pt.t[:].rearrange("p h q -> p (h q)"), in1=rd.t[:], op=ALU.mult), deps=[o_rd] + pt.rdeps() + pn.wdeps())
                    pn.wrote(o)
                    pt.read(o)
                    rd.read(o)
                    pns.append((pn, o))
                d0 = psX.wdeps() + [c_ov]
                n_mm = len(pns) * HPG
                i = 0
                mm = None
                for cc, (pn, o) in enumerate(pns):
                    for h in range(HPG):
                        mm = p.op("pe", X("matmul", psX.t[0:64, 0:NSB], lhsT=pn.t[:, h, :], rhs=ov[:, cc * NSB:(cc + 1) * NSB], start=(i == 0), stop=(i == n_mm - 1)), deps=(d0 if i == 0 else []) + ([o] if h == 0 else []))
                        i += 1
                    pn.read(mm)
                psX.wrote(mm)
                pm = PM.next()
                o1 = p.op("dve", X("tensor_copy", out=pm.t[:, 0:NSB], in_=psX.t[0:64, 0:NSB]), deps=[mm] + pm.wdeps())
                psX.read(o1)
                o2 = p.op("dve", X("memset", pm.t[:, 0:1], 1e30), deps=[o1])
                o3 = p.op("dve", X("memset", pm.t[:, qb:qb + 1], 1e30), deps=[o2])
                if qb + 1 < NSB:
                    o3 = p.op("dve", X("memset", pm.t[:, qb + 1:NSB], -1.0), deps=[o3])
                m8 = M8.next()
                o4 = p.op("dve", X("max", out=m8.t[:, 0:8], in_=pm.t[:, 0:NSB]), deps=[o3] + m8.wdeps())
                pm2 = PM2.next()
                o5 = p.op("dve", X("match_replace", out=pm2.t[:, 0:NSB], in_to_replace=m8.t[:, 0:8], in_values=pm.t[:, 0:NSB], imm_value=-2.0), deps=[o4] + pm2.wdeps())
                o6 = p.op("dve", X("max", out=m8.t[:, 8:16], in_=pm2.t[:, 0:NSB]), deps=[o5])
                pm2.wrote(o6)
                sel = SEL.next()
                o7 = p.op("dve", X("tensor_scalar", out=sel.t[:, 0:NSB], in0=pm.t[:, 0:NSB], scalar1=m8.t[:, 15:16], scalar2=None, op0=ALU.is_ge), deps=[o6] + sel.wdeps())
                m8.wrote(o7)
                pm.wrote(o7)
                sel.wrote(o7)
                mmT = p.op("pe", X("matmul", psX.t[0:NSB, 64:128], lhsT=sel.t[:, 0:NSB], rhs=i64[:], start=True, stop=True), deps=[o7, c_i64] + psX.wdeps())
                psX.wrote(mmT)
                sel.read(mmT)
                selT = SELT.next()
                o8 = p.op("act", X("copy", out=selT.t[0:NSB, :], in_=psX.t[0:NSB, 64:128]), deps=[mmT] + selT.wdeps())
                psX.read(o8)
                selT.wrote(o8)
                st["selT"] = selT

            jobs = []
            ncv = min(NCMP, (t0 + 63 - (CMP_BLOCK - 1)) // CMP_STRIDE + 1)
            ncc_q = max(1, (ncv + 127) // 128)
            for cc in range(ncc_q):
                def mask_cmp(pt, o, cc=cc):
                    o2 = p.op("pool", X("affine_select", out=pt.t[:], in_=pt.t[:], pattern=[[0, HPG], [1, 64]], compare_op=ALU.is_ge, fill=0.0,
                                        base=t0 - (CMP_BLOCK - 1) - CMP_STRIDE * 128 * cc, channel_multiplier=-CMP_STRIDE), deps=[o])
                    pt.wrote(o2)
                    st["pts"].append(pt)
                    return o2
                jobs.append((0, KC.t[:, cc * 128:(cc + 1) * 128], VC.t[:, cc, :], KC.rdeps() + VC.rdeps(), cc == 0, cc == ncc_q - 1, mask_cmp))
            kc_first = max(0, (t0 - (WINDOW - 1)) // 128)
            for kc in range(kc_first, kc_last + 1):
                def mask_win(pt, o, kc=kc):
                    if 128 * kc < t0 + 63 - (WINDOW - 1):
                        o = p.op("pool", X("affine_select", out=pt.t[:], in_=pt.t[:], pattern=[[0, HPG], [-1, 64]], compare_op=ALU.is_ge, fill=0.0,
                                           base=128 * kc - t0 + WINDOW - 1, channel_multiplier=1), deps=[o])
                        pt.wrote(o)
                    if 128 * kc + 127 > t0:
                        o = p.op("pool", X("affine_select", out=pt.t[:], in_=pt.t[:], pattern=[[0, HPG], [1, 64]], compare_op=ALU.is_ge, fill=0.0,
                                           base=t0 - 128 * kc, channel_multiplier=-1), deps=[o])
                        pt.wrote(o)
                    return o
                jobs.append((2, KW.t[:, kc * 128:(kc + 1) * 128], VW.t[:, kc, :], KW.rdeps() + VW.rdeps(), kc == kc_first, kc == kc_last, mask_win))
            for kc in range(kc_last + 1):
                def mask_sel(pt, o, kc=kc):
                    if need_topk:
                        selT = st["selT"]
                        bm = psM
                        mmx = p.op("pe", X("matmul", bm.t[:, 0:64], lhsT=ex[0:NSB, kc * 128:(kc + 1) * 128], rhs=selT.t[0:NSB, :], start=True, stop=True), deps=bm.wdeps() + selT.rdeps() + [c_ex])
                        bm.wrote(mmx)
                        selT.read(mmx)
                        o = p.op("dve", X("tensor_tensor", out=pt.t[:], in0=pt.t[:], in1=bm.t[:, 0:64].unsqueeze(1).to_broadcast([128, HPG, 64]), op=ALU.mult), deps=[o, mmx])
                        bm.read(o)
                        pt.wrote(o)
                    if kc == kc_last:
                        o = p.op("pool", X("affine_select", out=pt.t[:], in_=pt.t[:], pattern=[[0, HPG], [1, 64]], compare_op=ALU.is_ge, fill=0.0,
                                           base=t0 - 128 * kc, channel_multiplier=-1), deps=[o])
                        pt.wrote(o)
                    return o
                jobs.append((1, KS.t[:, kc * 128:(kc + 1) * 128], VS.t[:, kc, :], KS.rdeps() + VS.rdeps(), kc == 0, kc == kc_last, mask_sel))

            nj = len(jobs)
            accs = {}
            s1s = [None] * nj
            def can_issue(i):
                return not (jobs[i][0] == 1 and need_topk and st["selT"] is None)
            if can_issue(0):
                s1s[0] = stage1(jobs[0])
            first_branch = True
            for i in range(nj):
                if i + 1 < nj and s1s[i + 1] is None and can_issue(i + 1):
                    s1s[i + 1] = stage1(jobs[i + 1])
                if s1s[i] is None:
                    s1s[i] = stage1(jobs[i])
                br = jobs[i][0]
                if jobs[i][4]:
                    accs[br] = (psN.next(), psD.next())
                m2 = stage2(jobs[i], s1s[i], accs[br])
                if jobs[i][5]:
                    rd, o_rd = finish_branch(br, m2, first_branch, accs[br])
                    first_branch = False
                    if br == 0 and need_topk:
                        topk(rd, o_rd, st["pts"])

            ob = OB.next()
            o = p.op("act", X("copy", out=ob.t[:].rearrange("p h q -> p (h q)"), in_=OACC.t[:]), deps=OACC.rdeps() + ob.wdeps())
            OACC.read(o)
            ob.wrote(o)
            od = p.dma("sp", o_v[:, :, qsl], ob.t[:], deps=[o], ring="oout", nslots=2)
            ob.read(od)
            out_dmas.append(od)
        p.wait("sp", out_dmas)
        p.emit(ctx)
    return nc


D_MODEL = 4096
SEQ = 4096
BATCH = 2
D_FF = 4 * D_MODEL
M_HEADS = 8
M_DV = 512
M_DK = 256
N_KV = 4
HPG = 8
HEAD_DIM = 128
KV_DIM = 512
NCORES = 8


def _lay(g):
    return np.ascontiguousarray(np.asarray(g, np.float32).reshape(-1, 128).T)


def _consts_A():
    cc = np.tril(np.ones((64, 64), np.float32)).T
    cc = np.ascontiguousarray(np.concatenate([cc, cc], 0))
    seg = np.ones((2, 512), np.float32)
    seg[:, ::64] = 0
    neg = np.zeros((2, 512), np.float32)
    neg[:, ::64] = -1e30
    sel = np.zeros((2, 256), np.float32)
    sel[0, :128] = 1
    sel[1, 128:] = 1
    return {"c_causal": cc, "c_seg": np.concatenate([seg, neg], 1), "c_sel": sel, "c_i2": np.eye(2, dtype=np.float32)}


def _consts_C(S):
    NCMP = (S - 32) // 16 + 1
    NCC = (NCMP + 127) // 128
    NSB = S // 64
    NKC = S // 128
    c0 = np.arange(NCC * 128)[:, None] * 16
    s0 = np.arange(NSB)[None, :] * 64
    ovm = np.maximum(np.minimum(c0 + 32, s0 + 64) - np.maximum(c0, s0), 0) / 32.0
    ovm[NCMP:] = 0
    ov = np.ascontiguousarray(ovm.reshape(NCC, 128, NSB).transpose(1, 0, 2).reshape(128, NCC * NSB).astype(np.float32))
    ex = np.zeros((64, NKC * 128), np.float32)
    pp = np.arange(NKC * 128)
    jj = 2 * (pp // 128) + (pp % 128) // 64
    ok = jj < 64
    ex[jj[ok], pp[ok]] = 1
    selg = np.zeros((24, 24 * 128), np.float32)
    for r in range(24):
        selg[r, r * 128:(r + 1) * 128] = 1
    return {"c_ov": ov, "c_ex": ex, "c_selg": selg, "c_i64": np.eye(64, dtype=np.float32)}


_NC_CACHE = {}


def _get(name, fn):
    if name not in _NC_CACHE:
        _NC_CACHE[name] = fn()
    return _NC_CACHE[name]


def _run(nc, in_maps):
    res = run_bass_kernel_spmd(nc, in_maps, core_ids=list(range(len(in_maps))))
    return res.results


def kernel(x, attn_norm_g, mlp_norm_g, m_w_in, m_b_gate, m_head_g, m_w_out, kv_norm_g, w_kv,
           k_norm_g, cmp_pos, cmp_w1, cmp_w2, n_w_qg, q_norm_g, n_w_out, mlp_w_up, mlp_w_down):
    f32 = np.float32
    x = np.asarray(x, f32)
    B, S, D = x.shape
    T = S // 4
    xT = [np.ascontiguousarray(x[b].T) for b in range(B)]
    m_w_in = np.asarray(m_w_in, f32)[0]
    QK = M_HEADS * M_DK
    ncA = _get("A", lambda: build_A(S, D, M_DK, M_DV))
    cA = _consts_A()
    bgate = np.asarray(m_b_gate, f32)[0]
    hg_all = np.asarray(m_head_g, f32)[0]
    ga0 = _lay(np.asarray(attn_norm_g, f32)[0])
    mapsA = []
    for c in range(NCORES):
        b, hp = c // 4, c % 4
        h0 = 2 * hp
        cq = slice(h0 * M_DK, (h0 + 2) * M_DK)
        cv = slice(h0 * M_DV, (h0 + 2) * M_DV)
        gcols = [2 * QK + 2 * D + h0, 2 * QK + 2 * D + h0 + 1, 2 * QK + 2 * D + M_HEADS + h0, 2 * QK + 2 * D + M_HEADS + h0 + 1]
        m = {"xT": xT[b], "ga": ga0,
             "w_q": np.ascontiguousarray(m_w_in[:, 0:QK][:, cq]),
             "w_k": np.ascontiguousarray(m_w_in[:, QK:2 * QK][:, cq]),
             "w_v": np.ascontiguousarray(m_w_in[:, 2 * QK:2 * QK + D][:, cv]),
             "w_og": np.ascontiguousarray(m_w_in[:, 2 * QK + D:2 * QK + 2 * D][:, cv]),
             "w_g": np.ascontiguousarray(m_w_in[:, gcols]),
             "bg": np.ascontiguousarray(np.stack([bgate[[h0, h0 + 1]], bgate[[M_HEADS + h0, M_HEADS + h0 + 1]]], axis=1)),
             "hgain": _lay(hg_all[h0 * M_DV:(h0 + 2) * M_DV])}
        m.update(cA)
        mapsA.append(m)
    rA = _run(ncA, mapsA)
    aT = [np.concatenate([np.asarray(rA[b * 4 + hp]["hgT"]) for hp in range(4)], axis=0) for b in range(B)]
    del rA, mapsA

    ncB = _get("B", lambda: build_BD2(T, D, D_FF, True))
    gainsB = np.ascontiguousarray(np.concatenate([_lay(np.asarray(mlp_norm_g, f32)[0]), _lay(np.asarray(attn_norm_g, f32)[1]), _lay(np.asarray(kv_norm_g, f32))], axis=1))
    kngT = np.ascontiguousarray(np.asarray(k_norm_g, f32).T)
    qngT = np.ascontiguousarray(np.asarray(q_norm_g, f32)[0].T)
    w_o0 = np.asarray(m_w_out, f32)[0]
    w_up0 = np.asarray(mlp_w_up, f32)[0]
    w_dn0 = np.asarray(mlp_w_down, f32)[0]
    w_qg = np.asarray(n_w_qg, f32)[0]
    w_kv_ = np.asarray(w_kv, f32)
    mapsB = []
    for c in range(NCORES):
        b, r = c // 4, c % 4
        ts = slice(r * T, (r + 1) * T)
        mapsB.append({"xT": np.ascontiguousarray(xT[b][:, ts]), "aT": np.ascontiguousarray(aT[b][:, ts]),
                      "w_o": w_o0, "w_up": w_up0, "w_dn": w_dn0, "gains": gainsB,
                      "w_qg": w_qg, "w_kv": w_kv_, "kng": kngT, "qng": qngT})
    rB = _run(ncB, mapsB)
    del mapsB

    def catT(name, b):
        return np.concatenate([np.asarray(rB[b * 4 + r][name]) for r in range(4)], axis=1)

    def catR(name, b):
        return np.concatenate([np.asarray(rB[b * 4 + r][name]) for r in range(4)], axis=0)

    x2T = [catT("xoT", b) for b in range(B)]
    ncC = _get("C", lambda: build_C(S))
    cC = _consts_C(S)
    posT = np.ascontiguousarray(np.asarray(cmp_pos, f32).transpose(2, 0, 1).reshape(HEAD_DIM, -1))
    w1 = np.asarray(cmp_w1, f32)
    w2 = np.asarray(cmp_w2, f32)
    mapsC = []
    for b in range(B):
        qT = catT("qT", b)
        gT = catT("gT", b)
        kcT, vcT, ksT, kwT = catT("kcT", b), catT("vcT", b), catT("ksT", b), catT("kwT", b)
        vs, vw = catR("vs", b), catR("vw", b)
        for g in range(N_KV):
            gr = slice(g * 128, (g + 1) * 128)
            grow = np.concatenate([np.arange(r_ * 32 + g * 8, r_ * 32 + g * 8 + 8) for r_ in range(3)])
            m = {"qT": np.ascontiguousarray(qT[g * 1024:(g + 1) * 1024]), "gT": np.ascontiguousarray(gT[grow]),
                 "kcT": np.ascontiguousarray(kcT[gr]), "vcT": np.ascontiguousarray(vcT[gr]),
                 "ksT": np.ascontiguousarray(ksT[gr]), "kwT": np.ascontiguousarray(kwT[gr]),
                 "vs": np.ascontiguousarray(vs[:, gr]), "vw": np.ascontiguousarray(vw[:, gr]),
                 "cmp_w1": w1, "cmp_w2": w2, "posT": posT, "kng": kngT, "qng": qngT}
            m.update(cC)
            mapsC.append(m)
    del rB
    rC = _run(ncC, mapsC)
    aT2 = [np.concatenate([np.asarray(rC[b * 4 + g]["oT"]) for g in range(4)], axis=0) for b in range(B)]
    del rC, mapsC

    ncD = _get("D", lambda: build_BD2(T, D, D_FF, False))
    gainsD = np.ascontiguousarray(np.concatenate([_lay(np.asarray(mlp_norm_g, f32)[1])] * 3, axis=1))
    w_o1 = np.asarray(n_w_out, f32)[0]
    w_up1 = np.asarray(mlp_w_up, f32)[1]
    w_dn1 = np.asarray(mlp_w_down, f32)[1]
    mapsD = []
    for c in range(NCORES):
        b, r = c // 4, c % 4
        ts = slice(r * T, (r + 1) * T)
        mapsD.append({"xT": np.ascontiguousarray(x2T[b][:, ts]), "aT": np.ascontiguousarray(aT2[b][:, ts]),
                      "w_o": w_o1, "w_up": w_up1, "w_dn": w_dn1, "gains": gainsD})
    rD = _run(ncD, mapsD)
    out = np.empty((B, S, D), f32)
    for c in range(NCORES):
        b, r = c // 4, c % 4
        out[b, r * T:(r + 1) * T, :] = np.asarray(rD[c]["xoT"]).T
    return out


def build_BD2(T, D, DFF, tail, HD=128, NKV=4, NQH=32, NG=96):
    nc = bass.Bass("TRN2", target_bir_lowering=False)
    KD = D // 128
    KF = DFF // 128
    NP = T // 512
    JR = 8
    NR = KF // JR
    KH = KD // 2
    with ExitStack() as ctx:
        kb = KB(nc, ctx)
        p = kb.p
        xT_d = kb.din("xT", [D, T], F32)
        aT_d = kb.din("aT", [D, T], BF16)
        w_o = kb.din("w_o", [D, D], F32)
        w_up = kb.din("w_up", [D, DFF], F32)
        w_dn = kb.din("w_dn", [DFF, D], F32)
        g_d = kb.din("gains", [128, 3 * KD], F32)
        xo_d = kb.dout("xoT", [D, T], F32)
        if tail:
            KVD = NKV * HD
            w_qg = kb.din("w_qg", [D, D + NG], F32)
            w_kv = kb.din("w_kv", [D, 6 * KVD], F32)
            kng_d = kb.din("kng", [128, 3], F32)
            qng_d = kb.din("qng", [128, 3], F32)
            qT_o = kb.dout("qT", [D, T], BF16)
            gT_o = kb.dout("gT", [NG, T], F32)
            kcT_o = kb.dout("kcT", [KVD, T], BF16)
            vcT_o = kb.dout("vcT", [KVD, T], BF16)
            ksT_o = kb.dout("ksT", [KVD, T], BF16)
            kwT_o = kb.dout("kwT", [KVD, T], BF16)
            vs_o = kb.dout("vs", [T, KVD], BF16)
            vw_o = kb.dout("vw", [T, KVD], BF16)

        DW = min(1024, D)
        XT = Buf(kb.sb("XT", [128, KD, 512], F32))
        XN = Buf(kb.sb("XN", [128, KD, 512], BF16))
        HT = kb.sbring("HT", 2, [128, JR, 512], BF16)
        wA = kb.sbring("wA", 3, [128, KH, 512], BF16)
        wB = kb.sbring("wB", 2, [128, JR, DW], BF16)
        sqr = kb.sbring("SQ", 3, [128, 512], BF16)
        sqf = kb.sbring("SQF", 2, [128, 512], F32)
        R = Buf(kb.sb("R", [128, 512], F32))
        gains = kb.sb("gains_sb", [128, 3 * KD], F32)
        ones = kb.sb("ones", [128, 128], BF16)
        psR = kb.psring("psR", 8)
        if tail:
            kgain = kb.sb("kgain_sb", [128, 2], F32)
            w0, w1 = wB.bufs[0].t, wB.bufs[1].t
            Rq = Ring([Buf(w0[:, 0, 0:1024].bitcast(F32)), Buf(w0[:, 1, 0:1024].bitcast(F32))]) if DW >= 1024 else kb.sbring("Rq", 2, [128, 512], F32)
            GB = Buf(w0[:, 2, 0:1024].bitcast(F32)) if DW >= 1024 else Buf(kb.sb("GB", [128, 512], F32))
            OB = Ring([Buf(w1[:, j, 0:512]) for j in range(3)])
            tail_alias = [(wB.bufs[0], (Rq.bufs + [GB]) if DW >= 1024 else []), (wB.bufs[1], OB.bufs)]

        c_ones = p.op("dve", X("memset", ones[:], 1.0))
        kb.eps = kb.sb("eps_sb", [128, 1], F32)
        kb.eps_op = p.op("dve", X("memset", kb.eps[:], EPS))
        d_g = p.dma("sp", gains[:], g_d[:, :], ring="misc", nslots=4)
        if tail:
            kng = kb.sb("kng_sb", [128, 3], F32)
            qng = kb.sb("qng_sb", [128, 3], F32)
            d_k1 = p.dma("sp", kng[:], kng_d[:, :], ring="misc", nslots=4)
            d_k2 = p.dma("sp", qng[:], qng_d[:, :], ring="misc", nslots=4)
            o_ = p.op("dve", X("tensor_tensor", out=kgain[:], in0=kng[:, 1:3], in1=qng[:, 1:3], op=ALU.mult), deps=[d_k1, d_k2])
            d_kg = p.op("dve", X("tensor_scalar", out=kgain[:], in0=kgain[:], scalar1=HD ** -0.5, scalar2=None, op0=ALU.mult), deps=[o_])

        w_o_v = w_o.rearrange("(k p) n -> p k n", p=128)
        w_up_v = w_up.rearrange("(k p) n -> p k n", p=128)
        w_dn_v = w_dn.rearrange("(j p) n -> p j n", p=128)
        xT_v = xT_d.rearrange("(k p) t -> p k t", p=128)
        aT_v = aT_d.rearrange("(k p) t -> p k t", p=128)
        xo_v = xo_d.rearrange("(k p) t -> p k t", p=128)

        def load_half(src_v, half, c0, ncols):
            s = wA.next()
            o = p.dma("pool", s.t[:, :, 0:ncols], src_v[:, half * KH:(half + 1) * KH, c0:c0 + ncols], deps=s.wdeps(), ring="wA", nslots=3)
            s.wrote(o)
            return s

        def proj_fm(src_v, c0, ncols, ready):
            nj = (ncols + 127) // 128
            banks = [psR.next() for _ in range(nj)]
            mm = [None] * nj
            for half in range(2):
                s = load_half(src_v, half, c0, ncols)
                for j in range(nj):
                    w = min(128, ncols - j * 128)
                    for k in range(KH):
                        deps = []
                        if k == 0:
                            deps = s.rdeps()
                            if half == 0:
                                deps = deps + banks[j].wdeps() + list(ready) + [c_ones]
                        mm[j] = p.op("pe", X("matmul", banks[j].t[0:w, :], lhsT=s.t[:, k, j * 128:j * 128 + w], rhs=XN.t[:, half * KH + k, :],
                                             start=(half == 0 and k == 0), stop=(half == 1 and k == KH - 1)), deps=deps)
                    s.read(mm[j])
                XN.read(mm[nj - 1])
            out = []
            for j in range(nj):
                banks[j].wrote(mm[j])
                out.append((banks[j], mm[j], min(128, ncols - j * 128)))
            return out

        def proj_tm(src_v, c0, ready):
            banks = [psR.next() for _ in range(4)]
            mm = [None] * 4
            for half in range(2):
                s = load_half(src_v, half, c0, 512)
                for tt in range(4):
                    for k in range(KH):
                        deps = []
                        if k == 0:
                            deps = s.rdeps()
                            if half == 0:
                                deps = deps + banks[tt].wdeps() + list(ready)
                        mm[tt] = p.op("pe", X("matmul", banks[tt].t[:], lhsT=XN.t[:, half * KH + k, tt * 128:(tt + 1) * 128], rhs=s.t[:, k, :],
                                              start=(half == 0 and k == 0), stop=(half == 1 and k == KH - 1)), deps=deps)
                    s.read(mm[tt])
                XN.read(mm[3])
            for tt in range(4):
                banks[tt].wrote(mm[tt])
            return [(banks[tt], mm[tt]) for tt in range(4)]

        out_dmas = []
        for ps_ in range(NP):
            tsl = slice(ps_ * 512, (ps_ + 1) * 512)
            ld = []
            for q in range(4):
                ks = slice(q * KD // 4, (q + 1) * KD // 4)
                ld.append(p.dma("sp", XT.t[:, ks, :], xT_v[:, ks, tsl], deps=XT.wdeps(), ring="xin", nslots=4))
            XT.wrote(ld[0])
            for o_ in ld[1:]:
                XT.also_wrote(o_)
            xt_ready = list(ld)
            la = []
            for q in range(2):
                ks = slice(q * KD // 2, (q + 1) * KD // 2)
                la.append(p.dma("sp", XN.t[:, ks, :], aT_v[:, ks, tsl], deps=XN.wdeps(), ring="ain", nslots=2))
            XN.wrote(la[0])
            XN.also_wrote(la[1])
            xn_ready = list(la)
            adds = []
            for c0 in range(0, D, 512):
                for j, (bank, mm, w) in enumerate(proj_fm(w_o_v, c0, min(512, D - c0), xn_ready)):
                    m = c0 // 128 + j
                    o = p.op("dve", X("tensor_tensor", out=XT.t[:, m, :], in0=XT.t[:, m, :], in1=bank.t[:], op=ALU.add), deps=[mm] + xt_ready)
                    bank.read(o)
                    adds.append(o)
            XT.wrote(adds[-1])

            def norm_into_XN(goff, xt_dep):
                r_op = rms_stats(kb, XT, KD, D, ones, sqr, psR.next(), R, xt_dep)
                last = None
                for k in range(KD):
                    last = p.op("dve", X("scalar_tensor_tensor", out=XN.t[:, k, :], in0=XT.t[:, k, :], scalar=gains[:, goff + k:goff + k + 1], in1=R.t[:], op0=ALU.mult, op1=ALU.mult),
                                deps=[r_op, d_g] + XN.wdeps() + xt_dep)
                R.read(last)
                return [last]

            xn_ready = norm_into_XN(0, [adds[-1]])
            XN.wrote(xn_ready[0])

            hts = [None] * NR
            dn_adds = [adds[-1]]

            def up_round(r):
                ht = HT.next()
                evs = []
                for c0 in range(r * JR * 128, (r + 1) * JR * 128, 512):
                    for j, (bank, mm, w) in enumerate(proj_fm(w_up_v, c0, 512, xn_ready)):
                        jl = (c0 - r * JR * 128) // 128 + j
                        sf = sqf.next()
                        o0 = p.op("act", X("activation", out=sf.t[:], in_=bank.t[:], func=AF.Square), deps=[mm] + sf.wdeps())
                        sf.wrote(o0)
                        o = p.op("dve", X("scalar_tensor_tensor", out=ht.t[:, jl, :], in0=bank.t[:], scalar=0.0, in1=sf.t[:], op0=ALU.is_gt, op1=ALU.mult), deps=[mm, o0] + ht.wdeps())
                        sf.read(o)
                        bank.read(o0)
                        bank.read(o)
                        evs.append(o)
                ht.wrote(evs[-1])
                hts[r] = ht

            def down_round(r):
                ht = hts[r]
                for c0 in range(0, D, DW):
                    s = wB.next()
                    o = p.dma("pool", s.t[:], w_dn_v[:, r * JR:(r + 1) * JR, c0:c0 + DW], deps=s.wdeps(), ring="wB", nslots=2)
                    s.wrote(o)
                    for mh in range(DW // 512):
                        banks = [psR.next() for _ in range(4)]
                        last = None
                        for jl in range(JR):
                            for mi in range(4):
                                d = []
                                if jl == 0:
                                    d = banks[mi].wdeps()
                                    if mi == 0:
                                        d = d + s.rdeps() + ht.rdeps()
                                last = p.op("pe", X("matmul", banks[mi].t[:], lhsT=s.t[:, jl, (mh * 4 + mi) * 128:(mh * 4 + mi + 1) * 128], rhs=ht.t[:, jl, :], start=(jl == 0), stop=(jl == JR - 1)), deps=d)
                        s.read(last)
                        ht.read(last)
                        for mi in range(4):
                            banks[mi].wrote(last)
                            m = c0 // 128 + mh * 4 + mi
                            o = p.op("dve", X("tensor_tensor", out=XT.t[:, m, :], in0=XT.t[:, m, :], in1=banks[mi].t[:], op=ALU.add), deps=[last] + dn_adds[-1:])
                            banks[mi].read(o)
                            dn_adds.append(o)

            for r in range(NR + 1):
                if r < NR:
                    up_round(r)
                if r >= 1:
                    down_round(r - 1)
            XT.wrote(dn_adds[-1])
            xt_ready = [dn_adds[-1]]
            XN.read(dn_adds[-1])
            for q in range(4):
                ks = slice(q * KD // 4, (q + 1) * KD // 4)
                o = p.dma("sp", xo_v[:, ks, tsl], XT.t[:, ks, :], deps=xt_ready, ring="xout", nslots=4)
                XT.read(o)
                out_dmas.append(o)

            if tail:
                for wslot, als in tail_alias:
                    for al in als:
                        for op_ in wslot.wdeps():
                            al.read(op_)
                w_qg_v = w_qg.rearrange("(k p) n -> p k n", p=128)
                w_kv_v = w_kv.rearrange("(k p) n -> p k n", p=128)
                xn_ready = norm_into_XN(KD, xt_ready)
                XN.wrote(xn_ready[0])
                pend = [None]

                def finish_n(item):
                    bank, sq, o_sq, dst, row0, gi = item
                    ssb = psR.next()
                    if ssb is bank:
                        ssb = psR.next()
                    mm2 = mm_group(kb, ssb, [(ones[:], sq.t[:], [o_sq])])
                    sq.read(mm2)
                    rq = Rq.next()
                    o2 = rstd_from(kb, ssb, rq, 1.0 / HD, [mm2])
                    ob = OB.next()
                    if gi is None:
                        o3 = p.op("dve", X("tensor_tensor", out=ob.t[:], in0=bank.t[:], in1=rq.t[:], op=ALU.mult), deps=[o2] + ob.wdeps())
                    else:
                        o3 = p.op("dve", X("scalar_tensor_tensor", out=ob.t[:], in0=bank.t[:], scalar=kgain[:, gi:gi + 1], in1=rq.t[:], op0=ALU.mult, op1=ALU.mult), deps=[o2, d_kg] + ob.wdeps())
                    rq.wrote(o3)
                    bank.read(o3)
                    ob.wrote(o3)
                    o4 = p.dma("sp", dst[row0:row0 + 128, tsl], ob.t[:], deps=[o3], ring="oq", nslots=3)
                    ob.read(o4)
                    out_dmas.append(o4)

                def normed(bank, mm, dst, row0, gi):
                    sq = sqr.next()
                    o_sq = p.op("act", X("activation", out=sq.t[:], in_=bank.t[:], func=AF.Square), deps=[mm] + sq.wdeps())
                    sq.wrote(o_sq)
                    bank.read(o_sq)
                    if pend[0] is not None:
                        finish_n(pend[0])
                    pend[0] = (bank, sq, o_sq, dst, row0, gi)

                def flush():
                    if pend[0] is not None:
                        finish_n(pend[0])
                        pend[0] = None

                for c0 in range(0, D, 512):
                    for j, (bank, mm, w) in enumerate(proj_fm(w_qg_v, c0, 512, xn_ready)):
                        normed(bank, mm, qT_o, c0 + j * 128, None)
                    flush()
                (bank, mm, w), = proj_fm(w_qg_v, D, NG, xn_ready)
                o = p.op("act", X("activation", out=GB.t[0:NG, :], in_=bank.t[0:NG, :], func=AF.Sigmoid), deps=[mm] + GB.wdeps())
                bank.read(o)
                GB.wrote(o)
                o2 = p.dma("sp", gT_o[:, tsl], GB.t[0:NG, :], deps=[o], ring="og", nslots=1)
                GB.read(o2)
                out_dmas.append(o2)
                last = None
                for k in range(KD):
                    last = p.op("dve", X("scalar_tensor_tensor", out=XN.t[:, k, :], in0=XT.t[:, k, :], scalar=gains[:, 2 * KD + k:2 * KD + k + 1], in1=R.t[:], op0=ALU.mult, op1=ALU.mult),
                                deps=XN.wdeps() + xt_ready + R.rdeps())
                XN.wrote(last)
                R.read(last)
                xn_ready = [last]
                for (jb, dst, gi) in ((0, kcT_o, None), (1, vcT_o, None), (2, ksT_o, 0), (4, kwT_o, 1)):
                    for c0 in range(0, KVD, 512):
                        for j, (bank, mm, w) in enumerate(proj_fm(w_kv_v, jb * KVD + c0, min(512, KVD - c0), xn_ready)):
                            row0 = c0 + j * 128
                            if gi is None:
                                ob = OB.next()
                                o3 = p.op("act", X("copy", out=ob.t[:], in_=bank.t[:]), deps=[mm] + ob.wdeps())
                                bank.read(o3)
                                ob.wrote(o3)
                                o4 = p.dma("sp", dst[row0:row0 + 128, tsl], ob.t[:], deps=[o3], ring="oq", nslots=3)
                                ob.read(o4)
                                out_dmas.append(o4)
                            else:
                                normed(bank, mm, dst, row0, gi)
                        flush()
                for (jb, dst) in ((3, vs_o), (5, vw_o)):
                    for c0 in range(0, KVD, 512):
                        for tt, (bank, mm) in enumerate(proj_tm(w_kv_v, jb * KVD + c0, xn_ready)):
                            ob = OB.next()
                            o3 = p.op("act", X("copy", out=ob.t[:], in_=bank.t[:]), deps=[mm] + ob.wdeps())
                            bank.read(o3)
                            ob.wrote(o3)
                            o4 = p.dma("sp", dst[ps_ * 512 + tt * 128:ps_ * 512 + (tt + 1) * 128, c0:c0 + 512], ob.t[:], deps=[o3], ring="oq", nslots=3)
                            ob.read(o4)
                            out_dmas.append(o4)
                for wslot, als in tail_alias:
                    for al in als:
                        for op_ in al.wdeps():
                            wslot.read(op_)
        p.wait("sp", out_dmas)
        p.emit(ctx)
    return nc
```
